# Optimizing a Trainium2 kernel written in Bass

```python
import math
import jax, jax.numpy as jnp
from jax import lax
import numpy as np

D_MODEL = 2048
BATCH = 4
SEQ = 4096
DEPTH = 2

GRID_W = 64
CTX_LEN = 256
N_AB = (DEPTH + 1) // 2
N_CD = DEPTH // 2
N_MOD = 9
D_FF = 5632
EPS = 1e-6
ROPE_THETA = 10000.0
Q_BLOCK = 128

A_HEADS = 8
A_KV = 2
A_DH = 128
B_HEADS = 4
B_DQK = 128
B_DV = 256
B_CHUNK = 64
C_WIDTH = 1024
C_GROUP = 16
C_GROUPS = C_WIDTH // C_GROUP
C_STATE = 64
D_HEADS = 16
D_KV = 2
D_DH = 64
D_WINDOW = 128

AB_SIZES = (A_HEADS * A_DH, A_KV * A_DH, A_KV * A_DH, B_HEADS * B_DQK, B_HEADS * B_DQK,
            B_HEADS * B_DV, B_HEADS * B_DV, 4 * B_HEADS)
AB_IN = sum(AB_SIZES)
AB_MIX = A_HEADS * A_DH + B_HEADS * B_DV
CD_SIZES = (C_WIDTH, D_HEADS * D_DH, D_KV * D_DH, D_KV * D_DH)
CD_IN = sum(CD_SIZES)
CD_MIX = C_WIDTH + D_HEADS * D_DH
F32 = jnp.float32

kernel_name = 'hybrid_dit_gqa_mlstm_s5_swa'


def split_cols(z, sizes):
    out, start = [], 0
    for s in sizes:
        out.append(z[..., start:start + s])
        start += s
    return out


def rmsnorm(x, g):
    xf = x.astype(F32)
    y = xf * lax.rsqrt(jnp.mean(xf * xf, axis=-1, keepdims=True) + EPS)
    return (y * g.astype(F32)).astype(x.dtype)


def modulate(x, g, shift, scale):
    return rmsnorm(x, g) * (1 + scale) + shift


def swiglu(h, w1, w3, w2):
    return (jax.nn.silu(h @ w1) * (h @ w3)) @ w2


def to_heads(z, n, dh):
    b, t, _ = z.shape
    return z.reshape(b, t, n, dh).transpose(0, 2, 1, 3)


def from_heads(z):
    b, n, t, dh = z.shape
    return z.transpose(0, 2, 1, 3).reshape(b, t, n * dh)


def axial_rope(rows, head_dim):
    r, cidx = jnp.meshgrid(jnp.arange(rows, dtype=F32), jnp.arange(GRID_W, dtype=F32), indexing='ij')
    r, cidx = r.reshape(-1), cidx.reshape(-1)
    n_freq = head_dim // 4
    inv = ROPE_THETA ** (-jnp.arange(n_freq, dtype=F32) / n_freq)
    ang = jnp.concatenate([r[:, None] * inv, cidx[:, None] * inv], axis=-1)
    return jnp.cos(ang), jnp.sin(ang)


def apply_rope(x, cos, sin):
    half = x.shape[-1] // 2
    xf = x.astype(F32)
    x1, x2 = xf[..., :half], xf[..., half:]
    return jnp.concatenate([x1 * cos - x2 * sin, x2 * cos + x1 * sin], axis=-1).astype(x.dtype)


def attend(q, k, v, scale):
    s = jnp.einsum('bkgqd,bksd->bkgqs', q, k).astype(F32) * scale
    p = jax.nn.softmax(s, axis=-1).astype(v.dtype)
    return jnp.einsum('bkgqs,bksd->bkgqd', p, v)


def attn_a(zq, zk, zv, cq, ck, cv, g_q, g_k, cos, sin, want_ctx):
    b, t, _ = zq.shape
    grp = A_HEADS // A_KV
    scale = A_DH ** -0.5
    q = apply_rope(rmsnorm(to_heads(zq, A_HEADS, A_DH), g_q), cos, sin)
    k = apply_rope(rmsnorm(to_heads(zk, A_KV, A_DH), g_k), cos, sin)
    v = to_heads(zv, A_KV, A_DH)
    kc = rmsnorm(to_heads(ck, A_KV, A_DH), g_k)
    vc = to_heads(cv, A_KV, A_DH)
    k_all = jnp.concatenate([kc, k], axis=2)
    v_all = jnp.concatenate([vc, v], axis=2)
    nb = t // Q_BLOCK
    qb = jnp.moveaxis(q.reshape(b, A_KV, grp, nb, Q_BLOCK, A_DH), 3, 0)
    o = lax.map(lambda qq: attend(qq, k_all, v_all, scale), qb)
    y = from_heads(jnp.moveaxis(o, 0, 3).reshape(b, A_HEADS, t, A_DH))
    yc = None
    if want_ctx:
        tc = cq.shape[1]
        qc = rmsnorm(to_heads(cq, A_HEADS, A_DH), g_q).reshape(b, A_KV, grp, tc, A_DH)
        yc = from_heads(attend(qc, kc, vc, scale).reshape(b, A_HEADS, tc, A_DH))
    return y, yc


def mlstm_chunkwise(q, k, v, li, lf, state):
    b, h, t, _ = q.shape
    L = B_CHUNK
    nc = t // L

    def chunks(a):
        return jnp.moveaxis(a.reshape(a.shape[:2] + (nc, L) + a.shape[3:]), 2, 0)

    causal = jnp.tril(jnp.ones((L, L), dtype=bool))

    def step(carry, xs):
        C, n, m = carry
        qc, kc, vc, ic, fc = xs
        bcum = jnp.cumsum(fc, axis=-1)
        log_d = jnp.where(causal, bcum[..., :, None] - bcum[..., None, :] + ic[..., None, :], -jnp.inf)
        m_inter = bcum + m[..., None]
        m_t = jnp.maximum(m_inter, jnp.max(log_d, axis=-1))
        w_intra = jnp.exp(log_d - m_t[..., None])
        w_inter = jnp.exp(m_inter - m_t)
        s = jnp.einsum('bhtd,bhsd->bhts', qc, kc) * w_intra
        num = jnp.einsum('bhts,bhsv->bhtv', s, vc) + w_inter[..., None] * jnp.einsum('bhvd,bhtd->bhtv', C, qc)
        den = jnp.sum(s, axis=-1) + w_inter * jnp.einsum('bhd,bhtd->bht', n, qc)
        h_out = num / jnp.maximum(jnp.abs(den), jnp.exp(-m_t))[..., None]
        b_last = bcum[..., -1]
        log_w = b_last[..., None] - bcum + ic
        m_new = jnp.maximum(b_last + m, jnp.max(log_w, axis=-1))
        w = jnp.exp(log_w - m_new[..., None])
        decay = jnp.exp(b_last + m - m_new)
        C_new = decay[..., None, None] * C + jnp.einsum('bhsv,bhsd->bhvd', w[..., None] * vc, kc)
        n_new = decay[..., None] * n + jnp.einsum('bhs,bhsd->bhd', w, kc)
        return (C_new, n_new, m_new), h_out

    state, hs = lax.scan(step, state, tuple(chunks(a) for a in (q, k, v, li, lf)))
    return jnp.moveaxis(hs, 0, 2).reshape(b, h, t, v.shape[-1]), state


def mlstm_mixer(zq, zk, zv, zo, zg, cq, ck, cv, co, cg, b_gate, head_g, want_ctx):
    def prep(q, k, v, g):
        bz, tz, _ = q.shape
        qh = to_heads(q.astype(F32), B_HEADS, B_DQK) * (B_DQK ** -0.5)
        kh = to_heads(k.astype(F32), B_HEADS, B_DQK)
        vh = to_heads(v.astype(F32), B_HEADS, B_DV)
        gg = (g.astype(F32) + b_gate.astype(F32)).reshape(bz, tz, 4, B_HEADS).transpose(2, 0, 3, 1)
        return qh, kh, vh, (gg[0], jax.nn.log_sigmoid(gg[1])), (gg[2], jax.nn.log_sigmoid(gg[3]))

    def flip(a):
        return jnp.flip(a, axis=2)

    def bidir(q, k, v, gf, gb, st_f, st_b):
        hf, sf = mlstm_chunkwise(q, k, v, gf[0], gf[1], st_f)
        hb, sb = mlstm_chunkwise(flip(q), flip(k), flip(v), flip(gb[0]), flip(gb[1]), st_b)
        return hf + flip(hb), sf, sb

    def finish(hh, o):
        hn = hh * lax.rsqrt(jnp.mean(hh * hh, axis=-1, keepdims=True) + EPS)
        return (jax.nn.sigmoid(o.astype(F32)) * from_heads(hn) * head_g.astype(F32)).astype(o.dtype)

    bz = zq.shape[0]
    zero = (jnp.zeros((bz, B_HEADS, B_DV, B_DQK), F32), jnp.zeros((bz, B_HEADS, B_DQK), F32),
            jnp.zeros((bz, B_HEADS), F32))
    hc, sf, sb = bidir(*prep(cq, ck, cv, cg), zero, zero)
    h, _, _ = bidir(*prep(zq, zk, zv, zg), sf, sb)
    yc = finish(hc, co) if want_ctx else None
    return finish(h, zo), yc


def s5_combine(e1, e2):
    ar1, ai1, br1, bi1 = e1
    ar2, ai2, br2, bi2 = e2
    return (ar1 * ar2 - ai1 * ai2, ar1 * ai2 + ai1 * ar2,
            ar2 * br1 - ai2 * bi1 + br2, ar2 * bi1 + ai2 * br1 + bi2)


def s5_direction(u, a_re, a_im, log_dt, b_re, b_im, c_re, c_im, x0_re, x0_im):
    dt = jnp.exp(log_dt)[:, None]
    mag = jnp.exp(a_re * dt)
    ab_re, ab_im = mag * jnp.cos(a_im * dt), mag * jnp.sin(a_im * dt)
    den = a_re * a_re + a_im * a_im
    nr, ni = ab_re - 1.0, ab_im
    k_re, k_im = (nr * a_re + ni * a_im) / den, (ni * a_re - nr * a_im) / den
    bb_re = k_re[..., None] * b_re - k_im[..., None] * b_im
    bb_im = k_re[..., None] * b_im + k_im[..., None] * b_re
    bu_re = jnp.einsum('tbgi,gpi->tbgp', u, bb_re)
    bu_im = jnp.einsum('tbgi,gpi->tbgp', u, bb_im)
    bu_re = bu_re.at[0].add(ab_re * x0_re - ab_im * x0_im)
    bu_im = bu_im.at[0].add(ab_re * x0_im + ab_im * x0_re)
    t = u.shape[0]
    a_r = jnp.broadcast_to(ab_re[None, None], (t, 1) + ab_re.shape)
    a_i = jnp.broadcast_to(ab_im[None, None], (t, 1) + ab_im.shape)
    _, _, xr, xi = lax.associative_scan(s5_combine, (a_r, a_i, bu_re, bu_im), axis=0)
    y = jnp.einsum('tbgp,gip->tbgi', xr, c_re) - jnp.einsum('tbgp,gip->tbgi', xi, c_im)
    return y, xr[-1], xi[-1]


def s5_mixer(u, uc, a_re, a_im, log_dt, b_re, b_im, c_re, c_im, d_skip, glu_w, glu_b, want_ctx):
    def to_tbgi(z):
        bz, tz, _ = z.shape
        return z.astype(F32).reshape(bz, tz, C_GROUPS, C_GROUP).transpose(1, 0, 2, 3)

    def prm(d):
        return (a_re[d].astype(F32), a_im[d].astype(F32), log_dt[d].astype(F32), b_re[d].astype(F32),
                b_im[d].astype(F32), c_re[d].astype(F32), c_im[d].astype(F32))

    def flip(a):
        return jnp.flip(a, axis=0)

    def finish(y_f, y_b_rev, z):
        y = (y_f + flip(y_b_rev)).transpose(1, 0, 2, 3).reshape(z.shape) + d_skip.astype(F32) * z.astype(F32)
        g = jax.nn.gelu(y)
        return (g * jax.nn.sigmoid(g @ glu_w.astype(F32) + glu_b.astype(F32))).astype(z.dtype)

    ut, uct = to_tbgi(u), to_tbgi(uc)
    zero = jnp.zeros((u.shape[0], C_GROUPS, C_STATE), F32)
    ycf, sfr, sfi = s5_direction(uct, *prm(0), zero, zero)
    ycb, sbr, sbi = s5_direction(flip(uct), *prm(1), zero, zero)
    yf, _, _ = s5_direction(ut, *prm(0), sfr, sfi)
    yb, _, _ = s5_direction(flip(ut), *prm(1), sbr, sbi)
    yc = finish(ycf, ycb, uc) if want_ctx else None
    return finish(yf, yb, u), yc


def attn_d(zq, zk, zv, cq, ck, cv, sink, cos, sin, want_ctx):
    b, t, _ = zq.shape
    grp = D_HEADS // D_KV
    scale = D_DH ** -0.5
    q = apply_rope(to_heads(zq, D_HEADS, D_DH), cos, sin).reshape(b, D_KV, grp, t, D_DH)
    k = apply_rope(to_heads(zk, D_KV, D_DH), cos, sin)
    v = to_heads(zv, D_KV, D_DH)
    kc, vc = to_heads(ck, D_KV, D_DH), to_heads(cv, D_KV, D_DH)
    n_ctx = kc.shape[2]
    sink_l = sink.astype(F32).reshape(1, D_KV, grp, 1, 1)
    span = Q_BLOCK + 2 * D_WINDOW
    pad = ((0, 0), (0, 0), (D_WINDOW, D_WINDOW), (0, 0))
    kp, vp = jnp.pad(k, pad), jnp.pad(v, pad)

    def block(bi):
        start = bi * Q_BLOCK
        qb = lax.dynamic_slice_in_dim(q, start, Q_BLOCK, axis=3)
        kb = lax.dynamic_slice_in_dim(kp, start, span, axis=2)
        vb = lax.dynamic_slice_in_dim(vp, start, span, axis=2)
        qpos = start + jnp.arange(Q_BLOCK)
        kpos = start - D_WINDOW + jnp.arange(span)
        ok = (kpos[None, :] >= 0) & (kpos[None, :] < t) & (jnp.abs(qpos[:, None] - kpos[None, :]) <= D_WINDOW)
        s_loc = jnp.where(ok, jnp.einsum('bkgqd,bksd->bkgqs', qb, kb).astype(F32) * scale, -jnp.inf)
        s_ctx = jnp.einsum('bkgqd,bksd->bkgqs', qb, kc).astype(F32) * scale
        s = jnp.concatenate([jnp.broadcast_to(sink_l, s_ctx.shape[:-1] + (1,)), s_ctx, s_loc], axis=-1)
        p = jax.nn.softmax(s, axis=-1).astype(v.dtype)
        return (jnp.einsum('bkgqs,bksd->bkgqd', p[..., 1:1 + n_ctx], vc)
                + jnp.einsum('bkgqs,bksd->bkgqd', p[..., 1 + n_ctx:], vb))

    o = lax.map(block, jnp.arange(t // Q_BLOCK))
    y = from_heads(jnp.moveaxis(o, 0, 3).reshape(b, D_HEADS, t, D_DH))
    yc = None
    if want_ctx:
        tc = cq.shape[1]
        qc = to_heads(cq, D_HEADS, D_DH).reshape(b, D_KV, grp, tc, D_DH)
        s = jnp.einsum('bkgqd,bksd->bkgqs', qc, kc).astype(F32) * scale
        s = jnp.concatenate([jnp.broadcast_to(sink_l, s.shape[:-1] + (1,)), s], axis=-1)
        p = jax.nn.softmax(s, axis=-1).astype(vc.dtype)
        oc = jnp.einsum('bkgqs,bksd->bkgqd', p[..., 1:], vc)
        yc = from_heads(oc.reshape(b, D_HEADS, tc, D_DH))
    return y, yc


def ab_mixer(hn, hcn, w_in, b_gate, w_out, g_q, g_k, head_g, cos, sin, want_ctx):
    qa, ka, va, qb, kb, vb, ob, gb = split_cols(hn @ w_in, AB_SIZES)
    cqa, cka, cva, cqb, ckb, cvb, cob, cgb = split_cols(hcn @ w_in, AB_SIZES)
    ya, yac = attn_a(qa, ka, va, cqa, cka, cva, g_q, g_k, cos, sin, want_ctx)
    yb, ybc = mlstm_mixer(qb, kb, vb, ob, gb, cqb, ckb, cvb, cob, cgb, b_gate, head_g, want_ctx)
    y = jnp.concatenate([ya, yb], axis=-1) @ w_out
    yc = jnp.concatenate([yac, ybc], axis=-1) @ w_out if want_ctx else None
    return y, yc


def cd_mixer(hn, hcn, w_in, w_out, a_re, a_im, log_dt, b_re, b_im, c_re, c_im, d_skip, glu_w, glu_b,
             sink, cos, sin, want_ctx):
    u, qd, kd, vd = split_cols(hn @ w_in, CD_SIZES)
    uc, cqd, ckd, cvd = split_cols(hcn @ w_in, CD_SIZES)
    ys, ysc = s5_mixer(u, uc, a_re, a_im, log_dt, b_re, b_im, c_re, c_im, d_skip, glu_w, glu_b, want_ctx)
    yd, ydc = attn_d(qd, kd, vd, cqd, ckd, cvd, sink, cos, sin, want_ctx)
    y = jnp.concatenate([ys, yd], axis=-1) @ w_out
    yc = jnp.concatenate([ysc, ydc], axis=-1) @ w_out if want_ctx else None
    return y, yc


def setup_inputs(seed: int = 0) -> dict:
    key = jax.random.key(seed)
    ks = iter(jax.random.split(key, 40))

    def nrm(shape, scale):
        return jax.random.normal(next(ks), shape, F32) * scale

    D = D_MODEL
    ig = nrm((N_AB, 2, B_HEADS), 0.1)
    fg = jnp.linspace(3.0, 6.0, B_HEADS, dtype=F32)[None, None, :] + nrm((N_AB, 2, B_HEADS), 0.1)
    ab_b_gate = jnp.stack([ig[:, 0], fg[:, 0], ig[:, 1], fg[:, 1]], axis=1).reshape(N_AB, 4 * B_HEADS)
    n_idx = jnp.arange(C_STATE, dtype=F32)
    return {
        'x': nrm((BATCH, SEQ, D), 1.0),
        'c': nrm((BATCH, D), 1.0),
        'ctx': nrm((BATCH, CTX_LEN, D), 1.0),
        'c_ctx': nrm((D,), 1.0),
        'mod_w': nrm((DEPTH, D, N_MOD * D), 0.5 * D ** -0.5),
        'mod_b': nrm((DEPTH, N_MOD * D), 0.02),
        'norm_g': 1.0 + nrm((DEPTH, 3, D), 0.02),
        'ffn_w1': nrm((DEPTH, 2, D, D_FF), D ** -0.5),
        'ffn_w3': nrm((DEPTH, 2, D, D_FF), D ** -0.5),
        'ffn_w2': nrm((DEPTH, 2, D_FF, D), D_FF ** -0.5),
        'ab_w_in': nrm((N_AB, D, AB_IN), D ** -0.5),
        'ab_b_gate': ab_b_gate,
        'ab_w_out': nrm((N_AB, AB_MIX, D), AB_MIX ** -0.5),
        'a_gq': 1.0 + nrm((N_AB, A_DH), 0.02),
        'a_gk': 1.0 + nrm((N_AB, A_DH), 0.02),
        'b_head_g': 1.0 + nrm((N_AB, B_HEADS * B_DV), 0.02),
        'cd_w_in': nrm((N_CD, D, CD_IN), D ** -0.5),
        'cd_w_out': nrm((N_CD, CD_MIX, D), CD_MIX ** -0.5),
        's5_a_re': -0.5 * jnp.exp(nrm((N_CD, 2, C_GROUPS, C_STATE), 0.05)),
        's5_a_im': math.pi * n_idx + nrm((N_CD, 2, C_GROUPS, C_STATE), 0.01),
        's5_log_dt': jax.random.uniform(next(ks), (N_CD, 2, C_GROUPS), F32, math.log(1e-3), math.log(1e-1)),
        's5_b_re': nrm((N_CD, 2, C_GROUPS, C_STATE, C_GROUP), (2 * C_GROUP) ** -0.5),
        's5_b_im': nrm((N_CD, 2, C_GROUPS, C_STATE, C_GROUP), (2 * C_GROUP) ** -0.5),
        's5_c_re': nrm((N_CD, 2, C_GROUPS, C_GROUP, C_STATE), C_STATE ** -0.5),
        's5_c_im': nrm((N_CD, 2, C_GROUPS, C_GROUP, C_STATE), C_STATE ** -0.5),
        's5_d': nrm((N_CD, C_WIDTH), 1.0),
        's5_glu_w': nrm((N_CD, C_WIDTH, C_WIDTH), C_WIDTH ** -0.5),
        's5_glu_b': nrm((N_CD, C_WIDTH), 0.02),
        'd_sink': nrm((N_CD, D_HEADS), 0.5),
        'final_g': 1.0 + nrm((D,), 0.02),
    }


def reference(x, c, ctx, c_ctx, mod_w, mod_b, norm_g, ffn_w1, ffn_w3, ffn_w2, ab_w_in, ab_b_gate,
              ab_w_out, a_gq, a_gk, b_head_g, cd_w_in, cd_w_out, s5_a_re, s5_a_im, s5_log_dt, s5_b_re,
              s5_b_im, s5_c_re, s5_c_im, s5_d, s5_glu_w, s5_glu_b, d_sink, final_g):
    b, t, d = x.shape
    rows = t // GRID_W
    cos_a, sin_a = axial_rope(rows, A_DH)
    cos_d, sin_d = axial_rope(rows, D_DH)
    h, hc = x, ctx
    for l in range(DEPTH):
        want_ctx = l < DEPTH - 1
        m = (jax.nn.silu(c) @ mod_w[l] + mod_b[l]).reshape(b, N_MOD, 1, d)
        mc = (jax.nn.silu(c_ctx) @ mod_w[l] + mod_b[l]).reshape(1, N_MOD, 1, d)
        h = h + 0.5 * m[:, 2] * swiglu(modulate(h, norm_g[l, 0], m[:, 0], m[:, 1]),
                                       ffn_w1[l, 0], ffn_w3[l, 0], ffn_w2[l, 0])
        hc = hc + 0.5 * mc[:, 2] * swiglu(modulate(hc, norm_g[l, 0], mc[:, 0], mc[:, 1]),
                                          ffn_w1[l, 0], ffn_w3[l, 0], ffn_w2[l, 0])
        hn = modulate(h, norm_g[l, 1], m[:, 3], m[:, 4])
        hcn = modulate(hc, norm_g[l, 1], mc[:, 3], mc[:, 4])
        i = l // 2
        if l % 2 == 0:
            y, yc = ab_mixer(hn, hcn, ab_w_in[i], ab_b_gate[i], ab_w_out[i], a_gq[i], a_gk[i], b_head_g[i],
                             cos_a, sin_a, want_ctx)
        else:
            y, yc = cd_mixer(hn, hcn, cd_w_in[i], cd_w_out[i], s5_a_re[i], s5_a_im[i], s5_log_dt[i],
                             s5_b_re[i], s5_b_im[i], s5_c_re[i], s5_c_im[i], s5_d[i], s5_glu_w[i],
                             s5_glu_b[i], d_sink[i], cos_d, sin_d, want_ctx)
        h = h + m[:, 5] * y
        h = h + 0.5 * m[:, 8] * swiglu(modulate(h, norm_g[l, 2], m[:, 6], m[:, 7]),
                                       ffn_w1[l, 1], ffn_w3[l, 1], ffn_w2[l, 1])
        if want_ctx:
            hc = hc + mc[:, 5] * yc
            hc = hc + 0.5 * mc[:, 8] * swiglu(modulate(hc, norm_g[l, 2], mc[:, 6], mc[:, 7]),
                                              ffn_w1[l, 1], ffn_w3[l, 1], ffn_w2[l, 1])
    return rmsnorm(h, final_g)
```

```python
import math
import numpy as np
from contextlib import ExitStack
import concourse.bass as bass
import concourse.mybir as mybir
from concourse.bass_utils import run_bass_kernel_spmd

F32 = mybir.dt.float32
BF16 = mybir.dt.bfloat16
AF = mybir.ActivationFunctionType
ALU = mybir.AluOpType
AX = mybir.AxisListType
EPS = 1e-6


class Buf:
    __slots__ = ("name", "w", "r")

    def __init__(self, name=""):
        self.name = name
        self.w = []
        self.r = []


class Opnd:
    __slots__ = ("ap", "bufs")

    def __init__(self, ap, bufs):
        self.ap = ap
        self.bufs = bufs


class Tile:
    def __init__(self, t, nbuf=1, name=""):
        self.t = t
        self.bufs = [Buf(f"{name}{i}") for i in range(nbuf)]

    def __getitem__(self, idx):
        return Opnd(self.t[idx], self.bufs)

    def s(self, i, idx):
        if isinstance(i, int):
            return Opnd(self.t[idx], [self.bufs[i]])
        return Opnd(self.t[idx], [self.bufs[j] for j in i])

    def o(self, ap, i=None):
        if i is None:
            return Opnd(ap, self.bufs)
        if isinstance(i, int):
            return Opnd(ap, [self.bufs[i]])
        return Opnd(ap, [self.bufs[j] for j in i])


class Prog:
    def __init__(self, nc, es):
        self.nc = nc
        self.E = {"pe": nc.tensor, "act": nc.scalar, "dve": nc.vector, "pool": nc.gpsimd, "sp": nc.sync}
        self.semobj = {}
        self.cnt = {}
        for k in ("pe", "act", "dve", "pool"):
            self.semobj[k] = es.enter_context(nc.semaphore("s_" + k))
            self.cnt[k] = 0
        self.known = {k: {} for k in self.E}
        self.lanes = {}
        for q, n in (("sp", 20), ("act", 6), ("pool", 10)):
            self.lanes[q] = []
            for i in range(n):
                key = f"l_{q}{i}"
                self.semobj[key] = es.enter_context(nc.semaphore(key))
                self.lanes[q].append([key, 0])
        self.lane_rr = {q: 0 for q in self.lanes}
        self.nins = 0

    def _wait(self, e, deps):
        kn = self.known[e]
        best = {}
        for d in deps:
            k, v = d[0], d[1]
            if kn.get(k, 0) >= v:
                continue
            if best.get(k, 0) < v:
                best[k] = v
        for k, v in best.items():
            self.E[e].wait_ge(self.semobj[k], v)
            kn[k] = v
            self.nins += 1

    def _deps(self, e, reads, writes, pwrites):
        deps = []
        for o in reads:
            for b in o.bufs:
                deps += b.w
        for o in writes:
            for b in o.bufs:
                deps += b.w
                deps += b.r
        for o in pwrites:
            for b in o.bufs:
                deps += b.r
        return deps

    def _register(self, dep, reads, writes, pwrites):
        e = dep[2]
        for o in reads:
            for b in o.bufs:
                if not dep[3]:
                    b.r = [d for d in b.r if d[2] != e or d[3]]
                b.r.append(dep)
        for o in writes:
            for b in o.bufs:
                b.w = [dep]
                b.r = []
        for o in pwrites:
            for b in o.bufs:
                if b.r:
                    b.w = [dep]
                    b.r = []
                else:
                    if not dep[3]:
                        b.w = [d for d in b.w if d[2] != e or d[3]]
                    b.w.append(dep)

    def op(self, e, fn, reads=(), writes=(), pwrites=(), inc=True):
        self._wait(e, self._deps(e, reads, writes, pwrites))
        ins = fn(self.E[e])
        self.nins += 1
        if inc:
            self.cnt[e] += 1
            ins.then_inc(self.semobj[e], 1)
            dep = (e, self.cnt[e], e, False)
        else:
            dep = (e, self.cnt[e] + 1, e, False)
        self._register(dep, reads, writes, pwrites)
        return ins

    def dma(self, q, out, in_, reads=None, writes=None, pwrites=(), **kw):
        reads = [in_] if reads is None else reads
        writes = [out] if writes is None else writes
        lanes = self.lanes[q]
        i = self.lane_rr[q]
        self.lane_rr[q] = (i + 1) % len(lanes)
        lane = lanes[i]
        deps = self._deps(q, reads, writes, pwrites)
        deps.append((lane[0], lane[1] * 16, q, True))
        self._wait(q, deps)
        ins = self.E[q].dma_start(out=out.ap, in_=in_.ap, **kw)
        lane[1] += 1
        ins.then_inc(self.semobj[lane[0]], 16)
        self.nins += 1
        dep = (lane[0], lane[1] * 16, q, True)
        self._register(dep, reads, writes, pwrites)
        return ins

    def barrier(self, full=False):
        deps = [(k, v, k, False) for k, v in self.cnt.items() if v > 0]
        for q, lanes in self.lanes.items():
            if q == "pool" and not full:
                continue
            for key, c in lanes:
                if c > 0:
                    deps.append((key, c * 16, q, True))
        for e in self.E:
            self._wait(e, deps)

    def act(self, out, in_, func, bias=0.0, scale=1.0, extra=(), pw=False, e="act"):
        kw = {}
        rd = [in_] + list(extra)
        b = bias.ap if isinstance(bias, Opnd) else bias
        s = scale.ap if isinstance(scale, Opnd) else scale
        if isinstance(bias, Opnd):
            rd.append(bias)
        if isinstance(scale, Opnd):
            rd.append(scale)
        return self.op("act", lambda E: E.activation(out=out.ap, in_=in_.ap, func=func, bias=b, scale=s),
                       reads=rd, writes=() if pw else [out], pwrites=[out] if pw else ())

    def tt(self, e, out, in0, in1, op, pw=False):
        return self.op(e, lambda E: E.tensor_tensor(out=out.ap, in0=in0.ap, in1=in1.ap, op=op),
                       reads=[in0, in1], writes=() if pw else [out], pwrites=[out] if pw else ())

    def ts(self, e, out, in0, s1, s2, op0, op1=None, pw=False):
        rd = [in0]
        a1 = s1.ap if isinstance(s1, Opnd) else s1
        a2 = s2.ap if isinstance(s2, Opnd) else s2
        if isinstance(s1, Opnd):
            rd.append(s1)
        if isinstance(s2, Opnd):
            rd.append(s2)
        if op1 is None:
            f = lambda E: E.tensor_scalar(out=out.ap, in0=in0.ap, scalar1=a1, scalar2=None, op0=op0)
        else:
            f = lambda E: E.tensor_scalar(out=out.ap, in0=in0.ap, scalar1=a1, scalar2=a2, op0=op0, op1=op1)
        return self.op(e, f, reads=rd, writes=() if pw else [out], pwrites=[out] if pw else ())

    def stt(self, e, out, in0, sc, in1, op0, op1, pw=False):
        rd = [in0, in1]
        a = sc.ap if isinstance(sc, Opnd) else sc
        if isinstance(sc, Opnd):
            rd.append(sc)
        return self.op(e, lambda E: E.scalar_tensor_tensor(out=out.ap, in0=in0.ap, scalar=a, in1=in1.ap, op0=op0, op1=op1),
                       reads=rd, writes=() if pw else [out], pwrites=[out] if pw else ())

    def copy(self, e, out, in_, pw=False):
        if e == "act":
            f = lambda E: E.copy(out=out.ap, in_=in_.ap)
        else:
            f = lambda E: E.tensor_copy(out=out.ap, in_=in_.ap)
        return self.op(e, f, reads=[in_], writes=() if pw else [out], pwrites=[out] if pw else ())

    def memset(self, e, out, val, pw=False):
        if e == "act_":
            assert val == 0.0
            return self.op("act", lambda E: E.memzero(out.ap), reads=(), writes=() if pw else [out],
                           pwrites=[out] if pw else ())
        return self.op(e, lambda E: E.memset(out.ap, val), reads=(), writes=() if pw else [out],
                       pwrites=[out] if pw else ())

    def mm(self, out, lhsT, rhs, start, stop, extra_reads=(), inc=False):
        rd = [lhsT, rhs] + list(extra_reads)
        f = lambda E: E.matmul(out.ap, lhsT.ap, rhs.ap, start=start, stop=stop)
        if start:
            self._wait("pe", self._deps("pe", (), [out], ()))
        self._wait("pe", self._deps("pe", rd, (), ()))
        ins = f(self.E["pe"])
        self.nins += 1
        if stop or inc:
            self.cnt["pe"] += 1
            ins.then_inc(self.semobj["pe"], 1)
            dep = ("pe", self.cnt["pe"], "pe", False)
            self._register(dep, rd, [out] if stop else (), ())
        else:
            dep = ("pe", self.cnt["pe"] + 1, "pe", False)
            self._register(dep, rd, (), ())
        return ins

    def transpose(self, out, in_, ident):
        rd = [in_, ident]
        self._wait("pe", self._deps("pe", rd, [out], ()))
        ins = self.E["pe"].transpose(out.ap, in_.ap, ident.ap)
        self.nins += 1
        self.cnt["pe"] += 1
        ins.then_inc(self.semobj["pe"], 1)
        dep = ("pe", self.cnt["pe"], "pe", False)
        self._register(dep, rd, [out], ())
        return ins


class Ctx:
    def __init__(self, P):
        self.P = P
        self.nc = P.nc
        self.es = ExitStack()
        self.n = 0

    def __enter__(self):
        self.es.__enter__()
        return self

    def __exit__(self, *a):
        self.P.barrier()
        return self.es.__exit__(*a)

    def sb(self, shape, dt, nbuf=1, name="t"):
        self.n += 1
        t = self.es.enter_context(self.nc.sbuf_tensor(f"{name}_{self.P.nins}_{self.n}", list(shape), dt))
        return Tile(t, nbuf, name)

    def ps(self, shape, dt=F32, nbuf=1, name="p"):
        self.n += 1
        t = self.es.enter_context(self.nc.psum_tensor(f"{name}_{self.P.nins}_{self.n}", list(shape), dt))
        return Tile(t, nbuf, name)


class Cfg:
    def __init__(self, **kw):
        self.D = 2048
        self.FF = 5632
        self.T = 4096
        self.CTX = 256
        self.L = 2
        self.NT = 512
        self.A_H, self.A_KV, self.B_H = 8, 2, 4
        self.C_W, self.D_H, self.D_KV = 1024, 16, 2
        self.GRID_W = 64
        self.__dict__.update(kw)
        self.KC = self.D // 128
        self.FC = self.FF // 128
        self.N = self.T + self.CTX
        self.tiles = [(c0, min(self.NT, self.T - c0), 0) for c0 in range(0, self.T, self.NT)]
        self.tiles += [(self.T + c0, min(self.NT, self.CTX - c0), 1) for c0 in range(0, self.CTX, self.NT)]
        self.TOWN = self.T // 2
        self.own_tiles = [t for t in self.tiles if t[2] == 0 and t[0] < self.TOWN]


class Stream:
    def __init__(self, slots, n_items, loader):
        self.slots = slots
        self.n = n_items
        self.loader = loader
        self.next = 0
        self.consumed = 0

    def pump(self):
        while self.next < self.n and self.next - self.consumed < len(self.slots):
            self.loader(self.next, self.slots[self.next % len(self.slots)])
            self.next += 1

    def get(self, i):
        assert i == self.consumed and i < self.next, (i, self.consumed, self.next)
        return self.slots[i % len(self.slots)]

    def done(self, i):
        self.consumed = i + 1


def cast_ffn_weights(P, cfg, w1, w3, w2, w13s, w2s):
    for f in range(cfg.FC):
        for wi, w in enumerate((w1, w3)):
            src = w.t[:, f * 128:(f + 1) * 128].rearrange("(kc p) m -> p kc m", p=128)
            P.dma("pool", w13s.o(w13s.t[f, :, wi, :, :]), w.o(src), writes=(), pwrites=[w13s[:]])
    for d in range(cfg.KC):
        src = w2.t[:, d * 128:(d + 1) * 128].rearrange("(fc p) m -> p fc m", p=128)
        P.dma("pool", w2s.o(w2s.t[d]), w2.o(src), writes=(), pwrites=[w2s[:]])


def cast_rows(P, w, ws, nd):
    for d in range(nd):
        src = w.t[:, d * 128:(d + 1) * 128].rearrange("(fc p) m -> p fc m", p=128)
        P.dma("pool", ws.o(ws.t[d]), w.o(src), writes=(), pwrites=[ws[:]])


def mod_phase(P, cfg, G, cT, mod_w, mod_bT, norm_gT, layers=None, cext=None):
    KC, D, L = cfg.KC, cfg.D, cfg.L
    NB = 9 * D // 512
    with (Ctx(P) if cext is None else ExitStack()) as c_:
        c = c_ if cext is None else cext
        sc = c.sb([128, KC, 2], F32, name="sc")
        mb = c.sb([128, L, 9 * KC], F32, name="mb")
        ng = c.sb([128, L, 3, KC], F32, name="ng")
        wb = [c.sb([128, KC, 512], F32, name=f"wb{i}") for i in range(2)]
        ps = c.ps([128, 9 * KC, 2], F32, name="modps")
        P.dma("sp", sc[:], cT[:])
        P.dma("sp", mb[:], mod_bT[:])
        P.dma("sp", ng[:], norm_gT[:])
        P.act(sc[:], sc[:], AF.Silu)
        for l in (range(L) if layers is None else layers):
            for blk in range(NB):
                w = wb[blk % 2]
                src = mod_w.t[l, :, blk * 512:(blk + 1) * 512].rearrange("(kc p) n -> p kc n", p=128)
                P.dma("sp" if blk % 2 == 0 else "act", w[:], mod_w.o(src))
                for n4 in range(4):
                    cc = blk * 4 + n4
                    for kc in range(KC):
                        P.mm(ps.o(ps.t[:, cc, :]), w.o(w.t[:, kc, n4 * 128:(n4 + 1) * 128]), sc.o(sc.t[:, kc, :]),
                             start=(kc == 0), stop=(kc == KC - 1))
            mods_l = G.MODS.t[:, l].rearrange("p j k w -> p (j k) w")
            for wch in range(2):
                P.tt("dve", G.MODS.o(mods_l[:, :, wch]), ps.o(ps.t[:, :, wch]), mb.o(mb.t[:, l, :]), ALU.add, pw=True)
            for j in range(3):
                for wch in range(2):
                    P.stt("dve", G.GS.o(G.GS.t[:, l, j, :, wch]), G.MODS.o(G.MODS.t[:, l, 3 * j + 1, :, wch]), 1.0,
                          ng.o(ng.t[:, l, j, :]), ALU.add, ALU.mult, pw=True)
                    P.ts("dve", G.GT.o(G.GT.t[:, l, j, :, wch]), G.MODS.o(G.MODS.t[:, l, 3 * j + 2, :, wch]),
                         1.0 if j == 1 else 0.5, None, ALU.mult, pw=True)


def norm_mod_tile(P, cfg, G, l, j, wch, H, hn, ncol, sq, tmp, ps_ssq, rstd):
    KC = cfg.KC
    for kc in range(KC):
        s = sq[kc % 2]
        P.act(s.o(s.t[:, :ncol]), H.o(H.t[:, kc, :ncol]), AF.Square)
        P.mm(ps_ssq.o(ps_ssq.t[:, :ncol]), G.ones[:, :], s.o(s.t[:, :ncol]), start=(kc == 0), stop=(kc == KC - 1), inc=True)
    P.act(rstd.o(rstd.t[:, :ncol]), ps_ssq.o(ps_ssq.t[:, :ncol]), AF.Sqrt, bias=G.eps[:, 0:1], scale=1.0 / cfg.D)
    P.op("dve", lambda E: E.reciprocal(out=rstd.t[:, :ncol], in_=rstd.t[:, :ncol]),
         reads=[rstd[:]], writes=[rstd[:]])
    for kc in range(KC):
        t = tmp[kc % 2]
        P.tt("dve", t.o(t.t[:, :ncol]), H.o(H.t[:, kc, :ncol]), rstd.o(rstd.t[:, :ncol]), ALU.mult)
        P.act(hn.s(kc, (slice(None), kc, slice(0, ncol))), t.o(t.t[:, :ncol]), AF.Identity,
              bias=G.MODS.o(G.MODS.t[:, l, 3 * j, kc, wch:wch + 1]), scale=G.GS.o(G.GS.t[:, l, j, kc, wch:wch + 1]))


def ffn_phase(P, cfg, G, l, j, hin, hout, w13s, w2s, tiles, pre=None):
    KC, FC, NT = cfg.KC, cfg.FC, cfg.NT
    nj = 0 if j == 0 else 2
    with Ctx(P) as c:
        pf = pre is None
        Hb = [c.sb([128, KC, NT], F32, name="H") for _ in range(2 if pf else 1)]
        hn = c.sb([128, KC, NT], BF16, nbuf=KC, name="hn")
        Gt = c.sb([128, FC, NT], BF16, nbuf=FC, name="G")
        sq = [c.sb([128, NT], F32, name="sq") for _ in range(2)]
        tmp = [c.sb([128, NT], F32, name="tmp") for _ in range(2)]
        sil = [c.sb([128, NT], F32, name="sil") for _ in range(2)]
        ot = [c.sb([128, NT], F32, name="ot") for _ in range(2)]
        rstdb = [c.sb([128, NT], F32, name="rstd") for _ in range(2 if pf else 1)]
        w13 = [c.sb([128, 2, KC, 128], BF16, name="w13") for _ in range(3)]
        w2 = [c.sb([128, FC, 128], BF16, name="w2") for _ in range(2)]
        ps_ssq = c.ps([128, NT], F32, name="ssq")
        psA = [c.ps([128, NT], F32, name="psA") for _ in range(2)]
        psB = [c.ps([128, NT], F32, name="psB") for _ in range(2)]
        psO = [c.ps([128, NT], F32, name="psO") for _ in range(2)]
        if pre is not None:
            Yt = c.sb([128, pre["MC"], NT], BF16, name="Yt")
            wob = [c.sb([128, pre["MC"], 128], BF16, name="wob") for _ in range(2)]
        nt = len(tiles)
        s13 = Stream(w13, nt * FC, lambda i, slot: P.dma("sp", slot[:], w13s.o(w13s.t[i % FC])))
        s2 = Stream(w2, nt * KC, lambda i, slot: P.dma("act", slot[:], w2s.o(w2s.t[i % KC])))
        s13.pump()

        def load_h(ti_):
            c0_, ncol_, _ = tiles[ti_]
            H_ = Hb[ti_ % len(Hb)]
            src_ = hin.t[:, c0_:c0_ + ncol_].rearrange("(kc p) n -> p kc n", p=128)
            P.dma("sp", H_.o(H_.t[:, :, :ncol_]), hin.o(src_))

        def do_norm(ti_):
            c0_, ncol_, wch_ = tiles[ti_]
            norm_mod_tile(P, cfg, G, l, nj, wch_, Hb[ti_ % len(Hb)], hn, ncol_, sq, tmp, ps_ssq, rstdb[ti_ % len(rstdb)])

        if pf:
            load_h(0)
            do_norm(0)
        for ti, (c0, ncol, wch) in enumerate(tiles):
            H = Hb[ti % len(Hb)]
            rstd = rstdb[ti % len(rstdb)]
            if pf:
                if ti + 1 < len(tiles):
                    load_h(ti + 1)
            else:
                load_h(ti)
            if pre is not None:
                YM, wos, MC = pre["YM"], pre["wos"], pre["MC"]
                P.dma("act", Yt.o(Yt.t[:, :, :ncol]), YM.o(YM.t[:, c0:c0 + ncol].rearrange("(kc p) n -> p kc n", p=128)))
                for d in range(KC):
                    wo = wob[d % 2]
                    P.dma("sp", wo[:], wos.o(wos.t[d]))
                    o = psO[d % 2]
                    for kc in range(MC):
                        P.mm(o.o(o.t[:, :ncol]), wo.o(wo.t[:, kc, :]), Yt.o(Yt.t[:, kc, :ncol]), start=(kc == 0), stop=(kc == MC - 1))
                    P.stt("dve", H.o(H.t[:, d, :ncol]), o.o(o.t[:, :ncol]), G.GT.o(G.GT.t[:, l, 1, d, wch:wch + 1]),
                          H.o(H.t[:, d, :ncol]), ALU.mult, ALU.add)
            if not pf:
                norm_mod_tile(P, cfg, G, l, nj, wch, H, hn, ncol, sq, tmp, ps_ssq, rstd)
            if getattr(cfg, "dbg_ffn", 0) == 1:
                for d in range(KC):
                    otile = ot[d % 2]
                    P.copy("dve", otile.o(otile.t[:, :ncol]), hn.s(d, (slice(None), d, slice(0, ncol))))
                    P.dma("sp", hout.o(hout.t[d * 128:(d + 1) * 128, c0:c0 + ncol]), otile.o(otile.t[:, :ncol]),
                          writes=(), pwrites=[hout[:]])
                continue
            for f in range(FC):
                i13 = ti * FC + f
                s13.pump()
                s2.pump()
                w = s13.get(i13)
                a, b, sl = psA[f % 2], psB[f % 2], sil[f % 2]
                for kc in range(KC):
                    P.mm(a.o(a.t[:, :ncol]), w.o(w.t[:, 0, kc, :]), hn.s(kc, (slice(None), kc, slice(0, ncol))),
                         start=(kc == 0), stop=(kc == KC - 1))
                for kc in range(KC):
                    P.mm(b.o(b.t[:, :ncol]), w.o(w.t[:, 1, kc, :]), hn.s(kc, (slice(None), kc, slice(0, ncol))),
                         start=(kc == 0), stop=(kc == KC - 1))
                s13.done(i13)
                P.act(sl.o(sl.t[:, :ncol]), a.o(a.t[:, :ncol]), AF.Silu)
                P.tt("dve", Gt.s(f, (slice(None), f, slice(0, ncol))), sl.o(sl.t[:, :ncol]), b.o(b.t[:, :ncol]), ALU.mult)
            if getattr(cfg, "dbg_ffn", 0) == 2:
                for d in range(min(KC, FC)):
                    otile = ot[d % 2]
                    P.copy("dve", otile.o(otile.t[:, :ncol]), Gt.s(d, (slice(None), d, slice(0, ncol))))
                    P.dma("sp", hout.o(hout.t[d * 128:(d + 1) * 128, c0:c0 + ncol]), otile.o(otile.t[:, :ncol]),
                          writes=(), pwrites=[hout[:]])
                continue
            for d in range(KC):
                i2 = ti * KC + d
                s13.pump()
                s2.pump()
                w = s2.get(i2)
                o, otile = psO[d % 2], ot[d % 2]
                for f in range(FC):
                    P.mm(o.o(o.t[:, :ncol]), w.o(w.t[:, f, :]), Gt.s(f, (slice(None), f, slice(0, ncol))),
                         start=(f == 0), stop=(f == FC - 1))
                s2.done(i2)
                if pf and d == 1 and ti + 1 < len(tiles):
                    do_norm(ti + 1)
                P.stt("dve", otile.o(otile.t[:, :ncol]), o.o(o.t[:, :ncol]), G.GT.o(G.GT.t[:, l, nj, d, wch:wch + 1]),
                      H.o(H.t[:, d, :ncol]), ALU.mult, ALU.add)
                P.dma("sp", hout.o(hout.t[d * 128:(d + 1) * 128, c0:c0 + ncol]), otile.o(otile.t[:, :ncol]),
                      writes=(), pwrites=[hout[:]])


class Glob:
    pass


def dram(nc, name, shape, dt, kind="Internal", nbuf=1):
    return Tile(nc.dram_tensor(name, list(shape), dt, kind=kind).ap(), nbuf, name)


def split_segs(kind, col0, n, dst, r0, **kw):
    out = []
    if kind == "fm":
        M = kw.get("M", 128)
        for m0 in range(0, n, M):
            out.append(dict(kind="fm", col0=col0 + m0, n=min(M, n - m0), dst=dst, r0=r0 + m0, **kw))
    else:
        c = col0
        while c < col0 + n:
            e = min(col0 + n, (c // 512 + 1) * 512)
            out.append(dict(kind="tm", col0=c, n=e - c, dst=dst, r0=r0 + (c - col0)))
            c = e
    return out


def build(cfg, stage="full", outs=()):
    nc = bass.Bass("TRN2", target_bir_lowering=False)
    D, FF, N, L, KC, FC, T, NT = cfg.D, cfg.FF, cfg.N, cfg.L, cfg.KC, cfg.FC, cfg.T, cfg.NT
    A_H, A_KV, B_H = cfg.A_H, cfg.A_KV, cfg.B_H
    AB_SZ = (A_H * 128, A_KV * 128, A_KV * 128, B_H * 128, B_H * 128, B_H * 256, B_H * 256, 4 * B_H)
    AB_IN = sum(AB_SZ)
    AB_MIX = A_H * 128 + B_H * 256
    ABO = [sum(AB_SZ[:i]) for i in range(8)]

    def kind(name):
        return "ExternalOutput" if name in outs else "Internal"

    I = {}

    def inp(name, shape, dt=F32):
        I[name] = dram(nc, name, shape, dt, kind="ExternalInput")
        return I[name]

    def scr(name, shape, dt=F32):
        return dram(nc, name, shape, dt, kind=kind(name))

    hT0 = inp("hT0", [D, N])
    cT = inp("cT", [128, KC, 2])
    mod_w = inp("mod_w", [L, D, 9 * D])
    mod_bT = inp("mod_bT", [128, L, 9 * KC])
    norm_gT = inp("norm_gT", [128, L, 3, KC])
    ffn_w1 = [[inp(f"w1_{l}{j}", [D, FF]) for j in range(2)] for l in range(L)]
    ffn_w3 = [[inp(f"w3_{l}{j}", [D, FF]) for j in range(2)] for l in range(L)]
    ffn_w2 = [[inp(f"w2_{l}{j}", [FF, D]) for j in range(2)] for l in range(L)]
    ab_w_in = inp("ab_w_in", [D, AB_IN])
    ab_w_out = inp("ab_w_out", [AB_MIX, D])
    ab_bgT = inp("ab_bgT", [B_H, 4])
    gqT = inp("gqT", [128, 1])
    gkT = inp("gkT", [128, 1])
    hgT = inp("hgT", [128, 2 * B_H])
    cosA = inp("cosA", [128, T])
    sinA = inp("sinA", [128, T])
    RmA = inp("RmA", [128, 128])
    maskF = inp("maskF", [128, NT // 128, NT])
    maskB = inp("maskB", [128, NT // 128, NT])

    CW, D_H, D_KV = cfg.C_W, cfg.D_H, cfg.D_KV
    G2 = CW // 32
    CD_IN = CW + D_H * 64 + 2 * D_KV * 64
    CD_MIX = CW + D_H * 64
    cd_w_in = inp("cd_w_in", [D, CD_IN])
    cd_w_out = inp("cd_w_out", [CD_MIX, D])
    s5p = {k: inp("s5_" + k, [2, 128, G2]) for k in ("are", "aim", "ldt")}
    s5p.update({k: inp("s5_" + k, [2, 128, G2, 16]) for k in ("bre", "bim", "cre", "cim")})
    dskB = inp("dskB", [CW])
    gluw = inp("gluw", [CW, CW])
    glubT = inp("glubT", [128, CW // 128])
    sinkD = inp("sinkD", [D_H])
    cosD = inp("cosD", [64, T])
    sinD = inp("sinD", [64, T])
    RmD = inp("RmD", [64, 64])
    maskW = inp("maskW", [128, NT // 128 + 2, NT])
    ident = inp("ident", [128, 128])
    mT = inp("mT", [128, 2, 128])
    fgT = inp("fgT", [128, KC])
    outT = dram(nc, "outT", [D, cfg.TOWN], F32, kind="ExternalOutput")
    ws_cd = scr("ws_cd", [(CD_IN + 511) // 512, 128, KC, 512], BF16)
    wos_cd = scr("wos_cd", [KC, 128, CD_MIX // 128, 128], BF16)
    gws = scr("gws", [128, CW // 128, CW], BF16)
    U1 = scr("U1", [N, CW])
    QD = scr("QD", [D_H * 64, N])
    KD = scr("KD", [D_KV * 64, N])
    VD = scr("VD", [N, D_KV * 64], BF16)
    YS = scr("YS", [N, CW])
    YM2 = scr("YM2", [CD_MIX, N], BF16)
    w13s = [[scr(f"w13s_{l}{j}", [FC, 128, 2, KC, 128], BF16) for j in range(2)] for l in range(L)]
    w2s = [[scr(f"w2s_{l}{j}", [KC, 128, FC, 128], BF16) for j in range(2)] for l in range(L)]
    ws_ab = scr("ws_ab", [(AB_IN + 511) // 512, 128, KC, 512], BF16)
    wos_ab = scr("wos_ab", [KC, 128, AB_MIX // 128, 128], BF16)
    hA = scr("hA", [D, N])
    hB = scr("hB", [D, N])
    QA = scr("QA", [A_H * 128, N])
    KA = scr("KA", [A_KV * 128, N])
    VA = scr("VA", [N, A_KV * 128], BF16)
    QB = scr("QB", [B_H * 128, N], BF16)
    KB = scr("KB", [B_H * 128, N], BF16)
    VB = scr("VB", [N, B_H * 256], BF16)
    OB = scr("OB", [B_H * 256, N])
    GTd = scr("GTd", [4 * B_H, N])
    YM = scr("YM", [AB_MIX, N], BF16)
    AROW = scr("AROW", [2, B_H, N])
    NEGA = scr("NEGA", [2, B_H, N])
    CLMP = scr("CLMP", [2, B_H, N])

    with ExitStack() as es:
        P = Prog(nc, es)
        G = Glob()
        top = Ctx(P)
        es.enter_context(top)
        G.ones = top.sb([128, 128], F32, name="ones")
        G.MODS = top.sb([128, L, 9, KC, 2], F32, name="MODS")
        G.GS = top.sb([128, L, 3, KC, 2], F32, name="GS")
        G.GT = top.sb([128, L, 3, KC, 2], F32, name="GT")
        G.eps = top.sb([128, 1], F32, name="eps")
        P.memset("pool", G.ones[:], 1.0)
        P.memset("pool", G.eps[:], EPS)

        cast_ffn_weights(P, cfg, ffn_w1[0][0], ffn_w3[0][0], ffn_w2[0][0], w13s[0][0], w2s[0][0])
        cast_cols(P, ab_w_in, ws_ab, AB_IN)
        cast_rows(P, ab_w_out, wos_ab, KC)
        cast_ffn_weights(P, cfg, ffn_w1[0][1], ffn_w3[0][1], ffn_w2[0][1], w13s[0][1], w2s[0][1])
        cast_ffn_weights(P, cfg, ffn_w1[1][0], ffn_w3[1][0], ffn_w2[1][0], w13s[1][0], w2s[1][0])
        cast_cols(P, cd_w_in, ws_cd, CD_IN)
        P.dma("pool", gws[:], gluw.o(gluw.t[:, :].rearrange("(kc p) n -> p kc n", p=128)))
        cast_rows(P, cd_w_out, wos_cd, KC)
        cast_ffn_weights(P, cfg, ffn_w1[1][1], ffn_w3[1][1], ffn_w2[1][1], w13s[1][1], w2s[1][1])
        mod_phase(P, cfg, G, cT, mod_w, mod_bT, norm_gT, layers=[0])
        ffn_phase(P, cfg, G, 0, 0, hT0, hA, w13s[0][0], w2s[0][0], cfg.tiles)
        segs = (split_segs("fm", ABO[0], AB_SZ[0], QA, 0) + split_segs("fm", ABO[1], AB_SZ[1], KA, 0)
                + split_segs("tm", ABO[2], AB_SZ[2], VA, 0) + split_segs("fm", ABO[3], AB_SZ[3], QB, 0, scale=128 ** -0.5)
                + split_segs("fm", ABO[4], AB_SZ[4], KB, 0) + split_segs("tm", ABO[5], AB_SZ[5], VB, 0)
                + split_segs("fm", ABO[6], AB_SZ[6], OB, 0) + split_segs("fm", ABO[7], AB_SZ[7], GTd, 0, M=4 * B_H))
        inproj_phase(P, cfg, G, 0, hA, ws_ab, AB_IN, segs, cfg.tiles)
        if stage == "inproj0":
            return nc, P
        attnA_phase(P, cfg, G, QA, KA, VA, YM, gqT, gkT, cosA, sinA, RmA, want_ctx=True)
        if stage == "attnA":
            return nc, P
        mlstm_gates(P, cfg, G, GTd, ab_bgT, AROW, NEGA, CLMP,
                    hook=lambda cx: mod_phase(P, cfg, G, cT, mod_w, mod_bT, norm_gT, layers=list(range(1, L)), cext=cx))
        mlstm_phase(P, cfg, G, QB, KB, VB, OB, YM, A_H * 128, AROW, NEGA, CLMP, hgT, maskF, maskB, want_ctx=True)
        if stage == "mix0":
            return nc, P
        ffn_phase(P, cfg, G, 0, 1, hA, hB, w13s[0][1], w2s[0][1], cfg.tiles,
                  pre=dict(YM=YM, wos=wos_ab, MC=AB_MIX // 128))
        if stage == "layer0":
            return nc, P
        lat_tiles = [t for t in cfg.tiles if t[2] == 0]
        ffn_phase(P, cfg, G, 1, 0, hB, hA, w13s[1][0], w2s[1][0], cfg.tiles)
        segs = (split_segs("tm", 0, CW, U1, 0) + split_segs("fm", CW, D_H * 64, QD, 0, M=64, own=True)
                + split_segs("fm", CW + D_H * 64, D_KV * 64, KD, 0, M=64) + split_segs("tm", CW + D_H * 64 + D_KV * 64, D_KV * 64, VD, 0))
        inproj_phase(P, cfg, G, 1, hA, ws_cd, CD_IN, segs, cfg.tiles)
        if stage == "inproj1":
            return nc, P
        attnD_phase(P, cfg, G, QD, KD, VD, YM2, CW, sinkD, cosD, sinD, RmD, maskW)
        if stage == "attnD":
            return nc, P
        s5_phase(P, cfg, G, U1, s5p, YS, ident, mT)
        if stage == "s5":
            return nc, P
        s5_post_phase(P, cfg, G, YS, U1, dskB, gws, glubT, YM2, ident)
        if stage == "mix1":
            return nc, P
        ffn_phase(P, cfg, G, 1, 1, hA, hB, w13s[1][1], w2s[1][1], cfg.own_tiles, pre=dict(YM=YM2, wos=wos_cd, MC=CD_MIX // 128))
        final_phase(P, cfg, G, hB, outT, fgT)
        P.barrier(full=True)
    return nc, P


def rope_tables(T, grid_w, head_dim):
    rows = T // grid_w
    r, cidx = np.meshgrid(np.arange(rows, dtype=np.float32), np.arange(grid_w, dtype=np.float32), indexing="ij")
    r, cidx = r.reshape(-1), cidx.reshape(-1)
    n_freq = head_dim // 4
    inv = (np.float32(10000.0) ** (-np.arange(n_freq, dtype=np.float32) / np.float32(n_freq))).astype(np.float32)
    ang = np.concatenate([r[:, None] * inv, cidx[:, None] * inv], axis=-1).astype(np.float32)
    cos, sin = np.cos(ang).astype(np.float32), np.sin(ang).astype(np.float32)
    cosT = np.ascontiguousarray(np.concatenate([cos, cos], axis=1).T)
    sinT = np.ascontiguousarray(np.concatenate([sin, sin], axis=1).T)
    half = head_dim // 2
    Rm = np.zeros((head_dim, head_dim), np.float32)
    for dp in range(half):
        Rm[dp + half, dp] = -1.0
        Rm[dp, dp + half] = 1.0
    return cosT, sinT, Rm


def prepare(cfg, inp, b, flip=False):
    D, KC, L, T, NT = cfg.D, cfg.KC, cfg.L, cfg.T, cfg.NT
    f = np.ascontiguousarray
    m = {}
    xs, cs = inp["x"][b], inp["ctx"][b]
    if flip:
        xs, cs = xs[::-1], cs[::-1]
    m["hT0"] = f(np.concatenate([xs.T, cs.T], axis=1))
    c2 = np.stack([inp["c"][b], inp["c_ctx"]])
    m["cT"] = f(c2.reshape(2, KC, 128).transpose(2, 1, 0))
    m["mod_w"] = inp["mod_w"]
    m["mod_bT"] = f(inp["mod_b"].reshape(L, 9 * KC, 128).transpose(2, 0, 1))
    m["norm_gT"] = f(inp["norm_g"].reshape(L, 3, KC, 128).transpose(3, 0, 1, 2))
    for l in range(L):
        for j in range(2):
            m[f"w1_{l}{j}"] = inp["ffn_w1"][l, j]
            m[f"w3_{l}{j}"] = inp["ffn_w3"][l, j]
            m[f"w2_{l}{j}"] = inp["ffn_w2"][l, j]
    wi, bg = inp["ab_w_in"][0], inp["ab_b_gate"][0]
    if flip:
        nb_ = 2 * cfg.B_H
        wi = np.concatenate([wi[:, :-2 * nb_], wi[:, -nb_:], wi[:, -2 * nb_:-nb_]], axis=1)
        bg = np.concatenate([bg[nb_:], bg[:nb_]])
    m["ab_w_in"] = f(wi)
    m["ab_w_out"] = inp["ab_w_out"][0]
    m["ab_bgT"] = f(bg.reshape(4, cfg.B_H).T)
    m["gqT"] = f(inp["a_gq"][0].reshape(128, 1))
    m["gkT"] = f(inp["a_gk"][0].reshape(128, 1))
    m["hgT"] = f(inp["b_head_g"][0].reshape(2 * cfg.B_H, 128).T)
    cosA, sinA, RmA = rope_tables(T, cfg.GRID_W, 128)
    if flip:
        cosA, sinA = f(cosA[:, ::-1]), f(sinA[:, ::-1])
    m["cosA"], m["sinA"], m["RmA"] = cosA, sinA, RmA
    NJ = NT // 128
    s = np.arange(128)[:, None, None] + 128 * np.arange(NJ)[None, :, None]
    t = np.arange(NT)[None, None, :]
    m["maskF"] = np.where(s <= t, 0.0, -1e30).astype(np.float32)
    m["maskB"] = np.where(s >= t, 0.0, -1e30).astype(np.float32)
    CW, D_H = cfg.C_W, cfg.D_H
    Gn, G2 = CW // 16, CW // 32
    m["cd_w_in"] = inp["cd_w_in"][0]
    m["cd_w_out"] = inp["cd_w_out"][0]
    dsel = slice(None, None, -1) if flip else slice(None)
    m["s5_are"] = f(inp["s5_a_re"][0][dsel].reshape(2, G2, 128).transpose(0, 2, 1))
    m["s5_aim"] = f(inp["s5_a_im"][0][dsel].reshape(2, G2, 128).transpose(0, 2, 1))
    m["s5_ldt"] = f(np.repeat(inp["s5_log_dt"][0][dsel][:, :, None], 64, axis=2).reshape(2, G2, 128).transpose(0, 2, 1))
    m["s5_bre"] = f(inp["s5_b_re"][0][dsel].reshape(2, G2, 128, 16).transpose(0, 2, 1, 3))
    m["s5_bim"] = f(inp["s5_b_im"][0][dsel].reshape(2, G2, 128, 16).transpose(0, 2, 1, 3))
    m["s5_cre"] = f(inp["s5_c_re"][0][dsel].transpose(0, 1, 3, 2).reshape(2, G2, 128, 16).transpose(0, 2, 1, 3))
    m["s5_cim"] = f(inp["s5_c_im"][0][dsel].transpose(0, 1, 3, 2).reshape(2, G2, 128, 16).transpose(0, 2, 1, 3))
    m["dskB"] = f(inp["s5_d"][0])
    m["gluw"] = inp["s5_glu_w"][0]
    m["glubT"] = f(inp["s5_glu_b"][0].reshape(CW // 128, 128).T)
    m["sinkD"] = f(inp["d_sink"][0])
    cosD, sinD, RmD = rope_tables(T, cfg.GRID_W, 64)
    if flip:
        cosD, sinD = f(cosD[:, ::-1]), f(sinD[:, ::-1])
    m["cosD"], m["sinD"], m["RmD"] = cosD, sinD, RmD
    s2 = np.arange(128)[:, None, None]
    jj = np.arange(-1, NJ + 1)[None, :, None]
    m["maskW"] = np.where(np.abs(t - 128 * jj - s2) <= 128, 0.0, -1e30).astype(np.float32)
    m["ident"] = np.eye(128, dtype=np.float32)
    ii = (np.arange(128) // 16)
    mT = np.zeros((128, 2, 128), np.float32)
    mT[:, 0, :] = (ii[:, None] <= ii[None, :])
    mT[:, 1, :] = (ii[:, None] >= ii[None, :])
    m["mT"] = mT
    m["fgT"] = f(inp["final_g"].reshape(KC, 128).T)
    return m


def cast_cols(P, w, ws, IN):
    nblk = (IN + 511) // 512
    for b in range(nblk):
        wd = min(512, IN - b * 512)
        src = w.t[:, b * 512:b * 512 + wd].rearrange("(kc p) n -> p kc n", p=128)
        P.dma("pool", ws.o(ws.t[b, :, :, :wd]), w.o(src), writes=(), pwrites=[ws[:]])


def inproj_phase(P, cfg, G, l, hin, ws, IN, segs, tiles):
    KC, NT = cfg.KC, cfg.NT
    nblk = (IN + 511) // 512
    byblk = [[s for s in segs if s["col0"] // 512 == b] for b in range(nblk)]
    with Ctx(P) as c:
        Hb = [c.sb([128, KC, NT], F32, name="H") for _ in range(2)]
        hnb = [c.sb([128, KC, NT], BF16, nbuf=KC, name="hn") for _ in range(2)]
        sq = [c.sb([128, NT], F32, name="sq") for _ in range(2)]
        tmp = [c.sb([128, NT], F32, name="tmp") for _ in range(2)]
        rstdb = [c.sb([128, NT], F32, name="rstd") for _ in range(2)]
        wb = [c.sb([128, KC, 512], BF16, name="wb") for _ in range(3)]
        ev32 = [c.sb([128, NT], F32, name="ev32") for _ in range(3)]
        ev16 = [c.sb([128, NT], BF16, name="ev16") for _ in range(3)]
        ps_ssq = c.ps([128, NT], F32, name="ssq")
        psF = [c.ps([128, NT], F32, name="psF") for _ in range(3)]
        def ldw(i, slot):
            b = i % nblk
            wd = min(512, IN - b * 512)
            P.dma("sp", slot.o(slot.t[:, :, :wd]), ws.o(ws.t[b, :, :, :wd]))

        st = Stream(wb, len(tiles) * nblk, ldw)
        st.pump()
        nev = 0

        def load_norm(ti_):
            c0_, ncol_, wch_ = tiles[ti_]
            H_ = Hb[ti_ % 2]
            src_ = hin.t[:, c0_:c0_ + ncol_].rearrange("(kc p) n -> p kc n", p=128)
            P.dma("act", H_.o(H_.t[:, :, :ncol_]), hin.o(src_))
            norm_mod_tile(P, cfg, G, l, 1, wch_, H_, hnb[ti_ % 2], ncol_, sq, tmp, ps_ssq, rstdb[ti_ % 2])

        load_norm(0)
        for ti, (c0, ncol, wch) in enumerate(tiles):
            hn = hnb[ti % 2]
            for b in range(nblk):
                if b == 1 and ti + 1 < len(tiles):
                    load_norm(ti + 1)
                i = ti * nblk + b
                st.pump()
                w = st.get(i)
                for sg in byblk[b]:
                    if sg.get("own") and not (wch == 0 and c0 < cfg.TOWN):
                        continue
                    lc = sg["col0"] - b * 512
                    dst = sg["dst"]
                    if sg["kind"] == "fm":
                        mw = sg["n"]
                        ps = psF[nev % 3]
                        for kc in range(KC):
                            P.mm(ps.o(ps.t[:mw, :ncol]), w.o(w.t[:, kc, lc:lc + mw]),
                                 hn.s(kc, (slice(None), kc, slice(0, ncol))), start=(kc == 0), stop=(kc == KC - 1))
                        ev = (ev16 if dst.t.dtype == BF16 else ev32)[nev % 3]
                        if nev % 2 == 0:
                            P.act(ev.o(ev.t[:mw, :ncol]), ps.o(ps.t[:mw, :ncol]), AF.Copy, scale=sg.get("scale", 1.0))
                        else:
                            P.ts("dve", ev.o(ev.t[:mw, :ncol]), ps.o(ps.t[:mw, :ncol]), sg.get("scale", 1.0), None, ALU.mult)
                        r = sg["r0"]
                        P.dma("sp", dst.o(dst.t[r:r + mw, c0:c0 + ncol]), ev.o(ev.t[:mw, :ncol]), writes=(), pwrites=[dst[:]])
                        nev += 1
                    else:
                        n = sg["n"]
                        for t0 in range(0, ncol, 128):
                            ps = psF[nev % 3]
                            for kc in range(KC):
                                P.mm(ps.o(ps.t[:, :n]), hn.s(kc, (slice(None), kc, slice(t0, t0 + 128))),
                                     w.o(w.t[:, kc, lc:lc + n]), start=(kc == 0), stop=(kc == KC - 1))
                            ev = (ev16 if dst.t.dtype == BF16 else ev32)[nev % 3]
                            if nev % 2 == 0:
                                P.act(ev.o(ev.t[:, :n]), ps.o(ps.t[:, :n]), AF.Copy)
                            else:
                                P.copy("dve", ev.o(ev.t[:, :n]), ps.o(ps.t[:, :n]))
                            P.dma("sp", dst.o(dst.t[c0 + t0:c0 + t0 + 128, sg["r0"]:sg["r0"] + n]), ev.o(ev.t[:, :n]),
                                  writes=(), pwrites=[dst[:]])
                            nev += 1
                st.done(i)


def qk_prep(P, c, G, src, r0, dh, c0, ncol, out, gain, cosT, sinT, Rm, bufs, rope, norm):
    x, sqt, rot, t1, ps_s, ps_r, rs = bufs
    P.dma("act", x.o(x.t[:dh, :ncol]), src.o(src.t[r0:r0 + dh, c0:c0 + ncol]))
    if norm:
        P.act(sqt.o(sqt.t[:dh, :ncol]), x.o(x.t[:dh, :ncol]), AF.Square)
        P.mm(ps_s.o(ps_s.t[:dh, :ncol]), G.ones.o(G.ones.t[:dh, :dh]), sqt.o(sqt.t[:dh, :ncol]), start=True, stop=True)
        P.act(rs.o(rs.t[:dh, :ncol]), ps_s.o(ps_s.t[:dh, :ncol]), AF.Sqrt, bias=G.eps.o(G.eps.t[:dh, 0:1]), scale=1.0 / dh)
        P.op("dve", lambda E: E.reciprocal(out=rs.t[:dh, :ncol], in_=rs.t[:dh, :ncol]), reads=[rs[:]], writes=[rs[:]])
        P.ts("dve", x.o(x.t[:dh, :ncol]), x.o(x.t[:dh, :ncol]), gain, None, ALU.mult)
    if rope:
        P.mm(ps_r.o(ps_r.t[:dh, :ncol]), Rm.o(Rm.t[:dh, :dh]), x.o(x.t[:dh, :ncol]), start=True, stop=True)
        P.tt("dve", rot.o(rot.t[:dh, :ncol]), ps_r.o(ps_r.t[:dh, :ncol]), sinT.o(sinT.t[:dh, c0:c0 + ncol]), ALU.mult)
        P.tt("pool", t1.o(t1.t[:dh, :ncol]), x.o(x.t[:dh, :ncol]), cosT.o(cosT.t[:dh, c0:c0 + ncol]), ALU.mult)
        if norm:
            P.tt("dve", t1.o(t1.t[:dh, :ncol]), t1.o(t1.t[:dh, :ncol]), rot.o(rot.t[:dh, :ncol]), ALU.add)
            P.tt("dve", out, t1.o(t1.t[:dh, :ncol]), rs.o(rs.t[:dh, :ncol]), ALU.mult)
        else:
            P.tt("dve", out, t1.o(t1.t[:dh, :ncol]), rot.o(rot.t[:dh, :ncol]), ALU.add)
    else:
        if norm:
            P.tt("dve", out, x.o(x.t[:dh, :ncol]), rs.o(rs.t[:dh, :ncol]), ALU.mult)
        else:
            P.copy("dve", out, x.o(x.t[:dh, :ncol]))


def pipelined(n, stage1, stage2, la=2):
    for i in range(min(la, n)):
        stage1(i)
    for i in range(n):
        if i + la < n:
            stage1(i + la)
        stage2(i)


def prep_bufs(c, NT):
    return (c.sb([128, NT], F32, name="px"), c.sb([128, NT], F32, name="psq"), c.sb([128, NT], F32, name="prot"),
            c.sb([128, NT], F32, name="pt1"), c.ps([128, NT], F32, name="pps"), c.ps([128, NT], F32, name="ppr"),
            c.sb([128, NT], F32, name="prs"))


def attnA_phase(P, cfg, G, QA, KA, VA, YM, gq, gk, cosA, sinA, RmA, want_ctx):
    NT, N, T = cfg.NT, cfg.N, cfg.T
    NCH = N // 128
    TCH = T // 128
    grp = cfg.A_H // cfg.A_KV
    scale = 128 ** -0.5
    with Ctx(P) as c:
        cosT = c.sb([128, T], F32, name="cosT")
        sinT = c.sb([128, T], F32, name="sinT")
        Rm = c.sb([128, 128], F32, name="Rm")
        gqt = c.sb([128, 1], F32, name="gq")
        gkt = c.sb([128, 1], F32, name="gk")
        ones16 = c.sb([128, 128], BF16, name="ones16")
        KT = c.sb([128, N], BF16, name="KT")
        V = c.sb([128, NCH, 128], BF16, name="V")
        Qt = [c.sb([128, NT], BF16, name="Qt") for _ in range(2)]
        Et = [c.sb([128, NT], BF16, name="Et") for _ in range(5)]
        rd = c.sb([128, NT], F32, name="rd")
        osb = c.sb([128, NT], F32, name="osb")
        ob = [c.sb([128, NT], BF16, name="ob") for _ in range(2)]
        pb = prep_bufs(c, NT)
        psS = [c.ps([128, NT], F32, name="psS") for _ in range(4)]
        psO = [c.ps([128, NT], F32, name="psO") for _ in range(1)]
        psD = [c.ps([128, NT], F32, name="psD") for _ in range(1)]
        P.dma("sp", cosT[:], cosA[:])
        P.dma("sp", sinT[:], sinA[:])
        P.dma("sp", Rm[:], RmA[:])
        P.dma("sp", gqt[:], gq[:])
        P.dma("sp", gkt[:], gk[:])
        P.memset("pool", ones16[:], 1.0)
        nq = 0
        for g in range(cfg.A_KV):
            for (c0, ncol, wch) in cfg.tiles:
                qk_prep(P, c, G, KA, g * 128, 128, c0, ncol, KT.o(KT.t[:, c0:c0 + ncol]), gkt[:, 0:1], cosT, sinT, Rm, pb,
                        rope=(wch == 0), norm=True)
            lvl = getattr(cfg, "dbg_attn", 9)
            if lvl == 1:
                P.dma("sp", YM.o(YM.t[g * 128:(g + 1) * 128, :]), KT[:], writes=(), pwrites=[YM[:]])
                continue
            P.dma("sp", V[:], VA.o(VA.t[:, g * 128:(g + 1) * 128].rearrange("(c p) d -> p c d", p=128)))
            if lvl == 2:
                P.dma("sp", YM.o(YM.t[g * 128:(g + 1) * 128, :]), KT[:], writes=(), pwrites=[YM[:]])
                continue
            items = [(g * grp + hh, tl) for hh in range(grp) for tl in cfg.tiles if not (tl[2] == 1 and not want_ctx)]

            def qprep(ii, base=nq, items=items):
                h_, (c0_, ncol_, wch_) = items[ii]
                q_ = Qt[(base + ii) % 2]
                qk_prep(P, c, G, QA, h_ * 128, 128, c0_, ncol_, q_.o(q_.t[:, :ncol_]), gqt[:, 0:1], cosT, sinT, Rm, pb,
                        rope=(wch_ == 0), norm=True)

            qprep(0)
            for ii, (h, (c0, ncol, wch)) in enumerate(items):
                if True:
                    q = Qt[nq % 2]
                    if ii + 1 < len(items):
                        qprep(ii + 1)
                    if lvl == 3:
                        P.dma("sp", YM.o(YM.t[h * 128:(h + 1) * 128, c0:c0 + ncol]), q.o(q.t[:, :ncol]), writes=(), pwrites=[YM[:]])
                        nq += 1
                        continue
                    chunks = list(range(NCH)) if wch == 0 else list(range(TCH, NCH))
                    if lvl == 4:
                        chunks = chunks[:2]
                    O, Dn = psO[0], psD[0]
                    nch = len(chunks)

                    def st1(ci, chunks=chunks, q=q, ncol=ncol):
                        S, E = psS[ci % 4], Et[ci % 5]
                        ch = chunks[ci]
                        P.mm(S.o(S.t[:, :ncol]), KT.o(KT.t[:, ch * 128:(ch + 1) * 128]), q.o(q.t[:, :ncol]), start=True, stop=True)
                        P.act(E.o(E.t[:, :ncol]), S.o(S.t[:, :ncol]), AF.Exp, scale=scale)

                    def st2(ci, chunks=chunks, ncol=ncol, nch=nch, O=O, Dn=Dn):
                        E = Et[ci % 5]
                        ch = chunks[ci]
                        P.mm(O.o(O.t[:, :ncol]), V.o(V.t[:, ch, :]), E.o(E.t[:, :ncol]), start=(ci == 0), stop=(ci == nch - 1), inc=True)
                        P.mm(Dn.o(Dn.t[:, :ncol]), ones16[:, :], E.o(E.t[:, :ncol]), start=(ci == 0), stop=(ci == nch - 1), inc=True)

                    pipelined(nch, st1, st2, la=3)
                    P.copy("act", rd.o(rd.t[:, :ncol]), Dn.o(Dn.t[:, :ncol]))
                    P.copy("act", osb.o(osb.t[:, :ncol]), O.o(O.t[:, :ncol]))
                    P.op("dve", lambda E_: E_.reciprocal(out=rd.t[:, :ncol], in_=rd.t[:, :ncol]), reads=[rd[:]], writes=[rd[:]])
                    o = ob[nq % 2]
                    P.tt("dve", o.o(o.t[:, :ncol]), osb.o(osb.t[:, :ncol]), rd.o(rd.t[:, :ncol]), ALU.mult)
                    P.dma("sp", YM.o(YM.t[h * 128:(h + 1) * 128, c0:c0 + ncol]), o.o(o.t[:, :ncol]), writes=(), pwrites=[YM[:]])
                    nq += 1


def logscan(P, e, X, off, n, pad, op, sgn):
    cur = 0
    s = 1
    while s < n:
        a, b = X[cur], X[1 - cur]
        P.tt(e, b.o(b.t[:, off:off + n]), a.o(a.t[:, off:off + n]), a.o(a.t[:, off - sgn * s:off - sgn * s + n]), op)
        cur = 1 - cur
        s *= 2
    return cur


def mlstm_gates(P, cfg, G, GTd, bgT, AROW, NEGA, CLMP, hook=None):
    NH, N, T, CTX = cfg.B_H, cfg.N, cfg.T, cfg.CTX
    PAD = 1
    while PAD * 2 < N:
        PAD *= 2
    for d in range(2):
        with Ctx(P) as c:
            e = "dve"
            off = PAD if d == 0 else 0
            padlo, padhi = (0, PAD) if d == 0 else (N, N + PAD)
            sgn = 1 if d == 0 else -1
            X = [c.sb([NH, N + PAD], F32, name=f"X{d}{i}") for i in range(2)]
            ic = c.sb([NH, N], F32, name="ic")
            fl = c.sb([NH, N], F32, name="fl")
            Fs = c.sb([NH, N], F32, name="Fs")
            bg = c.sb([NH, 4], F32, name="bg")
            nbf = c.sb([NH, 1], F32, name="nbf")
            one = c.sb([NH, 1], F32, name="one")
            if hook is not None and d == 0:
                hook(c)
            P.dma("sp", bg[:], bgT[:])
            P.memset(e, one[:], 1.0)
            P.ts(e, nbf[:], bg.o(bg.t[:, 2 * d + 1:2 * d + 2]), -1.0, None, ALU.mult)
            segs = [(T, CTX, 0), (0, T, CTX)] if d == 0 else [(0, N, 0)]
            for (m0, ln, s0) in segs:
                P.dma("sp", ic.o(ic.t[:, s0:s0 + ln]), GTd.o(GTd.t[(2 * d) * NH:(2 * d + 1) * NH, m0:m0 + ln]), writes=(), pwrites=[ic[:]])
                P.dma("sp", fl.o(fl.t[:, s0:s0 + ln]), GTd.o(GTd.t[(2 * d + 1) * NH:(2 * d + 2) * NH, m0:m0 + ln]), writes=(), pwrites=[fl[:]])
            P.ts(e, ic[:], ic[:], bg.o(bg.t[:, 2 * d:2 * d + 1]), None, ALU.add)
            P.act(fl[:], fl[:], AF.Exp, bias=nbf[:, 0:1], scale=-1.0)
            P.act(fl[:], fl[:], AF.Ln, bias=one[:, 0:1], scale=1.0)
            for x in X:
                P.memset(e, x.o(x.t[:, padlo:padhi]), 0.0, pw=True)
            P.ts(e, X[0].o(X[0].t[:, off:off + N]), fl[:], -1.0, None, ALU.mult, pw=True)
            r = logscan(P, e, X, off, N, PAD, ALU.add, sgn)
            P.copy(e, Fs[:], X[r].o(X[r].t[:, off:off + N]))
            for x in X:
                P.memset(e, x.o(x.t[:, padlo:padhi]), -1e30, pw=True)
            P.tt(e, ic[:], ic[:], Fs[:], ALU.subtract)
            P.copy(e, X[0].o(X[0].t[:, off:off + N]), ic[:], pw=True)
            r = logscan(P, e, X, off, N, PAD, ALU.max, sgn)
            P.ts(e, fl[:], X[r].o(X[r].t[:, off:off + N]), 0.0, -1.0, ALU.max, ALU.mult)
            P.tt(e, Fs[:], fl[:], Fs[:], ALU.subtract)
            P.act(Fs[:], Fs[:], AF.Exp)
            for (m0, ln, s0) in segs:
                for (dst, srct) in ((AROW, ic), (NEGA, fl), (CLMP, Fs)):
                    P.dma("sp", dst.o(dst.t[d, :, m0:m0 + ln]), srct.o(srct.t[:, s0:s0 + ln]), writes=(), pwrites=[dst[:]])


def mlstm_phase(P, cfg, G, QB, KB, VB, OB, YM, ym_r0, AROW, NEGA, CLMP, hgT, maskF, maskB, want_ctx):
    NT, N, T, NH = cfg.NT, cfg.N, cfg.T, cfg.B_H
    NCH, TCH = N // 128, T // 128
    NJ = NT // 128
    with Ctx(P) as c:
        mk = [c.sb([128, NJ, NT], F32, name=f"mk{d}") for d in range(2)]
        P.dma("sp", mk[0][:], maskF[:])
        P.dma("sp", mk[1][:], maskB[:])
        hg = c.sb([128, NH * 2], F32, name="hg")
        P.dma("sp", hg[:], hgT[:])
        KT = c.sb([128, N], BF16, name="KT")
        Vx = c.sb([128, NCH, 384], BF16, name="Vx")
        AC = [c.sb([128, NCH], F32, name=f"AC{d}") for d in range(2)]
        Qtb = [c.sb([128, NT], BF16, name="Qt") for _ in range(2)]
        NAb = [[c.sb([128, NT], F32, name="NA") for _ in range(2)] for _ in range(2)]
        CLb = [[c.sb([128, NT], F32, name="CL") for _ in range(2)] for _ in range(2)]
        ogb = [[c.sb([128, NT], F32, name="og") for _ in range(2)] for _ in range(2)]
        arg = [c.sb([128, NT], F32, name="arg") for _ in range(2)]
        Wt = [c.sb([128, NT], F32, name="Wt") for _ in range(3)]
        Pm = [c.sb([128, NT], BF16, name="Pm") for _ in range(5)]
        rr = c.sb([128, NT], F32, name="rr")
        hs = [c.sb([128, NT], F32, name=f"hs{m}") for m in range(2)]
        hb = [c.sb([128, NT], F32, name=f"hb{m}") for m in range(2)]
        sqv = [c.sb([128, NT], F32, name="sqv") for _ in range(2)]
        yo = [c.sb([128, NT], BF16, name="yo") for _ in range(2)]
        psS = [c.ps([128, NT], F32, name="psS") for _ in range(4)]
        psN = [c.ps([128, NT], F32, name=f"psN{m}") for m in range(3)]
        psQ = c.ps([128, NT], F32, name="psQ")
        P.memset("pool", Vx.o(Vx.t[:, :, 256:384]), 1.0, pw=True)
        for h in range(NH):
            P.dma("sp", KT[:], KB.o(KB.t[h * 128:(h + 1) * 128, :]))
            P.dma("sp", Vx.o(Vx.t[:, :, 0:256]), VB.o(VB.t[:, h * 256:(h + 1) * 256].rearrange("(c p) d -> p c d", p=128)),
                  writes=(), pwrites=[Vx[:]])
            for d in range(2):
                P.dma("sp", AC[d][:], AROW.o(AROW.t[d, h, :].rearrange("(c p) -> p c", p=128)), allow_slow_non_contiguous=True)
            mtiles = [tl for tl in cfg.tiles if not (tl[2] == 1 and not want_ctx)]

            def loads(ii, h=h, mtiles=mtiles):
                c0_, ncol_, _ = mtiles[ii]
                pq = ii % 2
                P.dma("sp", Qtb[pq].o(Qtb[pq].t[:, :ncol_]), QB.o(QB.t[h * 128:(h + 1) * 128, c0_:c0_ + ncol_]))
                for d_ in range(2):
                    P.dma("sp", NAb[pq][d_].o(NAb[pq][d_].t[:, :ncol_]), NEGA.o(NEGA.t[d_, h, c0_:c0_ + ncol_].partition_broadcast(128)))
                    P.dma("sp", CLb[pq][d_].o(CLb[pq][d_].t[:, :ncol_]), CLMP.o(CLMP.t[d_, h, c0_:c0_ + ncol_].partition_broadcast(128)))
                for m_ in range(2):
                    P.dma("sp", ogb[pq][m_].o(ogb[pq][m_].t[:, :ncol_]),
                          OB.o(OB.t[h * 256 + m_ * 128:h * 256 + (m_ + 1) * 128, c0_:c0_ + ncol_]))

            loads(0)
            for ti_, (c0, ncol, wch) in enumerate(mtiles):
                if ti_ + 1 < len(mtiles):
                    loads(ti_ + 1)
                Qt, NA, CL, og = Qtb[ti_ % 2], NAb[ti_ % 2], CLb[ti_ % 2], ogb[ti_ % 2]
                nj = ncol // 128
                for d in range(2):
                    na, cl = NA[d], CL[d]
                    ch0 = c0 // 128
                    diag = [(ch0 + j, j) for j in range(nj)]
                    if wch == 0:
                        ctxc = [(ch, None) for ch in range(TCH, NCH)]
                        if d == 0:
                            chunks = ctxc + [(ch, None) for ch in range(0, ch0)] + diag
                        else:
                            chunks = ctxc + diag + [(ch, None) for ch in range(ch0 + nj, TCH)]
                    else:
                        chunks = diag
                    nch = len(chunks)

                    def st1(ci, chunks=chunks, ncol=ncol, d=d, na=na):
                        ch, mj = chunks[ci]
                        S, W, pm = psS[ci % 4], Wt[ci % 3], Pm[ci % 5]
                        P.mm(S.o(S.t[:, :ncol]), KT.o(KT.t[:, ch * 128:(ch + 1) * 128]), Qt.o(Qt.t[:, :ncol]), start=True, stop=True)
                        if mj is None:
                            P.act(W.o(W.t[:, :ncol]), na.o(na.t[:, :ncol]), AF.Exp, bias=AC[d].o(AC[d].t[:, ch:ch + 1]))
                        else:
                            ag = arg[ci % 2]
                            P.tt("pool", ag.o(ag.t[:, :ncol]), na.o(na.t[:, :ncol]), mk[d].o(mk[d].t[:, mj, :ncol]), ALU.add)
                            P.act(W.o(W.t[:, :ncol]), ag.o(ag.t[:, :ncol]), AF.Exp, bias=AC[d].o(AC[d].t[:, ch:ch + 1]))
                        P.tt("dve", pm.o(pm.t[:, :ncol]), S.o(S.t[:, :ncol]), W.o(W.t[:, :ncol]), ALU.mult)

                    def st2(ci, chunks=chunks, ncol=ncol, nch=nch):
                        ch, mj = chunks[ci]
                        pm = Pm[ci % 5]
                        for m in range(3):
                            P.mm(psN[m].o(psN[m].t[:, :ncol]), Vx.o(Vx.t[:, ch, m * 128:(m + 1) * 128]), pm.o(pm.t[:, :ncol]),
                                 start=(ci == 0), stop=(ci == nch - 1), inc=True)

                    pipelined(nch, st1, st2, la=3)
                    P.act(rr.o(rr.t[:, :ncol]), psN[2].o(psN[2].t[:, :ncol]), AF.Abs)
                    for m in range(2):
                        dstt = hs[m] if d == 0 else hb[m]
                        P.copy("act", dstt.o(dstt.t[:, :ncol]), psN[m].o(psN[m].t[:, :ncol]))
                    P.tt("dve", rr.o(rr.t[:, :ncol]), rr.o(rr.t[:, :ncol]), cl.o(cl.t[:, :ncol]), ALU.max)
                    P.op("dve", lambda E_: E_.reciprocal(out=rr.t[:, :ncol], in_=rr.t[:, :ncol]), reads=[rr[:]], writes=[rr[:]])
                    for m in range(2):
                        dstt = hs[m] if d == 0 else hb[m]
                        P.tt("dve", dstt.o(dstt.t[:, :ncol]), dstt.o(dstt.t[:, :ncol]), rr.o(rr.t[:, :ncol]), ALU.mult)
                for m in range(2):
                    P.tt("pool", hs[m].o(hs[m].t[:, :ncol]), hs[m].o(hs[m].t[:, :ncol]), hb[m].o(hb[m].t[:, :ncol]), ALU.add)
                    P.act(sqv[m].o(sqv[m].t[:, :ncol]), hs[m].o(hs[m].t[:, :ncol]), AF.Square)
                    P.mm(psQ.o(psQ.t[:, :ncol]), G.ones[:, :], sqv[m].o(sqv[m].t[:, :ncol]), start=(m == 0), stop=(m == 1), inc=True)
                    P.act(og[m].o(og[m].t[:, :ncol]), og[m].o(og[m].t[:, :ncol]), AF.Sigmoid)
                P.act(rr.o(rr.t[:, :ncol]), psQ.o(psQ.t[:, :ncol]), AF.Sqrt, bias=G.eps[:, 0:1], scale=1.0 / 256)
                P.op("dve", lambda E_: E_.reciprocal(out=rr.t[:, :ncol], in_=rr.t[:, :ncol]), reads=[rr[:]], writes=[rr[:]])
                for m in range(2):
                    P.tt("dve", hs[m].o(hs[m].t[:, :ncol]), hs[m].o(hs[m].t[:, :ncol]), rr.o(rr.t[:, :ncol]), ALU.mult)
                    P.stt("dve", yo[m].o(yo[m].t[:, :ncol]), hs[m].o(hs[m].t[:, :ncol]), hg.o(hg.t[:, 2 * h + m:2 * h + m + 1]),
                          og[m].o(og[m].t[:, :ncol]), ALU.mult, ALU.mult)
                    r0 = ym_r0 + h * 256 + m * 128
                    P.dma("sp", YM.o(YM.t[r0:r0 + 128, c0:c0 + ncol]), yo[m].o(yo[m].t[:, :ncol]), writes=(), pwrites=[YM[:]])


def attnD_phase(P, cfg, G, QD, KD, VD, YM, ym_r0, sinkD, cosD, sinD, RmD, maskW):
    NT, N, T = cfg.NT, cfg.N, cfg.T
    NCH, TCH = N // 128, T // 128
    NJ = NT // 128
    grp = cfg.D_H // cfg.D_KV
    scale = 64 ** -0.5
    lat_tiles = cfg.own_tiles
    with Ctx(P) as c:
        cosT = c.sb([64, T], F32, name="cosT")
        sinT = c.sb([64, T], F32, name="sinT")
        Rm = c.sb([64, 64], F32, name="Rm")
        mw = c.sb([128, NJ + 2, NT], F32, name="mw")
        se = c.sb([64, cfg.D_H], F32, name="se")
        ones16 = c.sb([128, 64], BF16, name="ones16")
        KT = c.sb([64, N], BF16, name="KT")
        V = c.sb([128, NCH, 64], BF16, name="V")
        Qt = [c.sb([64, NT], BF16, name="Qt") for _ in range(2)]
        Et = [c.sb([128, NT], BF16, name="Et") for _ in range(5)]
        ag = [c.sb([128, NT], F32, name="ag") for _ in range(2)]
        rd = c.sb([64, NT], F32, name="rd")
        osb = c.sb([64, NT], F32, name="osb")
        ob = [c.sb([64, NT], BF16, name="ob") for _ in range(2)]
        pb = prep_bufs(c, NT)
        psS = [c.ps([128, NT], F32, name="psS") for _ in range(4)]
        psO = [c.ps([128, NT], F32, name="psO") for _ in range(1)]
        psD = [c.ps([128, NT], F32, name="psD") for _ in range(1)]
        P.dma("sp", cosT[:], cosD[:])
        P.dma("sp", sinT[:], sinD[:])
        P.dma("sp", Rm[:], RmD[:])
        P.dma("sp", mw[:], maskW[:])
        P.dma("sp", se[:], sinkD.o(sinkD.t[:].partition_broadcast(64)))
        P.act(se[:], se[:], AF.Exp)
        P.memset("pool", ones16[:], 1.0)
        nq = 0
        for g in range(cfg.D_KV):
            for (c0, ncol, wch) in cfg.tiles:
                qk_prep(P, c, G, KD, g * 64, 64, c0, ncol, KT.o(KT.t[:, c0:c0 + ncol]), None, cosT, sinT, Rm, pb,
                        rope=(wch == 0), norm=False)
            P.dma("sp", V[:], VD.o(VD.t[:, g * 64:(g + 1) * 64].rearrange("(c p) d -> p c d", p=128)))
            items = [(g * grp + hh, tl) for hh in range(grp) for tl in lat_tiles]

            def qprep(ii, base=nq, items=items):
                h_, (c0_, ncol_, wch_) = items[ii]
                q_ = Qt[(base + ii) % 2]
                qk_prep(P, c, G, QD, h_ * 64, 64, c0_, ncol_, q_.o(q_.t[:, :ncol_]), None, cosT, sinT, Rm, pb, rope=True, norm=False)

            qprep(0)
            for ii, (h, (c0, ncol, wch)) in enumerate(items):
                if True:
                    q = Qt[nq % 2]
                    if ii + 1 < len(items):
                        qprep(ii + 1)
                    ch0 = c0 // 128
                    nj = ncol // 128
                    chunks = [(ch, None) for ch in range(TCH, NCH)]
                    chunks += [(ch0 + jj, jj + 1) for jj in range(-1, nj + 1) if 0 <= ch0 + jj < TCH]
                    O, Dn = psO[0], psD[0]
                    nch = len(chunks)

                    def st1(ci, chunks=chunks, q=q, ncol=ncol):
                        ch, mj = chunks[ci]
                        S, E = psS[ci % 4], Et[ci % 5]
                        P.mm(S.o(S.t[:, :ncol]), KT.o(KT.t[:, ch * 128:(ch + 1) * 128]), q.o(q.t[:, :ncol]), start=True, stop=True)
                        if mj is None:
                            P.act(E.o(E.t[:, :ncol]), S.o(S.t[:, :ncol]), AF.Exp, scale=scale)
                        else:
                            a = ag[ci % 2]
                            P.stt("dve", a.o(a.t[:, :ncol]), S.o(S.t[:, :ncol]), scale, mw.o(mw.t[:, mj, :ncol]), ALU.mult, ALU.add)
                            P.act(E.o(E.t[:, :ncol]), a.o(a.t[:, :ncol]), AF.Exp)

                    def st2(ci, chunks=chunks, ncol=ncol, nch=nch, O=O, Dn=Dn):
                        ch, mj = chunks[ci]
                        E = Et[ci % 5]
                        P.mm(O.o(O.t[:64, :ncol]), V.o(V.t[:, ch, :]), E.o(E.t[:, :ncol]), start=(ci == 0), stop=(ci == nch - 1), inc=True)
                        P.mm(Dn.o(Dn.t[:64, :ncol]), ones16[:, :], E.o(E.t[:, :ncol]), start=(ci == 0), stop=(ci == nch - 1), inc=True)

                    pipelined(nch, st1, st2, la=3)
                    P.act(rd.o(rd.t[:, :ncol]), Dn.o(Dn.t[:64, :ncol]), AF.Identity, bias=se.o(se.t[:, h:h + 1]))
                    P.copy("act", osb.o(osb.t[:, :ncol]), O.o(O.t[:64, :ncol]))
                    P.op("dve", lambda E_: E_.reciprocal(out=rd.t[:, :ncol], in_=rd.t[:, :ncol]), reads=[rd[:]], writes=[rd[:]])
                    o = ob[nq % 2]
                    P.tt("dve", o.o(o.t[:, :ncol]), osb.o(osb.t[:, :ncol]), rd.o(rd.t[:, :ncol]), ALU.mult)
                    r0 = ym_r0 + h * 64
                    P.dma("sp", YM.o(YM.t[r0:r0 + 64, c0:c0 + ncol]), o.o(o.t[:, :ncol]), writes=(), pwrites=[YM[:]])
                    nq += 1


def final_phase(P, cfg, G, hin, out, fgT):
    KC, NT = cfg.KC, cfg.NT
    with Ctx(P) as c:
        fg = c.sb([128, KC], F32, name="fg")
        P.dma("sp", fg[:], fgT[:])
        H = [c.sb([128, KC, NT], F32, name="H") for _ in range(2)]
        sq = [c.sb([128, NT], F32, name="sq") for _ in range(2)]
        rstd = c.sb([128, NT], F32, name="rstd")
        ps = c.ps([128, NT], F32, name="ssq")
        for ti, (c0, ncol, wch) in enumerate(cfg.own_tiles):
            Ht = H[ti % 2]
            P.dma("sp", Ht.o(Ht.t[:, :, :ncol]), hin.o(hin.t[:, c0:c0 + ncol].rearrange("(kc p) n -> p kc n", p=128)))
            for kc in range(KC):
                s = sq[kc % 2]
                P.act(s.o(s.t[:, :ncol]), Ht.o(Ht.t[:, kc, :ncol]), AF.Square)
                P.mm(ps.o(ps.t[:, :ncol]), G.ones[:, :], s.o(s.t[:, :ncol]), start=(kc == 0), stop=(kc == KC - 1), inc=True)
            P.act(rstd.o(rstd.t[:, :ncol]), ps.o(ps.t[:, :ncol]), AF.Sqrt, bias=G.eps[:, 0:1], scale=1.0 / cfg.D)
            P.op("dve", lambda E: E.reciprocal(out=rstd.t[:, :ncol], in_=rstd.t[:, :ncol]), reads=[rstd[:]], writes=[rstd[:]])
            for kc in range(KC):
                P.stt("dve", Ht.o(Ht.t[:, kc, :ncol]), Ht.o(Ht.t[:, kc, :ncol]), fg.o(fg.t[:, kc:kc + 1]),
                      rstd.o(rstd.t[:, :ncol]), ALU.mult, ALU.mult)
            P.dma("sp", out.o(out.t[:, c0:c0 + ncol].rearrange("(kc p) n -> p kc n", p=128)), Ht.o(Ht.t[:, :, :ncol]),
                  writes=(), pwrites=[out[:]])


def bc(ap, axis, shape):
    return ap.unsqueeze(axis).broadcast_to(list(shape))


def s5_phase(P, cfg, G, U, prm, YS, ident, mT):
    N, T, CTX, CW = cfg.N, cfg.T, cfg.CTX, cfg.C_W
    G2 = CW // 32
    NC8, LC, CC = N // 8, T // 8, CTX // 8
    PADC = 1
    while PADC < NC8:
        PADC *= 2
    PADC //= 2
    NST = 0
    while (1 << NST) < NC8:
        NST += 1
    BP = 2
    PI = math.pi
    e = "dve"
    cblocks = [(c0, min(128, LC - c0)) for c0 in range(0, LC, 128)] + [(LC + c0, min(128, CC - c0)) for c0 in range(0, CC, 128)]
    with Ctx(P) as c:
        idt = c.sb([128, 128], F32, name="idt")
        P.dma("sp", idt[:], ident[:])
        mTt = c.sb([128, 2, 128], F32, name="mTt")
        P.dma("sp", mTt[:], mT[:])
        negpi = c.sb([128, 1], F32, name="negpi")
        P.memset(e, negpi[:], -PI)
        Bb = [[c.sb([128, G2, 16], F32, name=f"Bb{d}{r}") for r in range(2)] for d in range(2)]
        Cm = [[c.sb([128, G2, 16], F32, name=f"Cm{d}{r}") for r in range(2)] for d in range(2)]
        TB_ = [[[c.sb([128, G2, 8], F32, name=f"T{t}{d}{r}") for r in range(2)] for d in range(2)] for t in range(4)]
        PK = [[c.sb([128, G2, NST], F32, name=f"PK{d}{r}") for r in range(3)] for d in range(2)]
        with Ctx(P) as c2:
            def t2(name):
                return c2.sb([128, G2], F32, name=name)
            for d in range(2):
                are, aim, dt, adt, th = t2("are"), t2("aim"), t2("dt"), t2("adt"), t2("th")
                P.dma("sp", are[:], prm["are"].o(prm["are"].t[d]))
                P.dma("sp", aim[:], prm["aim"].o(prm["aim"].t[d]))
                P.dma("sp", dt[:], prm["ldt"].o(prm["ldt"].t[d]))
                braw = [c2.sb([128, G2, 16], F32, name=f"braw{r}") for r in range(2)]
                P.dma("sp", braw[0][:], prm["bre"].o(prm["bre"].t[d]))
                P.dma("sp", braw[1][:], prm["bim"].o(prm["bim"].t[d]))
                P.dma("sp", Cm[d][0][:], prm["cre"].o(prm["cre"].t[d]))
                P.dma("sp", Cm[d][1][:], prm["cim"].o(prm["cim"].t[d]))
                P.act(dt[:], dt[:], AF.Exp)
                P.tt(e, adt[:], are[:], dt[:], ALU.mult)
                P.tt(e, th[:], aim[:], dt[:], ALU.mult)
                mag, magm, ang, sn, cs, pr, pi_, qr, qi, angf = [t2(f"w{i}") for i in range(10)]
                angi = c2.sb([128, G2], mybir.dt.int32, name="angi")
                lam = None
                for n in range(9):
                    P.act(mag[:], adt[:], AF.Exp, scale=float(n))
                    P.act(magm[:], adt[:], AF.Exp, scale=-float(n))
                    for (dst_, ph) in ((sn, 0.0), (cs, 0.25)):
                        P.ts(e, ang[:], th[:], float(n) / (2 * PI), ph, ALU.mult, ALU.add)
                        P.copy(e, angi[:], ang[:])
                        P.copy(e, angf[:], angi[:])
                        P.tt(e, ang[:], ang[:], angf[:], ALU.subtract)
                        P.ts(e, angf[:], ang[:], 0.5, None, ALU.is_gt)
                        P.tt(e, ang[:], ang[:], angf[:], ALU.subtract)
                        P.ts(e, angf[:], ang[:], -0.5, None, ALU.is_lt)
                        P.tt(e, ang[:], ang[:], angf[:], ALU.add)
                        P.act(dst_[:], ang[:], AF.Sin, scale=2 * PI)
                    P.tt(e, pr[:], mag[:], cs[:], ALU.mult)
                    P.tt(e, pi_[:], mag[:], sn[:], ALU.mult)
                    P.tt(e, qr[:], magm[:], cs[:], ALU.mult)
                    P.stt(e, qi[:], magm[:], -1.0, sn[:], ALU.mult, ALU.mult)
                    if d == 0:
                        place = [(0, n, "q"), (1, n, "p"), (2, 7 - n, "p"), (3, n - 1, "p")]
                    else:
                        place = [(0, 7 - n, "q"), (1, 7 - n, "p"), (2, n, "p"), (3, 8 - n, "p")]
                    for (tb, blk, w) in place:
                        if blk < 0 or blk > 7:
                            continue
                        srcs = (pr, pi_) if w == "p" else (qr, qi)
                        for r in range(2):
                            tt_ = TB_[tb][d][r]
                            P.copy(e, tt_.o(tt_.t[:, :, blk]), srcs[r][:], pw=True)
                    if n == 1:
                        nr, den, kr, ki, t0 = t2("nr"), t2("den"), t2("kr"), t2("ki"), t2("t0")
                        P.ts(e, nr[:], pr[:], -1.0, None, ALU.add)
                        P.tt(e, den[:], are[:], are[:], ALU.mult)
                        P.tt(e, t0[:], aim[:], aim[:], ALU.mult)
                        P.tt(e, den[:], den[:], t0[:], ALU.add)
                        P.op(e, lambda E_: E_.reciprocal(out=den.t[:], in_=den.t[:]), reads=[den[:]], writes=[den[:]])
                        P.tt(e, kr[:], nr[:], are[:], ALU.mult)
                        P.tt(e, t0[:], pi_[:], aim[:], ALU.mult)
                        P.tt(e, kr[:], kr[:], t0[:], ALU.add)
                        P.tt(e, kr[:], kr[:], den[:], ALU.mult)
                        P.tt(e, ki[:], pi_[:], are[:], ALU.mult)
                        P.tt(e, t0[:], nr[:], aim[:], ALU.mult)
                        P.tt(e, ki[:], ki[:], t0[:], ALU.subtract)
                        P.tt(e, ki[:], ki[:], den[:], ALU.mult)
                        sh = [128, G2, 16]
                        tb16 = c2.sb(sh, F32, name="tb16")
                        krb, kib = bc(kr.t[:], 2, sh), bc(ki.t[:], 2, sh)
                        P.tt(e, Bb[d][0][:], braw[0][:], kr.o(krb), ALU.mult)
                        P.tt(e, tb16[:], braw[1][:], ki.o(kib), ALU.mult)
                        P.tt(e, Bb[d][0][:], Bb[d][0][:], tb16[:], ALU.subtract)
                        P.tt(e, Bb[d][1][:], braw[1][:], kr.o(krb), ALU.mult)
                        P.tt(e, tb16[:], braw[0][:], ki.o(kib), ALU.mult)
                        P.tt(e, Bb[d][1][:], Bb[d][1][:], tb16[:], ALU.add)
                    if n == 8:
                        P.copy(e, PK[d][0].o(PK[d][0].t[:, :, 0]), pr[:], pw=True)
                        P.copy(e, PK[d][1].o(PK[d][1].t[:, :, 0]), pi_[:], pw=True)
                for k in range(1, NST):
                    r0_, i0_ = PK[d][0].t[:, :, k - 1], PK[d][1].t[:, :, k - 1]
                    P.tt(e, mag[:], PK[d][0].o(r0_), PK[d][0].o(r0_), ALU.mult)
                    P.tt(e, magm[:], PK[d][1].o(i0_), PK[d][1].o(i0_), ALU.mult)
                    P.tt(e, PK[d][0].o(PK[d][0].t[:, :, k]), mag[:], magm[:], ALU.subtract, pw=True)
                    P.stt(e, PK[d][1].o(PK[d][1].t[:, :, k]), PK[d][0].o(r0_), 2.0, PK[d][1].o(i0_), ALU.mult, ALU.mult, pw=True)
                P.ts(e, PK[d][2][:], PK[d][1][:], -1.0, None, ALU.mult)
        shm = [128, BP, 8, 16]
        Mb = [[[c.sb(shm, F32, name=f"M{q}{d}{i}") for i in range(8)] for d in range(2)] for q in range(2)]
        tmpm = c.sb(shm, F32, name="tmpm")
        Toepb = [[[c.sb([128, 128], F32, name=f"Toep{q}{d}{gl}") for gl in range(2 * BP)] for d in range(2)] for q in range(2)]
        BcTb = [[[[[c.sb([128, 128], F32, name=f"BcT{q}{d}{pl}{r}{par}") for par in range(2)] for r in range(2)] for pl in range(BP)]
                 for d in range(2)] for q in range(2)]
        for q in range(2):
            for d in range(2):
                for pl in range(BP):
                    for r in range(2):
                        for par in range(2):
                            P.memset("pool", BcTb[q][d][pl][r][par][:], 0.0)
        Ugb = [c.sb([128, 2 * BP, NC8], F32, name=f"Ug{q}") for q in range(2)]
        Ut = [c.sb([128, 2 * BP, 8, 16], F32, name="Ut") for _ in range(2)]
        Yt = [c.sb([128, 8, 16 * 2 * BP], F32, name="Yt") for _ in range(2)]
        X = [[c.sb([128, BP, 2, PADC + NC8], F32, name=f"X{d}{pp}") for pp in range(2)] for d in range(2)]
        for d in range(2):
            for pp in range(2):
                lo, hi = (0, PADC) if d == 0 else (NC8, NC8 + PADC)
                P.memset("pool", X[d][pp].o(X[d][pp].t[:, :, :, lo:hi]), 0.0, pw=True)
        psT_ = c.ps([128, 512], F32, name="psT")
        psTr_ = c.ps([128, 512], F32, name="psTr")
        psT = Tile(psT_.t[:, 0:128], 1, "psT")
        psTr = Tile(psTr_.t[:, 0:128], 1, "psTr")
        psZ = [c.ps([128, 512], F32, name=f"psZ{r}") for r in range(2)]
        psZc_ = [c.ps([128, 512], F32, name=f"psZc{r}") for r in range(2)]
        psY_ = [c.ps([128, 512], F32, name="psY") for _ in range(2)]
        psY = [Tile(t_.t[:, 0:128], 1, "psY") for t_ in psY_]
        gw = 16 * 2 * BP
        nb = G2 // BP
        nut = [0]

        def stageA(b):
            q = b % 2
            p0 = b * BP
            Ug, M, Toep, BcT = Ugb[q], Mb[q], Toepb[q], BcTb[q]
            for (cc0, cw) in cblocks:
                ut = Ut[nut[0] % 2]
                nut[0] += 1
                for gl in range(2 * BP):
                    src = U.t[cc0 * 8:(cc0 + cw) * 8, b * gw + gl * 16:b * gw + (gl + 1) * 16].rearrange("(c i) w -> c i w", i=8)
                    P.dma("sp", ut.o(ut.t[:cw, gl]), U.o(src), writes=(), pwrites=[ut[:]])
                for gl in range(2 * BP):
                    P.transpose(psTr.o(psTr.t[:, :cw]), ut.o(ut.t[:cw, gl].rearrange("p a b -> p (a b)")), idt.o(idt.t[:cw, :cw]))
                    P.copy("act", Ug.o(Ug.t[:, gl, cc0:cc0 + cw]), psTr.o(psTr.t[:, :cw]), pw=True)
            for d in range(2):
                for mi, (tb, src, neg) in enumerate(((0, Bb, False), (1, Cm, True), (2, Bb, False), (3, Cm, True))):
                    tr, ti = (bc(TB_[tb][d][r].t[:, p0:p0 + BP, :], 3, shm) for r in range(2))
                    sr, si = (bc(src[d][r].t[:, p0:p0 + BP, :], 2, shm) for r in range(2))
                    Mr, Mi = M[d][2 * mi], M[d][2 * mi + 1]
                    T0, T1, S0, S1 = TB_[tb][d][0], TB_[tb][d][1], src[d][0], src[d][1]
                    P.op(e, lambda E_, Mr=Mr, tr=tr, sr=sr: E_.tensor_tensor(out=Mr.t[:], in0=tr, in1=sr, op=ALU.mult), reads=[T0[:], S0[:]], writes=[Mr[:]])
                    P.op(e, lambda E_, ti=ti, si=si: E_.tensor_tensor(out=tmpm.t[:], in0=ti, in1=si, op=ALU.mult), reads=[T1[:], S1[:]], writes=[tmpm[:]])
                    P.tt(e, Mr[:], Mr[:], tmpm[:], ALU.subtract)
                    P.op(e, lambda E_, Mi=Mi, tr=tr, si=si: E_.tensor_tensor(out=Mi.t[:], in0=tr, in1=si, op=ALU.mult), reads=[T0[:], S1[:]], writes=[Mi[:]])
                    P.op(e, lambda E_, ti=ti, sr=sr: E_.tensor_tensor(out=tmpm.t[:], in0=ti, in1=sr, op=ALU.mult), reads=[T1[:], S0[:]], writes=[tmpm[:]])
                    if neg:
                        P.stt(e, Mi[:], Mi[:], -1.0, tmpm[:], ALU.mult, ALU.subtract)
                    else:
                        P.tt(e, Mi[:], Mi[:], tmpm[:], ALU.add)
                Gr, Gi, Hr, nHi, Bcr, Bci, Ccr, nCci = M[d]
                for pl in range(BP):
                    for par in range(2):
                        rows = slice(par * 64, par * 64 + 64)
                        gl = 2 * pl + par
                        fl = lambda t_, pl=pl, rows=rows: t_.o(t_.t[rows, pl].rearrange("p a b -> p (a b)"))
                        P.mm(psT[:], fl(Gr), fl(Hr), start=True, stop=False, inc=True)
                        P.mm(psT[:], fl(Gi), fl(nHi), start=False, stop=True)
                        P.tt(e, Toep[d][gl][:], psT[:], mTt.o(mTt.t[:, d, :]), ALU.mult)
                    for r, Bm in enumerate((Bcr, Bci)):
                        P.transpose(psTr[:], Bm.o(Bm.t[:, pl].rearrange("p a b -> p (a b)")), idt[:])
                        for par in range(2):
                            cols = slice(par * 64, par * 64 + 64)
                            bt = BcT[d][pl][r][par]
                            P.copy("act", bt.o(bt.t[:, cols]), psTr.o(psTr.t[:, cols]))

        def stageB(b):
            q = b % 2
            p0 = b * BP
            Ug, M, Toep, BcT = Ugb[q], Mb[q], Toepb[q], BcTb[q]
            for d in range(2):
                off = PADC if d == 0 else 0
                for pl in range(BP):
                    xt = X[d][0]
                    for r in range(2):
                        bt0, bt1 = BcT[d][pl][r]
                        z, zc = psZ[r], psZc_[r]
                        P.mm(z.o(z.t[:, :LC]), bt0[:], Ug.o(Ug.t[:, 2 * pl, 0:LC]), start=True, stop=False, inc=True)
                        P.mm(z.o(z.t[:, :LC]), bt1[:], Ug.o(Ug.t[:, 2 * pl + 1, 0:LC]), start=False, stop=True)
                        P.mm(zc.o(zc.t[:, :CC]), bt0[:], Ug.o(Ug.t[:, 2 * pl, LC:NC8]), start=True, stop=False, inc=True)
                        P.mm(zc.o(zc.t[:, :CC]), bt1[:], Ug.o(Ug.t[:, 2 * pl + 1, LC:NC8]), start=False, stop=True)
                        if d == 0:
                            P.copy("act", xt.o(xt.t[:, pl, r, off + 1:off + 1 + CC]), zc.o(zc.t[:, :CC]), pw=True)
                            P.copy("act", xt.o(xt.t[:, pl, r, off + 1 + CC:off + NC8]), z.o(z.t[:, :LC - 1]), pw=True)
                            P.memset("act_", xt.o(xt.t[:, pl, r, off:off + 1]), 0.0, pw=True)
                        else:
                            P.copy("act", xt.o(xt.t[:, pl, r, 0:LC - 1]), z.o(z.t[:, 1:LC]), pw=True)
                            P.copy("act", xt.o(xt.t[:, pl, r, LC - 1:NC8 - 1]), zc.o(zc.t[:, :CC]), pw=True)
                            P.memset("act_", xt.o(xt.t[:, pl, r, NC8 - 1:NC8]), 0.0, pw=True)
            ncols = [CC + sum(cw_ for (cc0_, cw_) in cblocks if cc0_ * 8 < cfg.TOWN), NC8]
            curd = [0, 0]
            for k in range(NST):
                s_ = 1 << k
                ops = [[], [], [], []]
                for d in range(2):
                    if s_ >= ncols[d]:
                        continue
                    sgn = 1 if d == 0 else -1
                    off = PADC if d == 0 else 0
                    a, bb = X[d][curd[d]], X[d][1 - curd[d]]
                    curd[d] = 1 - curd[d]
                    NCd = ncols[d]
                    for pl in range(BP):
                        g2 = p0 + pl
                        pr_ = PK[d][0].o(PK[d][0].t[:, g2, k:k + 1])
                        pi_ = PK[d][1].o(PK[d][1].t[:, g2, k:k + 1])
                        npi_ = PK[d][2].o(PK[d][2].t[:, g2, k:k + 1])
                        lo = off - sgn * s_
                        re, im = a.t[:, pl, 0, off:off + NCd], a.t[:, pl, 1, off:off + NCd]
                        res, ims = a.t[:, pl, 0, lo:lo + NCd], a.t[:, pl, 1, lo:lo + NCd]
                        ore, oim = bb.t[:, pl, 0, off:off + NCd], bb.t[:, pl, 1, off:off + NCd]
                        ops[0].append((bb.o(ore), a.o(res), pr_, a.o(re)))
                        ops[1].append((bb.o(oim), a.o(res), pi_, a.o(im)))
                        ops[2].append((bb.o(ore), a.o(ims), npi_, bb.o(ore)))
                        ops[3].append((bb.o(oim), a.o(ims), pr_, bb.o(oim)))
                for grp_ in ops:
                    for (o_, i0_, sc_, i1_) in grp_:
                        P.stt(e, o_, i0_, sc_, i1_, ALU.mult, ALU.add, pw=True)
            for bi, (cc0, cw) in enumerate(cblocks):
                if cc0 * 8 >= cfg.TOWN:
                    continue
                yt = Yt[bi % 2]
                for gl in range(2 * BP):
                    pl, par = gl // 2, gl % 2
                    rows = slice(par * 64, par * 64 + 64)
                    py = psY[gl % 2]
                    P.mm(py.o(py.t[:cw, :]), Ug.o(Ug.t[:, gl, cc0:cc0 + cw]), Toep[0][gl][:], start=True, stop=False, inc=True)
                    P.mm(py.o(py.t[:cw, :]), Ug.o(Ug.t[:, gl, cc0:cc0 + cw]), Toep[1][gl][:], start=False, stop=False, inc=True)
                    for d in range(2):
                        xt = X[d][curd[d]]
                        if d == 0:
                            xc0 = PADC + (CC + cc0 if cc0 < LC else cc0 - LC)
                        else:
                            xc0 = cc0
                        Ccr, nCci = M[d][6], M[d][7]
                        P.mm(py.o(py.t[:cw, :]), xt.o(xt.t[rows, pl, 0, xc0:xc0 + cw]),
                             Ccr.o(Ccr.t[rows, pl].rearrange("p a b -> p (a b)")), start=False, stop=False, inc=True)
                        P.mm(py.o(py.t[:cw, :]), xt.o(xt.t[rows, pl, 1, xc0:xc0 + cw]),
                             nCci.o(nCci.t[rows, pl].rearrange("p a b -> p (a b)")), start=False, stop=(d == 1), inc=True)
                    P.copy("act", yt.o(yt.t[:cw, :, gl * 16:(gl + 1) * 16]), py.o(py.t[:cw, :].rearrange("p (a b) -> p a b", b=16)), pw=True)
                dst = YS.t[cc0 * 8:(cc0 + cw) * 8, b * gw:(b + 1) * gw].rearrange("(c i) w -> c i w", i=8)
                P.dma("sp", YS.o(dst), yt.o(yt.t[:cw]), writes=(), pwrites=[YS[:]])

        stageA(0)
        for b in range(nb):
            if b + 1 < nb:
                stageA(b + 1)
            stageB(b)


def s5_post_phase(P, cfg, G, YS, U, dskB, gluw, glubT, YM, ident):
    CW, NT = cfg.C_W, cfg.NT
    KCc = CW // 128
    with Ctx(P) as c:
        idt = c.sb([128, 128], F32, name="idt")
        P.dma("sp", idt[:], ident[:])
        dsk = c.sb([128, CW], F32, name="dsk")
        P.dma("sp", dsk[:], dskB.o(dskB.t[:].partition_broadcast(128)))
        gb = c.sb([128, KCc], F32, name="gb")
        P.dma("sp", gb[:], glubT[:])
        gw = c.sb([128, KCc, CW], BF16, name="gw")
        P.dma("sp", gw[:], gluw[:])
        y = [c.sb([128, CW], F32, name="y") for _ in range(2)]
        u = [c.sb([128, CW], F32, name="u") for _ in range(2)]
        t = [c.sb([128, CW], F32, name="t") for _ in range(2)]
        g32 = c.sb([128, KCc, NT], F32, name="g32")
        g16 = c.sb([128, KCc, NT], BF16, name="g16")
        sg = [c.sb([128, NT], F32, name="sg") for _ in range(2)]
        ob = [c.sb([128, NT], BF16, name="ob") for _ in range(2)]
        psTr = [c.ps([128, 512], F32, name="psTr") for _ in range(2)]
        psG = [c.ps([128, 512], F32, name="psG") for _ in range(2)]
        k = 0
        for (c0, ncol, wch) in cfg.own_tiles:
            for tc in range(ncol // 128):
                r0 = c0 + tc * 128
                yy, uu, tt_ = y[k % 2], u[k % 2], t[k % 2]
                P.dma("sp", yy[:], YS.o(YS.t[r0:r0 + 128, :]))
                P.dma("act", uu[:], U.o(U.t[r0:r0 + 128, :]))
                P.tt("pool", uu[:], uu[:], dsk[:], ALU.mult)
                P.tt("dve", yy[:], yy[:], uu[:], ALU.add)
                P.tt("pool", tt_[:], yy[:], yy[:], ALU.mult)
                P.ts("dve", tt_[:], tt_[:], 0.044715, 1.0, ALU.mult, ALU.add)
                P.tt("dve", tt_[:], tt_[:], yy[:], ALU.mult)
                P.act(tt_[:], tt_[:], AF.Sigmoid, scale=1.5957691216057308)
                P.tt("dve", yy[:], yy[:], tt_[:], ALU.mult)
                for kc in range(KCc):
                    pt = psTr[kc % 2]
                    P.transpose(pt.o(pt.t[:, :128]), yy.o(yy.t[:, kc * 128:(kc + 1) * 128]), idt[:])
                    P.copy("act", g32.o(g32.t[:, kc, tc * 128:(tc + 1) * 128]), pt.o(pt.t[:, :128]), pw=True)
                    P.copy("dve", g16.o(g16.t[:, kc, tc * 128:(tc + 1) * 128]), g32.o(g32.t[:, kc, tc * 128:(tc + 1) * 128]), pw=True)
                k += 1
            for n in range(KCc):
                pg = psG[n % 2]
                for kc in range(KCc):
                    P.mm(pg.o(pg.t[:, :ncol]), gw.o(gw.t[:, kc, n * 128:(n + 1) * 128]), g16.o(g16.t[:, kc, :ncol]),
                         start=(kc == 0), stop=(kc == KCc - 1))
                s_ = sg[n % 2]
                P.act(s_.o(s_.t[:, :ncol]), pg.o(pg.t[:, :ncol]), AF.Sigmoid, bias=gb.o(gb.t[:, n:n + 1]))
                o = ob[n % 2]
                P.tt("dve", o.o(o.t[:, :ncol]), g32.o(g32.t[:, n, :ncol]), s_.o(s_.t[:, :ncol]), ALU.mult)
                P.dma("sp", YM.o(YM.t[n * 128:(n + 1) * 128, c0:c0 + ncol]), o.o(o.t[:, :ncol]), writes=(), pwrites=[YM[:]])


def kernel(**inputs):
    cfg = Cfg()
    inp = {k: np.asarray(v) for k, v in inputs.items()}
    nc, P = build(cfg)
    B = inp["x"].shape[0]
    in_maps = [prepare(cfg, inp, core // 2, flip=bool(core % 2)) for core in range(2 * B)]
    res = run_bass_kernel_spmd(nc, in_maps, core_ids=list(range(2 * B)))
    out = np.empty((B, cfg.T, cfg.D), np.float32)
    for b in range(B):
        out[b, :cfg.TOWN] = np.asarray(res.results[2 * b]["outT"]).T
        out[b, cfg.TOWN:] = np.asarray(res.results[2 * b + 1]["outT"]).T[::-1]
    return out
```

```python
import math
import numpy as np
from contextlib import ExitStack
import concourse.bass as bass
import concourse.mybir as mybir
from concourse.bass_utils import run_bass_kernel_spmd

F32 = mybir.dt.float32
BF16 = mybir.dt.bfloat16
AF = mybir.ActivationFunctionType
ALU = mybir.AluOpType
AX = mybir.AxisListType
EPS = 1e-6


class Buf:
    __slots__ = ("name", "w", "r")

    def __init__(self, name=""):
        self.name = name
        self.w = []
        self.r = []


class Opnd:
    __slots__ = ("ap", "bufs")

    def __init__(self, ap, bufs):
        self.ap = ap
        self.bufs = bufs


class Tile:
    def __init__(self, t, nbuf=1, name=""):
        self.t = t
        self.bufs = [Buf(f"{name}{i}") for i in range(nbuf)]

    def __getitem__(self, idx):
        return Opnd(self.t[idx], self.bufs)

    def s(self, i, idx):
        if isinstance(i, int):
            return Opnd(self.t[idx], [self.bufs[i]])
        return Opnd(self.t[idx], [self.bufs[j] for j in i])

    def o(self, ap, i=None):
        if i is None:
            return Opnd(ap, self.bufs)
        if isinstance(i, int):
            return Opnd(ap, [self.bufs[i]])
        return Opnd(ap, [self.bufs[j] for j in i])


class Prog:
    def __init__(self, nc, es):
        self.nc = nc
        self.E = {"pe": nc.tensor, "act": nc.scalar, "dve": nc.vector, "pool": nc.gpsimd, "sp": nc.sync}
        self.semobj = {}
        self.cnt = {}
        for k in ("pe", "act", "dve", "pool"):
            self.semobj[k] = es.enter_context(nc.semaphore("s_" + k))
            self.cnt[k] = 0
        self.known = {k: {} for k in self.E}
        self.lanes = {}
        for q, n in (("sp", 20), ("act", 6), ("pool", 10)):
            self.lanes[q] = []
            for i in range(n):
                key = f"l_{q}{i}"
                self.semobj[key] = es.enter_context(nc.semaphore(key))
                self.lanes[q].append([key, 0])
        self.lane_rr = {q: 0 for q in self.lanes}
        self.nins = 0

    def _wait(self, e, deps):
        kn = self.known[e]
        best = {}
        for d in deps:
            k, v = d[0], d[1]
            if kn.get(k, 0) >= v:
                continue
            if best.get(k, 0) < v:
                best[k] = v
        for k, v in best.items():
            self.E[e].wait_ge(self.semobj[k], v)
            kn[k] = v
            self.nins += 1

    def _deps(self, e, reads, writes, pwrites):
        deps = []
        for o in reads:
            for b in o.bufs:
                deps += b.w
        for o in writes:
            for b in o.bufs:
                deps += b.w
                deps += b.r
        for o in pwrites:
            for b in o.bufs:
                deps += b.r
        return deps

    def _register(self, dep, reads, writes, pwrites):
        e = dep[2]
        for o in reads:
            for b in o.bufs:
                if not dep[3]:
                    b.r = [d for d in b.r if d[2] != e or d[3]]
                b.r.append(dep)
        for o in writes:
            for b in o.bufs:
                b.w = [dep]
                b.r = []
        for o in pwrites:
            for b in o.bufs:
                if b.r:
                    b.w = [dep]
                    b.r = []
                else:
                    if not dep[3]:
                        b.w = [d for d in b.w if d[2] != e or d[3]]
                    b.w.append(dep)

    def op(self, e, fn, reads=(), writes=(), pwrites=(), inc=True):
        self._wait(e, self._deps(e, reads, writes, pwrites))
        ins = fn(self.E[e])
        self.nins += 1
        if inc:
            self.cnt[e] += 1
            ins.then_inc(self.semobj[e], 1)
            dep = (e, self.cnt[e], e, False)
        else:
            dep = (e, self.cnt[e] + 1, e, False)
        self._register(dep, reads, writes, pwrites)
        return ins

    def dma(self, q, out, in_, reads=None, writes=None, pwrites=(), **kw):
        reads = [in_] if reads is None else reads
        writes = [out] if writes is None else writes
        lanes = self.lanes[q]
        i = self.lane_rr[q]
        self.lane_rr[q] = (i + 1) % len(lanes)
        lane = lanes[i]
        deps = self._deps(q, reads, writes, pwrites)
        deps.append((lane[0], lane[1] * 16, q, True))
        self._wait(q, deps)
        ins = self.E[q].dma_start(out=out.ap, in_=in_.ap, **kw)
        lane[1] += 1
        ins.then_inc(self.semobj[lane[0]], 16)
        self.nins += 1
        dep = (lane[0], lane[1] * 16, q, True)
        self._register(dep, reads, writes, pwrites)
        return ins

    def barrier(self, full=False):
        deps = [(k, v, k, False) for k, v in self.cnt.items() if v > 0]
        for q, lanes in self.lanes.items():
            if q == "pool" and not full:
                continue
            for key, c in lanes:
                if c > 0:
                    deps.append((key, c * 16, q, True))
        for e in self.E:
            self._wait(e, deps)

    def act(self, out, in_, func, bias=0.0, scale=1.0, extra=(), pw=False, e="act"):
        kw = {}
        rd = [in_] + list(extra)
        b = bias.ap if isinstance(bias, Opnd) else bias
        s = scale.ap if isinstance(scale, Opnd) else scale
        if isinstance(bias, Opnd):
            rd.append(bias)
        if isinstance(scale, Opnd):
            rd.append(scale)
        return self.op("act", lambda E: E.activation(out=out.ap, in_=in_.ap, func=func, bias=b, scale=s),
                       reads=rd, writes=() if pw else [out], pwrites=[out] if pw else ())

    def tt(self, e, out, in0, in1, op, pw=False):
        return self.op(e, lambda E: E.tensor_tensor(out=out.ap, in0=in0.ap, in1=in1.ap, op=op),
                       reads=[in0, in1], writes=() if pw else [out], pwrites=[out] if pw else ())

    def ts(self, e, out, in0, s1, s2, op0, op1=None, pw=False):
        rd = [in0]
        a1 = s1.ap if isinstance(s1, Opnd) else s1
        a2 = s2.ap if isinstance(s2, Opnd) else s2
        if isinstance(s1, Opnd):
            rd.append(s1)
        if isinstance(s2, Opnd):
            rd.append(s2)
        if op1 is None:
            f = lambda E: E.tensor_scalar(out=out.ap, in0=in0.ap, scalar1=a1, scalar2=None, op0=op0)
        else:
            f = lambda E: E.tensor_scalar(out=out.ap, in0=in0.ap, scalar1=a1, scalar2=a2, op0=op0, op1=op1)
        return self.op(e, f, reads=rd, writes=() if pw else [out], pwrites=[out] if pw else ())

    def stt(self, e, out, in0, sc, in1, op0, op1, pw=False):
        rd = [in0, in1]
        a = sc.ap if isinstance(sc, Opnd) else sc
        if isinstance(sc, Opnd):
            rd.append(sc)
        return self.op(e, lambda E: E.scalar_tensor_tensor(out=out.ap, in0=in0.ap, scalar=a, in1=in1.ap, op0=op0, op1=op1),
                       reads=rd, writes=() if pw else [out], pwrites=[out] if pw else ())

    def copy(self, e, out, in_, pw=False):
        if e == "act":
            f = lambda E: E.copy(out=out.ap, in_=in_.ap)
        else:
            f = lambda E: E.tensor_copy(out=out.ap, in_=in_.ap)
        return self.op(e, f, reads=[in_], writes=() if pw else [out], pwrites=[out] if pw else ())

    def memset(self, e, out, val, pw=False):
        if e == "act_":
            assert val == 0.0
            return self.op("act", lambda E: E.memzero(out.ap), reads=(), writes=() if pw else [out],
                           pwrites=[out] if pw else ())
        return self.op(e, lambda E: E.memset(out.ap, val), reads=(), writes=() if pw else [out],
                       pwrites=[out] if pw else ())

    def mm(self, out, lhsT, rhs, start, stop, extra_reads=(), inc=False):
        rd = [lhsT, rhs] + list(extra_reads)
        f = lambda E: E.matmul(out.ap, lhsT.ap, rhs.ap, start=start, stop=stop)
        if start:
            self._wait("pe", self._deps("pe", (), [out], ()))
        self._wait("pe", self._deps("pe", rd, (), ()))
        ins = f(self.E["pe"])
        self.nins += 1
        if stop or inc:
            self.cnt["pe"] += 1
            ins.then_inc(self.semobj["pe"], 1)
            dep = ("pe", self.cnt["pe"], "pe", False)
            self._register(dep, rd, [out] if stop else (), ())
        else:
            dep = ("pe", self.cnt["pe"] + 1, "pe", False)
            self._register(dep, rd, (), ())
        return ins

    def transpose(self, out, in_, ident):
        rd = [in_, ident]
        self._wait("pe", self._deps("pe", rd, [out], ()))
        ins = self.E["pe"].transpose(out.ap, in_.ap, ident.ap)
        self.nins += 1
        self.cnt["pe"] += 1
        ins.then_inc(self.semobj["pe"], 1)
        dep = ("pe", self.cnt["pe"], "pe", False)
        self._register(dep, rd, [out], ())
        return ins


class Ctx:
    def __init__(self, P):
        self.P = P
        self.nc = P.nc
        self.es = ExitStack()
        self.n = 0

    def __enter__(self):
        self.es.__enter__()
        return self

    def __exit__(self, *a):
        self.P.barrier()
        return self.es.__exit__(*a)

    def sb(self, shape, dt, nbuf=1, name="t"):
        self.n += 1
        t = self.es.enter_context(self.nc.sbuf_tensor(f"{name}_{self.P.nins}_{self.n}", list(shape), dt))
        return Tile(t, nbuf, name)

    def ps(self, shape, dt=F32, nbuf=1, name="p"):
        self.n += 1
        t = self.es.enter_context(self.nc.psum_tensor(f"{name}_{self.P.nins}_{self.n}", list(shape), dt))
        return Tile(t, nbuf, name)


class Cfg:
    def __init__(self, **kw):
        self.D = 2048
        self.FF = 5632
        self.T = 4096
        self.CTX = 256
        self.L = 2
        self.NT = 512
        self.A_H, self.A_KV, self.B_H = 8, 2, 4
        self.C_W, self.D_H, self.D_KV = 1024, 16, 2
        self.GRID_W = 64
        self.__dict__.update(kw)
        self.KC = self.D // 128
        self.FC = self.FF // 128
        self.N = self.T + self.CTX
        self.tiles = [(c0, min(self.NT, self.T - c0), 0) for c0 in range(0, self.T, self.NT)]
        self.tiles += [(self.T + c0, min(self.NT, self.CTX - c0), 1) for c0 in range(0, self.CTX, self.NT)]
        self.TOWN = self.T // 2
        self.own_tiles = [t for t in self.tiles if t[2] == 0 and t[0] < self.TOWN]


class Stream:
    def __init__(self, slots, n_items, loader):
        self.slots = slots
        self.n = n_items
        self.loader = loader
        self.next = 0
        self.consumed = 0

    def pump(self):
        while self.next < self.n and self.next - self.consumed < len(self.slots):
            self.loader(self.next, self.slots[self.next % len(self.slots)])
            self.next += 1

    def get(self, i):
        assert i == self.consumed and i < self.next, (i, self.consumed, self.next)
        return self.slots[i % len(self.slots)]

    def done(self, i):
        self.consumed = i + 1


def cast_ffn_weights(P, cfg, w1, w3, w2, w13s, w2s):
    for f in range(cfg.FC):
        for wi, w in enumerate((w1, w3)):
            src = w.t[:, f * 128:(f + 1) * 128].rearrange("(kc p) m -> p kc m", p=128)
            P.dma("pool", w13s.o(w13s.t[f, :, wi, :, :]), w.o(src), writes=(), pwrites=[w13s[:]])
    for d in range(cfg.KC):
        src = w2.t[:, d * 128:(d + 1) * 128].rearrange("(fc p) m -> p fc m", p=128)
        P.dma("pool", w2s.o(w2s.t[d]), w2.o(src), writes=(), pwrites=[w2s[:]])


def cast_rows(P, w, ws, nd):
    for d in range(nd):
        src = w.t[:, d * 128:(d + 1) * 128].rearrange("(fc p) m -> p fc m", p=128)
        P.dma("pool", ws.o(ws.t[d]), w.o(src), writes=(), pwrites=[ws[:]])


def mod_phase(P, cfg, G, cT, mod_w, mod_bT, norm_gT, layers=None, cext=None):
    KC, D, L = cfg.KC, cfg.D, cfg.L
    NB = 9 * D // 512
    with (Ctx(P) if cext is None else ExitStack()) as c_:
        c = c_ if cext is None else cext
        sc = c.sb([128, KC, 2], F32, name="sc")
        mb = c.sb([128, L, 9 * KC], F32, name="mb")
        ng = c.sb([128, L, 3, KC], F32, name="ng")
        wb = [c.sb([128, KC, 512], F32, name=f"wb{i}") for i in range(2)]
        ps = c.ps([128, 9 * KC, 2], F32, name="modps")
        P.dma("sp", sc[:], cT[:])
        P.dma("sp", mb[:], mod_bT[:])
        P.dma("sp", ng[:], norm_gT[:])
        P.act(sc[:], sc[:], AF.Silu)
        for l in (range(L) if layers is None else layers):
            for blk in range(NB):
                w = wb[blk % 2]
                src = mod_w.t[l, :, blk * 512:(blk + 1) * 512].rearrange("(kc p) n -> p kc n", p=128)
                P.dma("sp" if blk % 2 == 0 else "act", w[:], mod_w.o(src))
                for n4 in range(4):
                    cc = blk * 4 + n4
                    for kc in range(KC):
                        P.mm(ps.o(ps.t[:, cc, :]), w.o(w.t[:, kc, n4 * 128:(n4 + 1) * 128]), sc.o(sc.t[:, kc, :]),
                             start=(kc == 0), stop=(kc == KC - 1))
            mods_l = G.MODS.t[:, l].rearrange("p j k w -> p (j k) w")
            for wch in range(2):
                P.tt("dve", G.MODS.o(mods_l[:, :, wch]), ps.o(ps.t[:, :, wch]), mb.o(mb.t[:, l, :]), ALU.add, pw=True)
            for j in range(3):
                for wch in range(2):
                    P.stt("dve", G.GS.o(G.GS.t[:, l, j, :, wch]), G.MODS.o(G.MODS.t[:, l, 3 * j + 1, :, wch]), 1.0,
                          ng.o(ng.t[:, l, j, :]), ALU.add, ALU.mult, pw=True)
                    P.ts("dve", G.GT.o(G.GT.t[:, l, j, :, wch]), G.MODS.o(G.MODS.t[:, l, 3 * j + 2, :, wch]),
                         1.0 if j == 1 else 0.5, None, ALU.mult, pw=True)


def norm_mod_tile(P, cfg, G, l, j, wch, H, hn, ncol, sq, tmp, ps_ssq, rstd):
    KC = cfg.KC
    for kc in range(KC):
        s = sq[kc % 2]
        P.act(s.o(s.t[:, :ncol]), H.o(H.t[:, kc, :ncol]), AF.Square)
        P.mm(ps_ssq.o(ps_ssq.t[:, :ncol]), G.ones[:, :], s.o(s.t[:, :ncol]), start=(kc == 0), stop=(kc == KC - 1), inc=True)
    P.act(rstd.o(rstd.t[:, :ncol]), ps_ssq.o(ps_ssq.t[:, :ncol]), AF.Sqrt, bias=G.eps[:, 0:1], scale=1.0 / cfg.D)
    P.op("dve", lambda E: E.reciprocal(out=rstd.t[:, :ncol], in_=rstd.t[:, :ncol]),
         reads=[rstd[:]], writes=[rstd[:]])
    for kc in range(KC):
        t = tmp[kc % 2]
        P.tt("dve", t.o(t.t[:, :ncol]), H.o(H.t[:, kc, :ncol]), rstd.o(rstd.t[:, :ncol]), ALU.mult)
        P.act(hn.s(kc, (slice(None), kc, slice(0, ncol))), t.o(t.t[:, :ncol]), AF.Identity,
              bias=G.MODS.o(G.MODS.t[:, l, 3 * j, kc, wch:wch + 1]), scale=G.GS.o(G.GS.t[:, l, j, kc, wch:wch + 1]))


def ffn_phase(P, cfg, G, l, j, hin, hout, w13s, w2s, tiles, pre=None):
    KC, FC, NT = cfg.KC, cfg.FC, cfg.NT
    nj = 0 if j == 0 else 2
    with Ctx(P) as c:
        pf = pre is None
        Hb = [c.sb([128, KC, NT], F32, name="H") for _ in range(2 if pf else 1)]
        hn = c.sb([128, KC, NT], BF16, nbuf=KC, name="hn")
        Gt = c.sb([128, FC, NT], BF16, nbuf=FC, name="G")
        sq = [c.sb([128, NT], F32, name="sq") for _ in range(2)]
        tmp = [c.sb([128, NT], F32, name="tmp") for _ in range(2)]
        sil = [c.sb([128, NT], F32, name="sil") for _ in range(2)]
        ot = [c.sb([128, NT], F32, name="ot") for _ in range(2)]
        rstdb = [c.sb([128, NT], F32, name="rstd") for _ in range(2 if pf else 1)]
        w13 = [c.sb([128, 2, KC, 128], BF16, name="w13") for _ in range(3)]
        w2 = [c.sb([128, FC, 128], BF16, name="w2") for _ in range(2)]
        ps_ssq = c.ps([128, NT], F32, name="ssq")
        psA = [c.ps([128, NT], F32, name="psA") for _ in range(2)]
        psB = [c.ps([128, NT], F32, name="psB") for _ in range(2)]
        psO = [c.ps([128, NT], F32, name="psO") for _ in range(2)]
        if pre is not None:
            Yt = c.sb([128, pre["MC"], NT], BF16, name="Yt")
            wob = [c.sb([128, pre["MC"], 128], BF16, name="wob") for _ in range(2)]
        nt = len(tiles)
        s13 = Stream(w13, nt * FC, lambda i, slot: P.dma("sp", slot[:], w13s.o(w13s.t[i % FC])))
        s2 = Stream(w2, nt * KC, lambda i, slot: P.dma("act", slot[:], w2s.o(w2s.t[i % KC])))
        s13.pump()

        def load_h(ti_):
            c0_, ncol_, _ = tiles[ti_]
            H_ = Hb[ti_ % len(Hb)]
            src_ = hin.t[:, c0_:c0_ + ncol_].rearrange("(kc p) n -> p kc n", p=128)
            P.dma("sp", H_.o(H_.t[:, :, :ncol_]), hin.o(src_))

        def do_norm(ti_):
            c0_, ncol_, wch_ = tiles[ti_]
            norm_mod_tile(P, cfg, G, l, nj, wch_, Hb[ti_ % len(Hb)], hn, ncol_, sq, tmp, ps_ssq, rstdb[ti_ % len(rstdb)])

        if pf:
            load_h(0)
            do_norm(0)
        for ti, (c0, ncol, wch) in enumerate(tiles):
            H = Hb[ti % len(Hb)]
            rstd = rstdb[ti % len(rstdb)]
            if pf:
                if ti + 1 < len(tiles):
                    load_h(ti + 1)
            else:
                load_h(ti)
            if pre is not None:
                YM, wos, MC = pre["YM"], pre["wos"], pre["MC"]
                P.dma("act", Yt.o(Yt.t[:, :, :ncol]), YM.o(YM.t[:, c0:c0 + ncol].rearrange("(kc p) n -> p kc n", p=128)))
                for d in range(KC):
                    wo = wob[d % 2]
                    P.dma("sp", wo[:], wos.o(wos.t[d]))
                    o = psO[d % 2]
                    for kc in range(MC):
                        P.mm(o.o(o.t[:, :ncol]), wo.o(wo.t[:, kc, :]), Yt.o(Yt.t[:, kc, :ncol]), start=(kc == 0), stop=(kc == MC - 1))
                    P.stt("dve", H.o(H.t[:, d, :ncol]), o.o(o.t[:, :ncol]), G.GT.o(G.GT.t[:, l, 1, d, wch:wch + 1]),
                          H.o(H.t[:, d, :ncol]), ALU.mult, ALU.add)
            if not pf:
                norm_mod_tile(P, cfg, G, l, nj, wch, H, hn, ncol, sq, tmp, ps_ssq, rstd)
            if getattr(cfg, "dbg_ffn", 0) == 1:
                for d in range(KC):
                    otile = ot[d % 2]
                    P.copy("dve", otile.o(otile.t[:, :ncol]), hn.s(d, (slice(None), d, slice(0, ncol))))
                    P.dma("sp", hout.o(hout.t[d * 128:(d + 1) * 128, c0:c0 + ncol]), otile.o(otile.t[:, :ncol]),
                          writes=(), pwrites=[hout[:]])
                continue
            for f in range(FC):
                i13 = ti * FC + f
                s13.pump()
                s2.pump()
                w = s13.get(i13)
                a, b, sl = psA[f % 2], psB[f % 2], sil[f % 2]
                for kc in range(KC):
                    P.mm(a.o(a.t[:, :ncol]), w.o(w.t[:, 0, kc, :]), hn.s(kc, (slice(None), kc, slice(0, ncol))),
                         start=(kc == 0), stop=(kc == KC - 1))
                for kc in range(KC):
                    P.mm(b.o(b.t[:, :ncol]), w.o(w.t[:, 1, kc, :]), hn.s(kc, (slice(None), kc, slice(0, ncol))),
                         start=(kc == 0), stop=(kc == KC - 1))
                s13.done(i13)
                P.act(sl.o(sl.t[:, :ncol]), a.o(a.t[:, :ncol]), AF.Silu)
                P.tt("dve", Gt.s(f, (slice(None), f, slice(0, ncol))), sl.o(sl.t[:, :ncol]), b.o(b.t[:, :ncol]), ALU.mult)
            if getattr(cfg, "dbg_ffn", 0) == 2:
                for d in range(min(KC, FC)):
                    otile = ot[d % 2]
                    P.copy("dve", otile.o(otile.t[:, :ncol]), Gt.s(d, (slice(None), d, slice(0, ncol))))
                    P.dma("sp", hout.o(hout.t[d * 128:(d + 1) * 128, c0:c0 + ncol]), otile.o(otile.t[:, :ncol]),
                          writes=(), pwrites=[hout[:]])
                continue
            for d in range(KC):
                i2 = ti * KC + d
                s13.pump()
                s2.pump()
                w = s2.get(i2)
                o, otile = psO[d % 2], ot[d % 2]
                for f in range(FC):
                    P.mm(o.o(o.t[:, :ncol]), w.o(w.t[:, f, :]), Gt.s(f, (slice(None), f, slice(0, ncol))),
                         start=(f == 0), stop=(f == FC - 1))
                s2.done(i2)
                if pf and d == 1 and ti + 1 < len(tiles):
                    do_norm(ti + 1)
                P.stt("dve", otile.o(otile.t[:, :ncol]), o.o(o.t[:, :ncol]), G.GT.o(G.GT.t[:, l, nj, d, wch:wch + 1]),
                      H.o(H.t[:, d, :ncol]), ALU.mult, ALU.add)
                P.dma("sp", hout.o(hout.t[d * 128:(d + 1) * 128, c0:c0 + ncol]), otile.o(otile.t[:, :ncol]),
                      writes=(), pwrites=[hout[:]])


class Glob:
    pass


def dram(nc, name, shape, dt, kind="Internal", nbuf=1):
    return Tile(nc.dram_tensor(name, list(shape), dt, kind=kind).ap(), nbuf, name)


def split_segs(kind, col0, n, dst, r0, **kw):
    out = []
    if kind == "fm":
        M = kw.get("M", 128)
        for m0 in range(0, n, M):
            out.append(dict(kind="fm", col0=col0 + m0, n=min(M, n - m0), dst=dst, r0=r0 + m0, **kw))
    else:
        c = col0
        while c < col0 + n:
            e = min(col0 + n, (c // 512 + 1) * 512)
            out.append(dict(kind="tm", col0=c, n=e - c, dst=dst, r0=r0 + (c - col0)))
            c = e
    return out


def build(cfg, stage="full", outs=()):
    nc = bass.Bass("TRN2", target_bir_lowering=False)
    D, FF, N, L, KC, FC, T, NT = cfg.D, cfg.FF, cfg.N, cfg.L, cfg.KC, cfg.FC, cfg.T, cfg.NT
    A_H, A_KV, B_H = cfg.A_H, cfg.A_KV, cfg.B_H
    AB_SZ = (A_H * 128, A_KV * 128, A_KV * 128, B_H * 128, B_H * 128, B_H * 256, B_H * 256, 4 * B_H)
    AB_IN = sum(AB_SZ)
    AB_MIX = A_H * 128 + B_H * 256
    ABO = [sum(AB_SZ[:i]) for i in range(8)]

    def kind(name):
        return "ExternalOutput" if name in outs else "Internal"

    I = {}

    def inp(name, shape, dt=F32):
        I[name] = dram(nc, name, shape, dt, kind="ExternalInput")
        return I[name]

    def scr(name, shape, dt=F32):
        return dram(nc, name, shape, dt, kind=kind(name))

    hT0 = inp("hT0", [D, N])
    cT = inp("cT", [128, KC, 2])
    mod_w = inp("mod_w", [L, D, 9 * D])
    mod_bT = inp("mod_bT", [128, L, 9 * KC])
    norm_gT = inp("norm_gT", [128, L, 3, KC])
    ffn_w1 = [[inp(f"w1_{l}{j}", [D, FF]) for j in range(2)] for l in range(L)]
    ffn_w3 = [[inp(f"w3_{l}{j}", [D, FF]) for j in range(2)] for l in range(L)]
    ffn_w2 = [[inp(f"w2_{l}{j}", [FF, D]) for j in range(2)] for l in range(L)]
    ab_w_in = inp("ab_w_in", [D, AB_IN])
    ab_w_out = inp("ab_w_out", [AB_MIX, D])
    ab_bgT = inp("ab_bgT", [B_H, 4])
    gqT = inp("gqT", [128, 1])
    gkT = inp("gkT", [128, 1])
    hgT = inp("hgT", [128, 2 * B_H])
    cosA = inp("cosA", [128, T])
    sinA = inp("sinA", [128, T])
    RmA = inp("RmA", [128, 128])
    maskF = inp("maskF", [128, NT // 128, NT])
    maskB = inp("maskB", [128, NT // 128, NT])

    CW, D_H, D_KV = cfg.C_W, cfg.D_H, cfg.D_KV
    G2 = CW // 32
    CD_IN = CW + D_H * 64 + 2 * D_KV * 64
    CD_MIX = CW + D_H * 64
    cd_w_in = inp("cd_w_in", [D, CD_IN])
    cd_w_out = inp("cd_w_out", [CD_MIX, D])
    s5p = {k: inp("s5_" + k, [2, 128, G2]) for k in ("are", "aim", "ldt")}
    s5p.update({k: inp("s5_" + k, [2, 128, G2, 16]) for k in ("bre", "bim", "cre", "cim")})
    dskB = inp("dskB", [CW])
    gluw = inp("gluw", [CW, CW])
    glubT = inp("glubT", [128, CW // 128])
    sinkD = inp("sinkD", [D_H])
    cosD = inp("cosD", [64, T])
    sinD = inp("sinD", [64, T])
    RmD = inp("RmD", [64, 64])
    maskW = inp("maskW", [128, NT // 128 + 2, NT])
    ident = inp("ident", [128, 128])
    mT = inp("mT", [128, 2, 128])
    fgT = inp("fgT", [128, KC])
    outT = dram(nc, "outT", [D, cfg.TOWN], F32, kind="ExternalOutput")
    ws_cd = scr("ws_cd", [(CD_IN + 511) // 512, 128, KC, 512], BF16)
    wos_cd = scr("wos_cd", [KC, 128, CD_MIX // 128, 128], BF16)
    gws = scr("gws", [128, CW // 128, CW], BF16)
    U1 = scr("U1", [N, CW])
    QD = scr("QD", [D_H * 64, N])
    KD = scr("KD", [D_KV * 64, N])
    VD = scr("VD", [N, D_KV * 64], BF16)
    YS = scr("YS", [N, CW])
    YM2 = scr("YM2", [CD_MIX, N], BF16)
    w13s = [[scr(f"w13s_{l}{j}", [FC, 128, 2, KC, 128], BF16) for j in range(2)] for l in range(L)]
    w2s = [[scr(f"w2s_{l}{j}", [KC, 128, FC, 128], BF16) for j in range(2)] for l in range(L)]
    ws_ab = scr("ws_ab", [(AB_IN + 511) // 512, 128, KC, 512], BF16)
    wos_ab = scr("wos_ab", [KC, 128, AB_MIX // 128, 128], BF16)
    hA = scr("hA", [D, N])
    hB = scr("hB", [D, N])
    QA = scr("QA", [A_H * 128, N])
    KA = scr("KA", [A_KV * 128, N])
    VA = scr("VA", [N, A_KV * 128], BF16)
    QB = scr("QB", [B_H * 128, N], BF16)
    KB = scr("KB", [B_H * 128, N], BF16)
    VB = scr("VB", [N, B_H * 256], BF16)
    OB = scr("OB", [B_H * 256, N])
    GTd = scr("GTd", [4 * B_H, N])
    YM = scr("YM", [AB_MIX, N], BF16)
    AROW = scr("AROW", [2, B_H, N])
    NEGA = scr("NEGA", [2, B_H, N])
    CLMP = scr("CLMP", [2, B_H, N])

    with ExitStack() as es:
        P = Prog(nc, es)
        G = Glob()
        top = Ctx(P)
        es.enter_context(top)
        G.ones = top.sb([128, 128], F32, name="ones")
        G.MODS = top.sb([128, L, 9, KC, 2], F32, name="MODS")
        G.GS = top.sb([128, L, 3, KC, 2], F32, name="GS")
        G.GT = top.sb([128, L, 3, KC, 2], F32, name="GT")
        G.eps = top.sb([128, 1], F32, name="eps")
        P.memset("pool", G.ones[:], 1.0)
        P.memset("pool", G.eps[:], EPS)

        cast_ffn_weights(P, cfg, ffn_w1[0][0], ffn_w3[0][0], ffn_w2[0][0], w13s[0][0], w2s[0][0])
        cast_cols(P, ab_w_in, ws_ab, AB_IN)
        cast_rows(P, ab_w_out, wos_ab, KC)
        cast_ffn_weights(P, cfg, ffn_w1[0][1], ffn_w3[0][1], ffn_w2[0][1], w13s[0][1], w2s[0][1])
        cast_ffn_weights(P, cfg, ffn_w1[1][0], ffn_w3[1][0], ffn_w2[1][0], w13s[1][0], w2s[1][0])
        cast_cols(P, cd_w_in, ws_cd, CD_IN)
        P.dma("pool", gws[:], gluw.o(gluw.t[:, :].rearrange("(kc p) n -> p kc n", p=128)))
        cast_rows(P, cd_w_out, wos_cd, KC)
        cast_ffn_weights(P, cfg, ffn_w1[1][1], ffn_w3[1][1], ffn_w2[1][1], w13s[1][1], w2s[1][1])
        mod_phase(P, cfg, G, cT, mod_w, mod_bT, norm_gT, layers=[0])
        ffn_phase(P, cfg, G, 0, 0, hT0, hA, w13s[0][0], w2s[0][0], cfg.tiles)
        segs = (split_segs("fm", ABO[0], AB_SZ[0], QA, 0) + split_segs("fm", ABO[1], AB_SZ[1], KA, 0)
                + split_segs("tm", ABO[2], AB_SZ[2], VA, 0) + split_segs("fm", ABO[3], AB_SZ[3], QB, 0, scale=128 ** -0.5)
                + split_segs("fm", ABO[4], AB_SZ[4], KB, 0) + split_segs("tm", ABO[5], AB_SZ[5], VB, 0)
                + split_segs("fm", ABO[6], AB_SZ[6], OB, 0) + split_segs("fm", ABO[7], AB_SZ[7], GTd, 0, M=4 * B_H))
        inproj_phase(P, cfg, G, 0, hA, ws_ab, AB_IN, segs, cfg.tiles)
        if stage == "inproj0":
            return nc, P
        attnA_phase(P, cfg, G, QA, KA, VA, YM, gqT, gkT, cosA, sinA, RmA, want_ctx=True)
        if stage == "attnA":
            return nc, P
        mlstm_gates(P, cfg, G, GTd, ab_bgT, AROW, NEGA, CLMP,
                    hook=lambda cx: mod_phase(P, cfg, G, cT, mod_w, mod_bT, norm_gT, layers=list(range(1, L)), cext=cx))
        mlstm_phase(P, cfg, G, QB, KB, VB, OB, YM, A_H * 128, AROW, NEGA, CLMP, hgT, maskF, maskB, want_ctx=True)
        if stage == "mix0":
            return nc, P
        ffn_phase(P, cfg, G, 0, 1, hA, hB, w13s[0][1], w2s[0][1], cfg.tiles,
                  pre=dict(YM=YM, wos=wos_ab, MC=AB_MIX // 128))
        if stage == "layer0":
            return nc, P
        lat_tiles = [t for t in cfg.tiles if t[2] == 0]
        ffn_phase(P, cfg, G, 1, 0, hB, hA, w13s[1][0], w2s[1][0], cfg.tiles)
        segs = (split_segs("tm", 0, CW, U1, 0) + split_segs("fm", CW, D_H * 64, QD, 0, M=64, own=True)
                + split_segs("fm", CW + D_H * 64, D_KV * 64, KD, 0, M=64) + split_segs("tm", CW + D_H * 64 + D_KV * 64, D_KV * 64, VD, 0))
        inproj_phase(P, cfg, G, 1, hA, ws_cd, CD_IN, segs, cfg.tiles)
        if stage == "inproj1":
            return nc, P
        attnD_phase(P, cfg, G, QD, KD, VD, YM2, CW, sinkD, cosD, sinD, RmD, maskW)
        if stage == "attnD":
            return nc, P
        s5_phase(P, cfg, G, U1, s5p, YS, ident, mT)
        if stage == "s5":
            return nc, P
        s5_post_phase(P, cfg, G, YS, U1, dskB, gws, glubT, YM2, ident)
        if stage == "mix1":
            return nc, P
        ffn_phase(P, cfg, G, 1, 1, hA, hB, w13s[1][1], w2s[1][1], cfg.own_tiles, pre=dict(YM=YM2, wos=wos_cd, MC=CD_MIX // 128))
        final_phase(P, cfg, G, hB, outT, fgT)
        P.barrier(full=True)
    return nc, P


def rope_tables(T, grid_w, head_dim):
    rows = T // grid_w
    r, cidx = np.meshgrid(np.arange(rows, dtype=np.float32), np.arange(grid_w, dtype=np.float32), indexing="ij")
    r, cidx = r.reshape(-1), cidx.reshape(-1)
    n_freq = head_dim // 4
    inv = (np.float32(10000.0) ** (-np.arange(n_freq, dtype=np.float32) / np.float32(n_freq))).astype(np.float32)
    ang = np.concatenate([r[:, None] * inv, cidx[:, None] * inv], axis=-1).astype(np.float32)
    cos, sin = np.cos(ang).astype(np.float32), np.sin(ang).astype(np.float32)
    cosT = np.ascontiguousarray(np.concatenate([cos, cos], axis=1).T)
    sinT = np.ascontiguousarray(np.concatenate([sin, sin], axis=1).T)
    half = head_dim // 2
    Rm = np.zeros((head_dim, head_dim), np.float32)
    for dp in range(half):
        Rm[dp + half, dp] = -1.0
        Rm[dp, dp + half] = 1.0
    return cosT, sinT, Rm


def prepare(cfg, inp, b, flip=False):
    D, KC, L, T, NT = cfg.D, cfg.KC, cfg.L, cfg.T, cfg.NT
    f = np.ascontiguousarray
    m = {}
    xs, cs = inp["x"][b], inp["ctx"][b]
    if flip:
        xs, cs = xs[::-1], cs[::-1]
    m["hT0"] = f(np.concatenate([xs.T, cs.T], axis=1))
    c2 = np.stack([inp["c"][b], inp["c_ctx"]])
    m["cT"] = f(c2.reshape(2, KC, 128).transpose(2, 1, 0))
    m["mod_w"] = inp["mod_w"]
    m["mod_bT"] = f(inp["mod_b"].reshape(L, 9 * KC, 128).transpose(2, 0, 1))
    m["norm_gT"] = f(inp["norm_g"].reshape(L, 3, KC, 128).transpose(3, 0, 1, 2))
    for l in range(L):
        for j in range(2):
            m[f"w1_{l}{j}"] = inp["ffn_w1"][l, j]
            m[f"w3_{l}{j}"] = inp["ffn_w3"][l, j]
            m[f"w2_{l}{j}"] = inp["ffn_w2"][l, j]
    wi, bg = inp["ab_w_in"][0], inp["ab_b_gate"][0]
    if flip:
        nb_ = 2 * cfg.B_H
        wi = np.concatenate([wi[:, :-2 * nb_], wi[:, -nb_:], wi[:, -2 * nb_:-nb_]], axis=1)
        bg = np.concatenate([bg[nb_:], bg[:nb_]])
    m["ab_w_in"] = f(wi)
    m["ab_w_out"] = inp["ab_w_out"][0]
    m["ab_bgT"] = f(bg.reshape(4, cfg.B_H).T)
    m["gqT"] = f(inp["a_gq"][0].reshape(128, 1))
    m["gkT"] = f(inp["a_gk"][0].reshape(128, 1))
    m["hgT"] = f(inp["b_head_g"][0].reshape(2 * cfg.B_H, 128).T)
    cosA, sinA, RmA = rope_tables(T, cfg.GRID_W, 128)
    if flip:
        cosA, sinA = f(cosA[:, ::-1]), f(sinA[:, ::-1])
    m["cosA"], m["sinA"], m["RmA"] = cosA, sinA, RmA
    NJ = NT // 128
    s = np.arange(128)[:, None, None] + 128 * np.arange(NJ)[None, :, None]
    t = np.arange(NT)[None, None, :]
    m["maskF"] = np.where(s <= t, 0.0, -1e30).astype(np.float32)
    m["maskB"] = np.where(s >= t, 0.0, -1e30).astype(np.float32)
    CW, D_H = cfg.C_W, cfg.D_H
    Gn, G2 = CW // 16, CW // 32
    m["cd_w_in"] = inp["cd_w_in"][0]
    m["cd_w_out"] = inp["cd_w_out"][0]
    dsel = slice(None, None, -1) if flip else slice(None)
    m["s5_are"] = f(inp["s5_a_re"][0][dsel].reshape(2, G2, 128).transpose(0, 2, 1))
    m["s5_aim"] = f(inp["s5_a_im"][0][dsel].reshape(2, G2, 128).transpose(0, 2, 1))
    m["s5_ldt"] = f(np.repeat(inp["s5_log_dt"][0][dsel][:, :, None], 64, axis=2).reshape(2, G2, 128).transpose(0, 2, 1))
    m["s5_bre"] = f(inp["s5_b_re"][0][dsel].reshape(2, G2, 128, 16).transpose(0, 2, 1, 3))
    m["s5_bim"] = f(inp["s5_b_im"][0][dsel].reshape(2, G2, 128, 16).transpose(0, 2, 1, 3))
    m["s5_cre"] = f(inp["s5_c_re"][0][dsel].transpose(0, 1, 3, 2).reshape(2, G2, 128, 16).transpose(0, 2, 1, 3))
    m["s5_cim"] = f(inp["s5_c_im"][0][dsel].transpose(0, 1, 3, 2).reshape(2, G2, 128, 16).transpose(0, 2, 1, 3))
    m["dskB"] = f(inp["s5_d"][0])
    m["gluw"] = inp["s5_glu_w"][0]
    m["glubT"] = f(inp["s5_glu_b"][0].reshape(CW // 128, 128).T)
    m["sinkD"] = f(inp["d_sink"][0])
    cosD, sinD, RmD = rope_tables(T, cfg.GRID_W, 64)
    if flip:
        cosD, sinD = f(cosD[:, ::-1]), f(sinD[:, ::-1])
    m["cosD"], m["sinD"], m["RmD"] = cosD, sinD, RmD
    s2 = np.arange(128)[:, None, None]
    jj = np.arange(-1, NJ + 1)[None, :, None]
    m["maskW"] = np.where(np.abs(t - 128 * jj - s2) <= 128, 0.0, -1e30).astype(np.float32)
    m["ident"] = np.eye(128, dtype=np.float32)
    ii = (np.arange(128) // 16)
    mT = np.zeros((128, 2, 128), np.float32)
    mT[:, 0, :] = (ii[:, None] <= ii[None, :])
    mT[:, 1, :] = (ii[:, None] >= ii[None, :])
    m["mT"] = mT
    m["fgT"] = f(inp["final_g"].reshape(KC, 128).T)
    return m


def cast_cols(P, w, ws, IN):
    nblk = (IN + 511) // 512
    for b in range(nblk):
        wd = min(512, IN - b * 512)
        src = w.t[:, b * 512:b * 512 + wd].rearrange("(kc p) n -> p kc n", p=128)
        P.dma("pool", ws.o(ws.t[b, :, :, :wd]), w.o(src), writes=(), pwrites=[ws[:]])


def inproj_phase(P, cfg, G, l, hin, ws, IN, segs, tiles):
    KC, NT = cfg.KC, cfg.NT
    nblk = (IN + 511) // 512
    byblk = [[s for s in segs if s["col0"] // 512 == b] for b in range(nblk)]
    with Ctx(P) as c:
        Hb = [c.sb([128, KC, NT], F32, name="H") for _ in range(2)]
        hnb = [c.sb([128, KC, NT], BF16, nbuf=KC, name="hn") for _ in range(2)]
        sq = [c.sb([128, NT], F32, name="sq") for _ in range(2)]
        tmp = [c.sb([128, NT], F32, name="tmp") for _ in range(2)]
        rstdb = [c.sb([128, NT], F32, name="rstd") for _ in range(2)]
        wb = [c.sb([128, KC, 512], BF16, name="wb") for _ in range(3)]
        ev32 = [c.sb([128, NT], F32, name="ev32") for _ in range(3)]
        ev16 = [c.sb([128, NT], BF16, name="ev16") for _ in range(3)]
        ps_ssq = c.ps([128, NT], F32, name="ssq")
        psF = [c.ps([128, NT], F32, name="psF") for _ in range(3)]
        def ldw(i, slot):
            b = i % nblk
            wd = min(512, IN - b * 512)
            P.dma("sp", slot.o(slot.t[:, :, :wd]), ws.o(ws.t[b, :, :, :wd]))

        st = Stream(wb, len(tiles) * nblk, ldw)
        st.pump()
        nev = 0

        def load_norm(ti_):
            c0_, ncol_, wch_ = tiles[ti_]
            H_ = Hb[ti_ % 2]
            src_ = hin.t[:, c0_:c0_ + ncol_].rearrange("(kc p) n -> p kc n", p=128)
            P.dma("act", H_.o(H_.t[:, :, :ncol_]), hin.o(src_))
            norm_mod_tile(P, cfg, G, l, 1, wch_, H_, hnb[ti_ % 2], ncol_, sq, tmp, ps_ssq, rstdb[ti_ % 2])

        load_norm(0)
        for ti, (c0, ncol, wch) in enumerate(tiles):
            hn = hnb[ti % 2]
            for b in range(nblk):
                if b == 1 and ti + 1 < len(tiles):
                    load_norm(ti + 1)
                i = ti * nblk + b
                st.pump()
                w = st.get(i)
                for sg in byblk[b]:
                    if sg.get("own") and not (wch == 0 and c0 < cfg.TOWN):
                        continue
                    lc = sg["col0"] - b * 512
                    dst = sg["dst"]
                    if sg["kind"] == "fm":
                        mw = sg["n"]
                        ps = psF[nev % 3]
                        for kc in range(KC):
                            P.mm(ps.o(ps.t[:mw, :ncol]), w.o(w.t[:, kc, lc:lc + mw]),
                                 hn.s(kc, (slice(None), kc, slice(0, ncol))), start=(kc == 0), stop=(kc == KC - 1))
                        ev = (ev16 if dst.t.dtype == BF16 else ev32)[nev % 3]
                        if nev % 2 == 0:
                            P.act(ev.o(ev.t[:mw, :ncol]), ps.o(ps.t[:mw, :ncol]), AF.Copy, scale=sg.get("scale", 1.0))
                        else:
                            P.ts("dve", ev.o(ev.t[:mw, :ncol]), ps.o(ps.t[:mw, :ncol]), sg.get("scale", 1.0), None, ALU.mult)
                        r = sg["r0"]
                        P.dma("sp", dst.o(dst.t[r:r + mw, c0:c0 + ncol]), ev.o(ev.t[:mw, :ncol]), writes=(), pwrites=[dst[:]])
                        nev += 1
                    else:
                        n = sg["n"]
                        for t0 in range(0, ncol, 128):
                            ps = psF[nev % 3]
                            for kc in range(KC):
                                P.mm(ps.o(ps.t[:, :n]), hn.s(kc, (slice(None), kc, slice(t0, t0 + 128))),
                                     w.o(w.t[:, kc, lc:lc + n]), start=(kc == 0), stop=(kc == KC - 1))
                            ev = (ev16 if dst.t.dtype == BF16 else ev32)[nev % 3]
                            if nev % 2 == 0:
                                P.act(ev.o(ev.t[:, :n]), ps.o(ps.t[:, :n]), AF.Copy)
                            else:
                                P.copy("dve", ev.o(ev.t[:, :n]), ps.o(ps.t[:, :n]))
                            P.dma("sp", dst.o(dst.t[c0 + t0:c0 + t0 + 128, sg["r0"]:sg["r0"] + n]), ev.o(ev.t[:, :n]),
                                  writes=(), pwrites=[dst[:]])
                            nev += 1
                st.done(i)


def qk_prep(P, c, G, src, r0, dh, c0, ncol, out, gain, cosT, sinT, Rm, bufs, rope, norm):
    x, sqt, rot, t1, ps_s, ps_r, rs = bufs
    P.dma("act", x.o(x.t[:dh, :ncol]), src.o(src.t[r0:r0 + dh, c0:c0 + ncol]))
    if norm:
        P.tt("pool", sqt.o(sqt.t[:dh, :ncol]), x.o(x.t[:dh, :ncol]), x.o(x.t[:dh, :ncol]), ALU.mult)
        P.mm(ps_s.o(ps_s.t[:dh, :ncol]), G.ones.o(G.ones.t[:dh, :dh]), sqt.o(sqt.t[:dh, :ncol]), start=True, stop=True)
        P.act(rs.o(rs.t[:dh, :ncol]), ps_s.o(ps_s.t[:dh, :ncol]), AF.Ln, bias=G.eps.o(G.eps.t[:dh, 0:1]), scale=1.0 / dh)
        P.act(rs.o(rs.t[:dh, :ncol]), rs.o(rs.t[:dh, :ncol]), AF.Exp, scale=-0.5)
        P.ts("dve", x.o(x.t[:dh, :ncol]), x.o(x.t[:dh, :ncol]), gain, None, ALU.mult)
    if rope:
        P.mm(ps_r.o(ps_r.t[:dh, :ncol]), Rm.o(Rm.t[:dh, :dh]), x.o(x.t[:dh, :ncol]), start=True, stop=True)
        P.tt("dve", rot.o(rot.t[:dh, :ncol]), ps_r.o(ps_r.t[:dh, :ncol]), sinT.o(sinT.t[:dh, c0:c0 + ncol]), ALU.mult)
        P.tt("pool", t1.o(t1.t[:dh, :ncol]), x.o(x.t[:dh, :ncol]), cosT.o(cosT.t[:dh, c0:c0 + ncol]), ALU.mult)
        if norm:
            P.tt("dve", t1.o(t1.t[:dh, :ncol]), t1.o(t1.t[:dh, :ncol]), rot.o(rot.t[:dh, :ncol]), ALU.add)
            P.tt("dve", out, t1.o(t1.t[:dh, :ncol]), rs.o(rs.t[:dh, :ncol]), ALU.mult)
        else:
            P.tt("dve", out, t1.o(t1.t[:dh, :ncol]), rot.o(rot.t[:dh, :ncol]), ALU.add)
    else:
        if norm:
            P.tt("dve", out, x.o(x.t[:dh, :ncol]), rs.o(rs.t[:dh, :ncol]), ALU.mult)
        else:
            P.copy("dve", out, x.o(x.t[:dh, :ncol]))


def pipelined(n, stage1, stage2, la=2):
    for i in range(min(la, n)):
        stage1(i)
    for i in range(n):
        if i + la < n:
            stage1(i + la)
        stage2(i)


def prep_bufs(c, NT):
    return (c.sb([128, NT], F32, name="px"), c.sb([128, NT], F32, name="psq"), c.sb([128, NT], F32, name="prot"),
            c.sb([128, NT], F32, name="pt1"), c.ps([128, NT], F32, name="pps"), c.ps([128, NT], F32, name="ppr"),
            c.sb([128, NT], F32, name="prs"))


def attnA_phase(P, cfg, G, QA, KA, VA, YM, gq, gk, cosA, sinA, RmA, want_ctx):
    NT, N, T = cfg.NT, cfg.N, cfg.T
    NCH = N // 128
    TCH = T // 128
    grp = cfg.A_H // cfg.A_KV
    scale = 128 ** -0.5
    with Ctx(P) as c:
        cosT = c.sb([128, T], F32, name="cosT")
        sinT = c.sb([128, T], F32, name="sinT")
        Rm = c.sb([128, 128], F32, name="Rm")
        gqt = c.sb([128, 1], F32, name="gq")
        gkt = c.sb([128, 1], F32, name="gk")
        ones16 = c.sb([128, 128], BF16, name="ones16")
        KT = c.sb([128, N], BF16, name="KT")
        V = c.sb([128, NCH, 128], BF16, name="V")
        Qt = [c.sb([128, NT], BF16, name="Qt") for _ in range(2)]
        Et = [c.sb([128, NT], BF16, name="Et") for _ in range(5)]
        rd = c.sb([128, NT], F32, name="rd")
        osb = c.sb([128, NT], F32, name="osb")
        ob = [c.sb([128, NT], BF16, name="ob") for _ in range(2)]
        pb = prep_bufs(c, NT)
        psS = [c.ps([128, NT], F32, name="psS") for _ in range(4)]
        psO = [c.ps([128, NT], F32, name="psO") for _ in range(1)]
        psD = [c.ps([128, NT], F32, name="psD") for _ in range(1)]
        P.dma("sp", cosT[:], cosA[:])
        P.dma("sp", sinT[:], sinA[:])
        P.dma("sp", Rm[:], RmA[:])
        P.dma("sp", gqt[:], gq[:])
        P.dma("sp", gkt[:], gk[:])
        P.memset("pool", ones16[:], 1.0)
        nq = 0
        for g in range(cfg.A_KV):
            for (c0, ncol, wch) in cfg.tiles:
                qk_prep(P, c, G, KA, g * 128, 128, c0, ncol, KT.o(KT.t[:, c0:c0 + ncol]), gkt[:, 0:1], cosT, sinT, Rm, pb,
                        rope=(wch == 0), norm=True)
            lvl = getattr(cfg, "dbg_attn", 9)
            if lvl == 1:
                P.dma("sp", YM.o(YM.t[g * 128:(g + 1) * 128, :]), KT[:], writes=(), pwrites=[YM[:]])
                continue
            P.dma("sp", V[:], VA.o(VA.t[:, g * 128:(g + 1) * 128].rearrange("(c p) d -> p c d", p=128)))
            if lvl == 2:
                P.dma("sp", YM.o(YM.t[g * 128:(g + 1) * 128, :]), KT[:], writes=(), pwrites=[YM[:]])
                continue
            items = [(g * grp + hh, tl) for hh in range(grp) for tl in cfg.tiles if not (tl[2] == 1 and not want_ctx)]

            def qprep(ii, base=nq, items=items):
                h_, (c0_, ncol_, wch_) = items[ii]
                q_ = Qt[(base + ii) % 2]
                qk_prep(P, c, G, QA, h_ * 128, 128, c0_, ncol_, q_.o(q_.t[:, :ncol_]), gqt[:, 0:1], cosT, sinT, Rm, pb,
                        rope=(wch_ == 0), norm=True)

            qprep(0)
            for ii, (h, (c0, ncol, wch)) in enumerate(items):
                if True:
                    q = Qt[nq % 2]
                    if ii + 1 < len(items):
                        qprep(ii + 1)
                    if lvl == 3:
                        P.dma("sp", YM.o(YM.t[h * 128:(h + 1) * 128, c0:c0 + ncol]), q.o(q.t[:, :ncol]), writes=(), pwrites=[YM[:]])
                        nq += 1
                        continue
                    chunks = list(range(NCH)) if wch == 0 else list(range(TCH, NCH))
                    if lvl == 4:
                        chunks = chunks[:2]
                    O, Dn = psO[0], psD[0]
                    nch = len(chunks)

                    def st1(ci, chunks=chunks, q=q, ncol=ncol):
                        S, E = psS[ci % 4], Et[ci % 5]
                        ch = chunks[ci]
                        P.mm(S.o(S.t[:, :ncol]), KT.o(KT.t[:, ch * 128:(ch + 1) * 128]), q.o(q.t[:, :ncol]), start=True, stop=True)
                        P.act(E.o(E.t[:, :ncol]), S.o(S.t[:, :ncol]), AF.Exp, scale=scale)

                    def st2(ci, chunks=chunks, ncol=ncol, nch=nch, O=O, Dn=Dn):
                        E = Et[ci % 5]
                        ch = chunks[ci]
                        P.mm(O.o(O.t[:, :ncol]), V.o(V.t[:, ch, :]), E.o(E.t[:, :ncol]), start=(ci == 0), stop=(ci == nch - 1), inc=True)
                        P.mm(Dn.o(Dn.t[:, :ncol]), ones16[:, :], E.o(E.t[:, :ncol]), start=(ci == 0), stop=(ci == nch - 1), inc=True)

                    pipelined(nch, st1, st2, la=3)
                    P.copy("act", rd.o(rd.t[:, :ncol]), Dn.o(Dn.t[:, :ncol]))
                    P.copy("act", osb.o(osb.t[:, :ncol]), O.o(O.t[:, :ncol]))
                    P.op("dve", lambda E_: E_.reciprocal(out=rd.t[:, :ncol], in_=rd.t[:, :ncol]), reads=[rd[:]], writes=[rd[:]])
                    o = ob[nq % 2]
                    P.tt("dve", o.o(o.t[:, :ncol]), osb.o(osb.t[:, :ncol]), rd.o(rd.t[:, :ncol]), ALU.mult)
                    P.dma("sp", YM.o(YM.t[h * 128:(h + 1) * 128, c0:c0 + ncol]), o.o(o.t[:, :ncol]), writes=(), pwrites=[YM[:]])
                    nq += 1


def logscan(P, e, X, off, n, pad, op, sgn):
    cur = 0
    s = 1
    while s < n:
        a, b = X[cur], X[1 - cur]
        P.tt(e, b.o(b.t[:, off:off + n]), a.o(a.t[:, off:off + n]), a.o(a.t[:, off - sgn * s:off - sgn * s + n]), op)
        cur = 1 - cur
        s *= 2
    return cur


def mlstm_gates(P, cfg, G, GTd, bgT, AROW, NEGA, CLMP, hook=None):
    NH, N, T, CTX = cfg.B_H, cfg.N, cfg.T, cfg.CTX
    PAD = 1
    while PAD * 2 < N:
        PAD *= 2
    for d in range(2):
        with Ctx(P) as c:
            e = "dve"
            off = PAD if d == 0 else 0
            padlo, padhi = (0, PAD) if d == 0 else (N, N + PAD)
            sgn = 1 if d == 0 else -1
            X = [c.sb([NH, N + PAD], F32, name=f"X{d}{i}") for i in range(2)]
            ic = c.sb([NH, N], F32, name="ic")
            fl = c.sb([NH, N], F32, name="fl")
            Fs = c.sb([NH, N], F32, name="Fs")
            bg = c.sb([NH, 4], F32, name="bg")
            nbf = c.sb([NH, 1], F32, name="nbf")
            one = c.sb([NH, 1], F32, name="one")
            if hook is not None and d == 0:
                hook(c)
            P.dma("sp", bg[:], bgT[:])
            P.memset(e, one[:], 1.0)
            P.ts(e, nbf[:], bg.o(bg.t[:, 2 * d + 1:2 * d + 2]), -1.0, None, ALU.mult)
            segs = [(T, CTX, 0), (0, T, CTX)] if d == 0 else [(0, N, 0)]
            for (m0, ln, s0) in segs:
                P.dma("sp", ic.o(ic.t[:, s0:s0 + ln]), GTd.o(GTd.t[(2 * d) * NH:(2 * d + 1) * NH, m0:m0 + ln]), writes=(), pwrites=[ic[:]])
                P.dma("sp", fl.o(fl.t[:, s0:s0 + ln]), GTd.o(GTd.t[(2 * d + 1) * NH:(2 * d + 2) * NH, m0:m0 + ln]), writes=(), pwrites=[fl[:]])
            P.ts(e, ic[:], ic[:], bg.o(bg.t[:, 2 * d:2 * d + 1]), None, ALU.add)
            P.act(fl[:], fl[:], AF.Exp, bias=nbf[:, 0:1], scale=-1.0)
            P.act(fl[:], fl[:], AF.Ln, bias=one[:, 0:1], scale=1.0)
            for x in X:
                P.memset(e, x.o(x.t[:, padlo:padhi]), 0.0, pw=True)
            P.ts(e, X[0].o(X[0].t[:, off:off + N]), fl[:], -1.0, None, ALU.mult, pw=True)
            r = logscan(P, e, X, off, N, PAD, ALU.add, sgn)
            P.copy(e, Fs[:], X[r].o(X[r].t[:, off:off + N]))
            for x in X:
                P.memset(e, x.o(x.t[:, padlo:padhi]), -1e30, pw=True)
            P.tt(e, ic[:], ic[:], Fs[:], ALU.subtract)
            P.copy(e, X[0].o(X[0].t[:, off:off + N]), ic[:], pw=True)
            r = logscan(P, e, X, off, N, PAD, ALU.max, sgn)
            P.ts(e, fl[:], X[r].o(X[r].t[:, off:off + N]), 0.0, -1.0, ALU.max, ALU.mult)
            P.tt(e, Fs[:], fl[:], Fs[:], ALU.subtract)
            P.act(Fs[:], Fs[:], AF.Exp)
            for (m0, ln, s0) in segs:
                for (dst, srct) in ((AROW, ic), (NEGA, fl), (CLMP, Fs)):
                    P.dma("sp", dst.o(dst.t[d, :, m0:m0 + ln]), srct.o(srct.t[:, s0:s0 + ln]), writes=(), pwrites=[dst[:]])


def mlstm_phase(P, cfg, G, QB, KB, VB, OB, YM, ym_r0, AROW, NEGA, CLMP, hgT, maskF, maskB, want_ctx):
    NT, N, T, NH = cfg.NT, cfg.N, cfg.T, cfg.B_H
    NCH, TCH = N // 128, T // 128
    NJ = NT // 128
    with Ctx(P) as c:
        mk = [c.sb([128, NJ, NT], F32, name=f"mk{d}") for d in range(2)]
        P.dma("sp", mk[0][:], maskF[:])
        P.dma("sp", mk[1][:], maskB[:])
        hg = c.sb([128, NH * 2], F32, name="hg")
        P.dma("sp", hg[:], hgT[:])
        KT = c.sb([128, N], BF16, name="KT")
        Vx = c.sb([128, NCH, 384], BF16, name="Vx")
        AC = [c.sb([128, NCH], F32, name=f"AC{d}") for d in range(2)]
        Qtb = [c.sb([128, NT], BF16, name="Qt") for _ in range(2)]
        NAb = [[c.sb([128, NT], F32, name="NA") for _ in range(2)] for _ in range(2)]
        CLb = [[c.sb([128, NT], F32, name="CL") for _ in range(2)] for _ in range(2)]
        ogb = [[c.sb([128, NT], F32, name="og") for _ in range(2)] for _ in range(2)]
        arg = [c.sb([128, NT], F32, name="arg") for _ in range(2)]
        Wt = [c.sb([128, NT], F32, name="Wt") for _ in range(3)]
        Pm = [c.sb([128, NT], BF16, name="Pm") for _ in range(5)]
        rr = c.sb([128, NT], F32, name="rr")
        hs = [c.sb([128, NT], F32, name=f"hs{m}") for m in range(2)]
        hb = [c.sb([128, NT], F32, name=f"hb{m}") for m in range(2)]
        sqv = [c.sb([128, NT], F32, name="sqv") for _ in range(2)]
        yo = [c.sb([128, NT], BF16, name="yo") for _ in range(2)]
        psS = [c.ps([128, NT], F32, name="psS") for _ in range(4)]
        psN = [c.ps([128, NT], F32, name=f"psN{m}") for m in range(3)]
        psQ = c.ps([128, NT], F32, name="psQ")
        P.memset("pool", Vx.o(Vx.t[:, :, 256:384]), 1.0, pw=True)
        for h in range(NH):
            P.dma("sp", KT[:], KB.o(KB.t[h * 128:(h + 1) * 128, :]))
            P.dma("sp", Vx.o(Vx.t[:, :, 0:256]), VB.o(VB.t[:, h * 256:(h + 1) * 256].rearrange("(c p) d -> p c d", p=128)),
                  writes=(), pwrites=[Vx[:]])
            for d in range(2):
                P.dma("sp", AC[d][:], AROW.o(AROW.t[d, h, :].rearrange("(c p) -> p c", p=128)), allow_slow_non_contiguous=True)
            mtiles = [tl for tl in cfg.tiles if not (tl[2] == 1 and not want_ctx)]

            def loads(ii, h=h, mtiles=mtiles):
                c0_, ncol_, _ = mtiles[ii]
                pq = ii % 2
                P.dma("sp", Qtb[pq].o(Qtb[pq].t[:, :ncol_]), QB.o(QB.t[h * 128:(h + 1) * 128, c0_:c0_ + ncol_]))
                for d_ in range(2):
                    P.dma("sp", NAb[pq][d_].o(NAb[pq][d_].t[:, :ncol_]), NEGA.o(NEGA.t[d_, h, c0_:c0_ + ncol_].partition_broadcast(128)))
                    P.dma("sp", CLb[pq][d_].o(CLb[pq][d_].t[:, :ncol_]), CLMP.o(CLMP.t[d_, h, c0_:c0_ + ncol_].partition_broadcast(128)))
                for m_ in range(2):
                    P.dma("sp", ogb[pq][m_].o(ogb[pq][m_].t[:, :ncol_]),
                          OB.o(OB.t[h * 256 + m_ * 128:h * 256 + (m_ + 1) * 128, c0_:c0_ + ncol_]))

            loads(0)
            for ti_, (c0, ncol, wch) in enumerate(mtiles):
                if ti_ + 1 < len(mtiles):
                    loads(ti_ + 1)
                Qt, NA, CL, og = Qtb[ti_ % 2], NAb[ti_ % 2], CLb[ti_ % 2], ogb[ti_ % 2]
                nj = ncol // 128
                for d in range(2):
                    na, cl = NA[d], CL[d]
                    ch0 = c0 // 128
                    diag = [(ch0 + j, j) for j in range(nj)]
                    if wch == 0:
                        ctxc = [(ch, None) for ch in range(TCH, NCH)]
                        if d == 0:
                            chunks = ctxc + [(ch, None) for ch in range(0, ch0)] + diag
                        else:
                            chunks = ctxc + diag + [(ch, None) for ch in range(ch0 + nj, TCH)]
                    else:
                        chunks = diag
                    nch = len(chunks)

                    def st1(ci, chunks=chunks, ncol=ncol, d=d, na=na):
                        ch, mj = chunks[ci]
                        S, W, pm = psS[ci % 4], Wt[ci % 3], Pm[ci % 5]
                        P.mm(S.o(S.t[:, :ncol]), KT.o(KT.t[:, ch * 128:(ch + 1) * 128]), Qt.o(Qt.t[:, :ncol]), start=True, stop=True)
                        if mj is None:
                            P.act(W.o(W.t[:, :ncol]), na.o(na.t[:, :ncol]), AF.Exp, bias=AC[d].o(AC[d].t[:, ch:ch + 1]))
                        else:
                            ag = arg[ci % 2]
                            P.tt("pool", ag.o(ag.t[:, :ncol]), na.o(na.t[:, :ncol]), mk[d].o(mk[d].t[:, mj, :ncol]), ALU.add)
                            P.act(W.o(W.t[:, :ncol]), ag.o(ag.t[:, :ncol]), AF.Exp, bias=AC[d].o(AC[d].t[:, ch:ch + 1]))
                        P.tt("dve", pm.o(pm.t[:, :ncol]), S.o(S.t[:, :ncol]), W.o(W.t[:, :ncol]), ALU.mult)

                    def st2(ci, chunks=chunks, ncol=ncol, nch=nch):
                        ch, mj = chunks[ci]
                        pm = Pm[ci % 5]
                        for m in range(3):
                            P.mm(psN[m].o(psN[m].t[:, :ncol]), Vx.o(Vx.t[:, ch, m * 128:(m + 1) * 128]), pm.o(pm.t[:, :ncol]),
                                 start=(ci == 0), stop=(ci == nch - 1), inc=True)

                    pipelined(nch, st1, st2, la=3)
                    P.act(rr.o(rr.t[:, :ncol]), psN[2].o(psN[2].t[:, :ncol]), AF.Abs)
                    for m in range(2):
                        dstt = hs[m] if d == 0 else hb[m]
                        P.copy("act", dstt.o(dstt.t[:, :ncol]), psN[m].o(psN[m].t[:, :ncol]))
                    P.tt("dve", rr.o(rr.t[:, :ncol]), rr.o(rr.t[:, :ncol]), cl.o(cl.t[:, :ncol]), ALU.max)
                    P.act(rr.o(rr.t[:, :ncol]), rr.o(rr.t[:, :ncol]), AF.Ln)
                    P.act(rr.o(rr.t[:, :ncol]), rr.o(rr.t[:, :ncol]), AF.Exp, scale=-1.0)
                    for m in range(2):
                        dstt = hs[m] if d == 0 else hb[m]
                        P.tt("dve", dstt.o(dstt.t[:, :ncol]), dstt.o(dstt.t[:, :ncol]), rr.o(rr.t[:, :ncol]), ALU.mult)
                for m in range(2):
                    P.tt("pool", hs[m].o(hs[m].t[:, :ncol]), hs[m].o(hs[m].t[:, :ncol]), hb[m].o(hb[m].t[:, :ncol]), ALU.add)
                    P.tt("pool", sqv[m].o(sqv[m].t[:, :ncol]), hs[m].o(hs[m].t[:, :ncol]), hs[m].o(hs[m].t[:, :ncol]), ALU.mult)
                    P.mm(psQ.o(psQ.t[:, :ncol]), G.ones[:, :], sqv[m].o(sqv[m].t[:, :ncol]), start=(m == 0), stop=(m == 1), inc=True)
                    P.act(og[m].o(og[m].t[:, :ncol]), og[m].o(og[m].t[:, :ncol]), AF.Sigmoid)
                P.act(rr.o(rr.t[:, :ncol]), psQ.o(psQ.t[:, :ncol]), AF.Ln, bias=G.eps[:, 0:1], scale=1.0 / 256)
                P.act(rr.o(rr.t[:, :ncol]), rr.o(rr.t[:, :ncol]), AF.Exp, scale=-0.5)
                for m in range(2):
                    P.tt("dve", hs[m].o(hs[m].t[:, :ncol]), hs[m].o(hs[m].t[:, :ncol]), rr.o(rr.t[:, :ncol]), ALU.mult)
                    P.stt("dve", yo[m].o(yo[m].t[:, :ncol]), hs[m].o(hs[m].t[:, :ncol]), hg.o(hg.t[:, 2 * h + m:2 * h + m + 1]),
                          og[m].o(og[m].t[:, :ncol]), ALU.mult, ALU.mult)
                    r0 = ym_r0 + h * 256 + m * 128
                    P.dma("sp", YM.o(YM.t[r0:r0 + 128, c0:c0 + ncol]), yo[m].o(yo[m].t[:, :ncol]), writes=(), pwrites=[YM[:]])


def attnD_phase(P, cfg, G, QD, KD, VD, YM, ym_r0, sinkD, cosD, sinD, RmD, maskW):
    NT, N, T = cfg.NT, cfg.N, cfg.T
    NCH, TCH = N // 128, T // 128
    NJ = NT // 128
    grp = cfg.D_H // cfg.D_KV
    scale = 64 ** -0.5
    lat_tiles = cfg.own_tiles
    with Ctx(P) as c:
        cosT = c.sb([64, T], F32, name="cosT")
        sinT = c.sb([64, T], F32, name="sinT")
        Rm = c.sb([64, 64], F32, name="Rm")
        mw = c.sb([128, NJ + 2, NT], F32, name="mw")
        se = c.sb([64, cfg.D_H], F32, name="se")
        ones16 = c.sb([128, 64], BF16, name="ones16")
        KT = c.sb([64, N], BF16, name="KT")
        V = c.sb([128, NCH, 64], BF16, name="V")
        Qt = [c.sb([64, NT], BF16, name="Qt") for _ in range(2)]
        Et = [c.sb([128, NT], BF16, name="Et") for _ in range(5)]
        ag = [c.sb([128, NT], F32, name="ag") for _ in range(2)]
        rd = c.sb([64, NT], F32, name="rd")
        osb = c.sb([64, NT], F32, name="osb")
        ob = [c.sb([64, NT], BF16, name="ob") for _ in range(2)]
        pb = prep_bufs(c, NT)
        psS = [c.ps([128, NT], F32, name="psS") for _ in range(4)]
        psO = [c.ps([128, NT], F32, name="psO") for _ in range(1)]
        psD = [c.ps([128, NT], F32, name="psD") for _ in range(1)]
        P.dma("sp", cosT[:], cosD[:])
        P.dma("sp", sinT[:], sinD[:])
        P.dma("sp", Rm[:], RmD[:])
        P.dma("sp", mw[:], maskW[:])
        P.dma("sp", se[:], sinkD.o(sinkD.t[:].partition_broadcast(64)))
        P.act(se[:], se[:], AF.Exp)
        P.memset("pool", ones16[:], 1.0)
        nq = 0
        for g in range(cfg.D_KV):
            for (c0, ncol, wch) in cfg.tiles:
                qk_prep(P, c, G, KD, g * 64, 64, c0, ncol, KT.o(KT.t[:, c0:c0 + ncol]), None, cosT, sinT, Rm, pb,
                        rope=(wch == 0), norm=False)
            P.dma("sp", V[:], VD.o(VD.t[:, g * 64:(g + 1) * 64].rearrange("(c p) d -> p c d", p=128)))
            items = [(g * grp + hh, tl) for hh in range(grp) for tl in lat_tiles]

            def qprep(ii, base=nq, items=items):
                h_, (c0_, ncol_, wch_) = items[ii]
                q_ = Qt[(base + ii) % 2]
                qk_prep(P, c, G, QD, h_ * 64, 64, c0_, ncol_, q_.o(q_.t[:, :ncol_]), None, cosT, sinT, Rm, pb, rope=True, norm=False)

            qprep(0)
            for ii, (h, (c0, ncol, wch)) in enumerate(items):
                if True:
                    q = Qt[nq % 2]
                    if ii + 1 < len(items):
                        qprep(ii + 1)
                    ch0 = c0 // 128
                    nj = ncol // 128
                    chunks = [(ch, None) for ch in range(TCH, NCH)]
                    chunks += [(ch0 + jj, jj + 1) for jj in range(-1, nj + 1) if 0 <= ch0 + jj < TCH]
                    O, Dn = psO[0], psD[0]
                    nch = len(chunks)

                    def st1(ci, chunks=chunks, q=q, ncol=ncol):
                        ch, mj = chunks[ci]
                        S, E = psS[ci % 4], Et[ci % 5]
                        P.mm(S.o(S.t[:, :ncol]), KT.o(KT.t[:, ch * 128:(ch + 1) * 128]), q.o(q.t[:, :ncol]), start=True, stop=True)
                        if mj is None:
                            P.act(E.o(E.t[:, :ncol]), S.o(S.t[:, :ncol]), AF.Exp, scale=scale)
                        else:
                            a = ag[ci % 2]
                            P.stt("dve", a.o(a.t[:, :ncol]), S.o(S.t[:, :ncol]), scale, mw.o(mw.t[:, mj, :ncol]), ALU.mult, ALU.add)
                            P.act(E.o(E.t[:, :ncol]), a.o(a.t[:, :ncol]), AF.Exp)

                    def st2(ci, chunks=chunks, ncol=ncol, nch=nch, O=O, Dn=Dn):
                        ch, mj = chunks[ci]
                        E = Et[ci % 5]
                        P.mm(O.o(O.t[:64, :ncol]), V.o(V.t[:, ch, :]), E.o(E.t[:, :ncol]), start=(ci == 0), stop=(ci == nch - 1), inc=True)
                        P.mm(Dn.o(Dn.t[:64, :ncol]), ones16[:, :], E.o(E.t[:, :ncol]), start=(ci == 0), stop=(ci == nch - 1), inc=True)

                    pipelined(nch, st1, st2, la=3)
                    P.act(rd.o(rd.t[:, :ncol]), Dn.o(Dn.t[:64, :ncol]), AF.Identity, bias=se.o(se.t[:, h:h + 1]))
                    P.copy("act", osb.o(osb.t[:, :ncol]), O.o(O.t[:64, :ncol]))
                    P.op("dve", lambda E_: E_.reciprocal(out=rd.t[:, :ncol], in_=rd.t[:, :ncol]), reads=[rd[:]], writes=[rd[:]])
                    o = ob[nq % 2]
                    P.tt("dve", o.o(o.t[:, :ncol]), osb.o(osb.t[:, :ncol]), rd.o(rd.t[:, :ncol]), ALU.mult)
                    r0 = ym_r0 + h * 64
                    P.dma("sp", YM.o(YM.t[r0:r0 + 64, c0:c0 + ncol]), o.o(o.t[:, :ncol]), writes=(), pwrites=[YM[:]])
                    nq += 1


def final_phase(P, cfg, G, hin, out, fgT):
    KC, NT = cfg.KC, cfg.NT
    with Ctx(P) as c:
        fg = c.sb([128, KC], F32, name="fg")
        P.dma("sp", fg[:], fgT[:])
        H = [c.sb([128, KC, NT], F32, name="H") for _ in range(2)]
        sq = [c.sb([128, NT], F32, name="sq") for _ in range(2)]
        rstd = c.sb([128, NT], F32, name="rstd")
        ps = c.ps([128, NT], F32, name="ssq")
        for ti, (c0, ncol, wch) in enumerate(cfg.own_tiles):
            Ht = H[ti % 2]
            P.dma("sp", Ht.o(Ht.t[:, :, :ncol]), hin.o(hin.t[:, c0:c0 + ncol].rearrange("(kc p) n -> p kc n", p=128)))
            for kc in range(KC):
                s = sq[kc % 2]
                P.act(s.o(s.t[:, :ncol]), Ht.o(Ht.t[:, kc, :ncol]), AF.Square)
                P.mm(ps.o(ps.t[:, :ncol]), G.ones[:, :], s.o(s.t[:, :ncol]), start=(kc == 0), stop=(kc == KC - 1), inc=True)
            P.act(rstd.o(rstd.t[:, :ncol]), ps.o(ps.t[:, :ncol]), AF.Sqrt, bias=G.eps[:, 0:1], scale=1.0 / cfg.D)
            P.op("dve", lambda E: E.reciprocal(out=rstd.t[:, :ncol], in_=rstd.t[:, :ncol]), reads=[rstd[:]], writes=[rstd[:]])
            for kc in range(KC):
                P.stt("dve", Ht.o(Ht.t[:, kc, :ncol]), Ht.o(Ht.t[:, kc, :ncol]), fg.o(fg.t[:, kc:kc + 1]),
                      rstd.o(rstd.t[:, :ncol]), ALU.mult, ALU.mult)
            P.dma("sp", out.o(out.t[:, c0:c0 + ncol].rearrange("(kc p) n -> p kc n", p=128)), Ht.o(Ht.t[:, :, :ncol]),
                  writes=(), pwrites=[out[:]])


def bc(ap, axis, shape):
    return ap.unsqueeze(axis).broadcast_to(list(shape))


def s5_phase(P, cfg, G, U, prm, YS, ident, mT):
    N, T, CTX, CW = cfg.N, cfg.T, cfg.CTX, cfg.C_W
    G2 = CW // 32
    NC8, LC, CC = N // 8, T // 8, CTX // 8
    PADC = 1
    while PADC < NC8:
        PADC *= 2
    PADC //= 2
    NST = 0
    while (1 << NST) < NC8:
        NST += 1
    BP = 2
    PI = math.pi
    e = "dve"
    cblocks = [(c0, min(128, LC - c0)) for c0 in range(0, LC, 128)] + [(LC + c0, min(128, CC - c0)) for c0 in range(0, CC, 128)]
    with Ctx(P) as c:
        idt = c.sb([128, 128], F32, name="idt")
        P.dma("sp", idt[:], ident[:])
        mTt = c.sb([128, 2, 128], F32, name="mTt")
        P.dma("sp", mTt[:], mT[:])
        negpi = c.sb([128, 1], F32, name="negpi")
        P.memset(e, negpi[:], -PI)
        Bb = [[c.sb([128, G2, 16], F32, name=f"Bb{d}{r}") for r in range(2)] for d in range(2)]
        Cm = [[c.sb([128, G2, 16], F32, name=f"Cm{d}{r}") for r in range(2)] for d in range(2)]
        TB_ = [[[c.sb([128, G2, 8], F32, name=f"T{t}{d}{r}") for r in range(2)] for d in range(2)] for t in range(4)]
        PK = [[c.sb([128, G2, NST], F32, name=f"PK{d}{r}") for r in range(3)] for d in range(2)]
        with Ctx(P) as c2:
            def t2(name):
                return c2.sb([128, G2], F32, name=name)
            for d in range(2):
                are, aim, dt, adt, th = t2("are"), t2("aim"), t2("dt"), t2("adt"), t2("th")
                P.dma("sp", are[:], prm["are"].o(prm["are"].t[d]))
                P.dma("sp", aim[:], prm["aim"].o(prm["aim"].t[d]))
                P.dma("sp", dt[:], prm["ldt"].o(prm["ldt"].t[d]))
                braw = [c2.sb([128, G2, 16], F32, name=f"braw{r}") for r in range(2)]
                P.dma("sp", braw[0][:], prm["bre"].o(prm["bre"].t[d]))
                P.dma("sp", braw[1][:], prm["bim"].o(prm["bim"].t[d]))
                P.dma("sp", Cm[d][0][:], prm["cre"].o(prm["cre"].t[d]))
                P.dma("sp", Cm[d][1][:], prm["cim"].o(prm["cim"].t[d]))
                P.act(dt[:], dt[:], AF.Exp)
                P.tt(e, adt[:], are[:], dt[:], ALU.mult)
                P.tt(e, th[:], aim[:], dt[:], ALU.mult)
                mag, magm, ang, sn, cs, pr, pi_, qr, qi, angf = [t2(f"w{i}") for i in range(10)]
                angi = c2.sb([128, G2], mybir.dt.int32, name="angi")
                lam = None
                for n in range(9):
                    P.act(mag[:], adt[:], AF.Exp, scale=float(n))
                    P.act(magm[:], adt[:], AF.Exp, scale=-float(n))
                    for (dst_, ph) in ((sn, 0.0), (cs, 0.25)):
                        P.ts(e, ang[:], th[:], float(n) / (2 * PI), ph, ALU.mult, ALU.add)
                        P.copy(e, angi[:], ang[:])
                        P.copy(e, angf[:], angi[:])
                        P.tt(e, ang[:], ang[:], angf[:], ALU.subtract)
                        P.ts(e, angf[:], ang[:], 0.5, None, ALU.is_gt)
                        P.tt(e, ang[:], ang[:], angf[:], ALU.subtract)
                        P.ts(e, angf[:], ang[:], -0.5, None, ALU.is_lt)
                        P.tt(e, ang[:], ang[:], angf[:], ALU.add)
                        P.act(dst_[:], ang[:], AF.Sin, scale=2 * PI)
                    P.tt(e, pr[:], mag[:], cs[:], ALU.mult)
                    P.tt(e, pi_[:], mag[:], sn[:], ALU.mult)
                    P.tt(e, qr[:], magm[:], cs[:], ALU.mult)
                    P.stt(e, qi[:], magm[:], -1.0, sn[:], ALU.mult, ALU.mult)
                    if d == 0:
                        place = [(0, n, "q"), (1, n, "p"), (2, 7 - n, "p"), (3, n - 1, "p")]
                    else:
                        place = [(0, 7 - n, "q"), (1, 7 - n, "p"), (2, n, "p"), (3, 8 - n, "p")]
                    for (tb, blk, w) in place:
                        if blk < 0 or blk > 7:
                            continue
                        srcs = (pr, pi_) if w == "p" else (qr, qi)
                        for r in range(2):
                            tt_ = TB_[tb][d][r]
                            P.copy(e, tt_.o(tt_.t[:, :, blk]), srcs[r][:], pw=True)
                    if n == 1:
                        nr, den, kr, ki, t0 = t2("nr"), t2("den"), t2("kr"), t2("ki"), t2("t0")
                        P.ts(e, nr[:], pr[:], -1.0, None, ALU.add)
                        P.tt(e, den[:], are[:], are[:], ALU.mult)
                        P.tt(e, t0[:], aim[:], aim[:], ALU.mult)
                        P.tt(e, den[:], den[:], t0[:], ALU.add)
                        P.op(e, lambda E_: E_.reciprocal(out=den.t[:], in_=den.t[:]), reads=[den[:]], writes=[den[:]])
                        P.tt(e, kr[:], nr[:], are[:], ALU.mult)
                        P.tt(e, t0[:], pi_[:], aim[:], ALU.mult)
                        P.tt(e, kr[:], kr[:], t0[:], ALU.add)
                        P.tt(e, kr[:], kr[:], den[:], ALU.mult)
                        P.tt(e, ki[:], pi_[:], are[:], ALU.mult)
                        P.tt(e, t0[:], nr[:], aim[:], ALU.mult)
                        P.tt(e, ki[:], ki[:], t0[:], ALU.subtract)
                        P.tt(e, ki[:], ki[:], den[:], ALU.mult)
                        sh = [128, G2, 16]
                        tb16 = c2.sb(sh, F32, name="tb16")
                        krb, kib = bc(kr.t[:], 2, sh), bc(ki.t[:], 2, sh)
                        P.tt(e, Bb[d][0][:], braw[0][:], kr.o(krb), ALU.mult)
                        P.tt(e, tb16[:], braw[1][:], ki.o(kib), ALU.mult)
                        P.tt(e, Bb[d][0][:], Bb[d][0][:], tb16[:], ALU.subtract)
                        P.tt(e, Bb[d][1][:], braw[1][:], kr.o(krb), ALU.mult)
                        P.tt(e, tb16[:], braw[0][:], ki.o(kib), ALU.mult)
                        P.tt(e, Bb[d][1][:], Bb[d][1][:], tb16[:], ALU.add)
                    if n == 8:
                        P.copy(e, PK[d][0].o(PK[d][0].t[:, :, 0]), pr[:], pw=True)
                        P.copy(e, PK[d][1].o(PK[d][1].t[:, :, 0]), pi_[:], pw=True)
                for k in range(1, NST):
                    r0_, i0_ = PK[d][0].t[:, :, k - 1], PK[d][1].t[:, :, k - 1]
                    P.tt(e, mag[:], PK[d][0].o(r0_), PK[d][0].o(r0_), ALU.mult)
                    P.tt(e, magm[:], PK[d][1].o(i0_), PK[d][1].o(i0_), ALU.mult)
                    P.tt(e, PK[d][0].o(PK[d][0].t[:, :, k]), mag[:], magm[:], ALU.subtract, pw=True)
                    P.stt(e, PK[d][1].o(PK[d][1].t[:, :, k]), PK[d][0].o(r0_), 2.0, PK[d][1].o(i0_), ALU.mult, ALU.mult, pw=True)
                P.ts(e, PK[d][2][:], PK[d][1][:], -1.0, None, ALU.mult)
        shm = [128, BP, 8, 16]
        Mb = [[[c.sb(shm, F32, name=f"M{q}{d}{i}") for i in range(8)] for d in range(2)] for q in range(2)]
        tmpm = c.sb(shm, F32, name="tmpm")
        Toepb = [[[c.sb([128, 128], F32, name=f"Toep{q}{d}{gl}") for gl in range(2 * BP)] for d in range(2)] for q in range(2)]
        BcTb = [[[[[c.sb([128, 128], F32, name=f"BcT{q}{d}{pl}{r}{par}") for par in range(2)] for r in range(2)] for pl in range(BP)]
                 for d in range(2)] for q in range(2)]
        for q in range(2):
            for d in range(2):
                for pl in range(BP):
                    for r in range(2):
                        for par in range(2):
                            P.memset("pool", BcTb[q][d][pl][r][par][:], 0.0)
        Ugb = [c.sb([128, 2 * BP, NC8], F32, name=f"Ug{q}") for q in range(2)]
        Ut = [c.sb([128, 2 * BP, 8, 16], F32, name="Ut") for _ in range(2)]
        Yt = [c.sb([128, 8, 16 * 2 * BP], F32, name="Yt") for _ in range(2)]
        X = [[c.sb([128, BP, 2, PADC + NC8], F32, name=f"X{d}{pp}") for pp in range(2)] for d in range(2)]
        for d in range(2):
            for pp in range(2):
                lo, hi = (0, PADC) if d == 0 else (NC8, NC8 + PADC)
                P.memset("pool", X[d][pp].o(X[d][pp].t[:, :, :, lo:hi]), 0.0, pw=True)
        psT_ = c.ps([128, 512], F32, name="psT")
        psTr_ = c.ps([128, 512], F32, name="psTr")
        psT = Tile(psT_.t[:, 0:128], 1, "psT")
        psTr = Tile(psTr_.t[:, 0:128], 1, "psTr")
        psZ = [c.ps([128, 512], F32, name=f"psZ{r}") for r in range(2)]
        psZc_ = [c.ps([128, 512], F32, name=f"psZc{r}") for r in range(2)]
        psY_ = [c.ps([128, 512], F32, name="psY") for _ in range(2)]
        psY = [Tile(t_.t[:, 0:128], 1, "psY") for t_ in psY_]
        gw = 16 * 2 * BP
        nb = G2 // BP
        nut = [0]

        def stageA(b):
            q = b % 2
            p0 = b * BP
            Ug, M, Toep, BcT = Ugb[q], Mb[q], Toepb[q], BcTb[q]
            for (cc0, cw) in cblocks:
                ut = Ut[nut[0] % 2]
                nut[0] += 1
                for gl in range(2 * BP):
                    src = U.t[cc0 * 8:(cc0 + cw) * 8, b * gw + gl * 16:b * gw + (gl + 1) * 16].rearrange("(c i) w -> c i w", i=8)
                    P.dma("sp", ut.o(ut.t[:cw, gl]), U.o(src), writes=(), pwrites=[ut[:]])
                for gl in range(2 * BP):
                    P.transpose(psTr.o(psTr.t[:, :cw]), ut.o(ut.t[:cw, gl].rearrange("p a b -> p (a b)")), idt.o(idt.t[:cw, :cw]))
                    P.copy("act", Ug.o(Ug.t[:, gl, cc0:cc0 + cw]), psTr.o(psTr.t[:, :cw]), pw=True)
            for d in range(2):
                for mi, (tb, src, neg) in enumerate(((0, Bb, False), (1, Cm, True), (2, Bb, False), (3, Cm, True))):
                    tr, ti = (bc(TB_[tb][d][r].t[:, p0:p0 + BP, :], 3, shm) for r in range(2))
                    sr, si = (bc(src[d][r].t[:, p0:p0 + BP, :], 2, shm) for r in range(2))
                    Mr, Mi = M[d][2 * mi], M[d][2 * mi + 1]
                    T0, T1, S0, S1 = TB_[tb][d][0], TB_[tb][d][1], src[d][0], src[d][1]
                    P.op(e, lambda E_, Mr=Mr, tr=tr, sr=sr: E_.tensor_tensor(out=Mr.t[:], in0=tr, in1=sr, op=ALU.mult), reads=[T0[:], S0[:]], writes=[Mr[:]])
                    P.op(e, lambda E_, ti=ti, si=si: E_.tensor_tensor(out=tmpm.t[:], in0=ti, in1=si, op=ALU.mult), reads=[T1[:], S1[:]], writes=[tmpm[:]])
                    P.tt(e, Mr[:], Mr[:], tmpm[:], ALU.subtract)
                    P.op(e, lambda E_, Mi=Mi, tr=tr, si=si: E_.tensor_tensor(out=Mi.t[:], in0=tr, in1=si, op=ALU.mult), reads=[T0[:], S1[:]], writes=[Mi[:]])
                    P.op(e, lambda E_, ti=ti, sr=sr: E_.tensor_tensor(out=tmpm.t[:], in0=ti, in1=sr, op=ALU.mult), reads=[T1[:], S0[:]], writes=[tmpm[:]])
                    if neg:
                        P.stt(e, Mi[:], Mi[:], -1.0, tmpm[:], ALU.mult, ALU.subtract)
                    else:
                        P.tt(e, Mi[:], Mi[:], tmpm[:], ALU.add)
                Gr, Gi, Hr, nHi, Bcr, Bci, Ccr, nCci = M[d]
                for pl in range(BP):
                    for par in range(2):
                        rows = slice(par * 64, par * 64 + 64)
                        gl = 2 * pl + par
                        fl = lambda t_, pl=pl, rows=rows: t_.o(t_.t[rows, pl].rearrange("p a b -> p (a b)"))
                        P.mm(psT[:], fl(Gr), fl(Hr), start=True, stop=False, inc=True)
                        P.mm(psT[:], fl(Gi), fl(nHi), start=False, stop=True)
                        P.tt(e, Toep[d][gl][:], psT[:], mTt.o(mTt.t[:, d, :]), ALU.mult)
                    for r, Bm in enumerate((Bcr, Bci)):
                        P.transpose(psTr[:], Bm.o(Bm.t[:, pl].rearrange("p a b -> p (a b)")), idt[:])
                        for par in range(2):
                            cols = slice(par * 64, par * 64 + 64)
                            bt = BcT[d][pl][r][par]
                            P.copy("act", bt.o(bt.t[:, cols]), psTr.o(psTr.t[:, cols]))

        def stageB(b):
            q = b % 2
            p0 = b * BP
            Ug, M, Toep, BcT = Ugb[q], Mb[q], Toepb[q], BcTb[q]
            for d in range(2):
                off = PADC if d == 0 else 0
                for pl in range(BP):
                    xt = X[d][0]
                    for r in range(2):
                        bt0, bt1 = BcT[d][pl][r]
                        z, zc = psZ[r], psZc_[r]
                        P.mm(z.o(z.t[:, :LC]), bt0[:], Ug.o(Ug.t[:, 2 * pl, 0:LC]), start=True, stop=False, inc=True)
                        P.mm(z.o(z.t[:, :LC]), bt1[:], Ug.o(Ug.t[:, 2 * pl + 1, 0:LC]), start=False, stop=True)
                        P.mm(zc.o(zc.t[:, :CC]), bt0[:], Ug.o(Ug.t[:, 2 * pl, LC:NC8]), start=True, stop=False, inc=True)
                        P.mm(zc.o(zc.t[:, :CC]), bt1[:], Ug.o(Ug.t[:, 2 * pl + 1, LC:NC8]), start=False, stop=True)
                        if d == 0:
                            P.copy("act", xt.o(xt.t[:, pl, r, off + 1:off + 1 + CC]), zc.o(zc.t[:, :CC]), pw=True)
                            P.copy("act", xt.o(xt.t[:, pl, r, off + 1 + CC:off + NC8]), z.o(z.t[:, :LC - 1]), pw=True)
                            P.memset("act_", xt.o(xt.t[:, pl, r, off:off + 1]), 0.0, pw=True)
                        else:
                            P.copy("act", xt.o(xt.t[:, pl, r, 0:LC - 1]), z.o(z.t[:, 1:LC]), pw=True)
                            P.copy("act", xt.o(xt.t[:, pl, r, LC - 1:NC8 - 1]), zc.o(zc.t[:, :CC]), pw=True)
                            P.memset("act_", xt.o(xt.t[:, pl, r, NC8 - 1:NC8]), 0.0, pw=True)
            ncols = [CC + sum(cw_ for (cc0_, cw_) in cblocks if cc0_ * 8 < cfg.TOWN), NC8]
            curd = [0, 0]
            for k in range(NST):
                s_ = 1 << k
                ops = [[], [], [], []]
                for d in range(2):
                    if s_ >= ncols[d]:
                        continue
                    sgn = 1 if d == 0 else -1
                    off = PADC if d == 0 else 0
                    a, bb = X[d][curd[d]], X[d][1 - curd[d]]
                    curd[d] = 1 - curd[d]
                    NCd = ncols[d]
                    for pl in range(BP):
                        g2 = p0 + pl
                        pr_ = PK[d][0].o(PK[d][0].t[:, g2, k:k + 1])
                        pi_ = PK[d][1].o(PK[d][1].t[:, g2, k:k + 1])
                        npi_ = PK[d][2].o(PK[d][2].t[:, g2, k:k + 1])
                        lo = off - sgn * s_
                        re, im = a.t[:, pl, 0, off:off + NCd], a.t[:, pl, 1, off:off + NCd]
                        res, ims = a.t[:, pl, 0, lo:lo + NCd], a.t[:, pl, 1, lo:lo + NCd]
                        ore, oim = bb.t[:, pl, 0, off:off + NCd], bb.t[:, pl, 1, off:off + NCd]
                        ops[0].append((bb.o(ore), a.o(res), pr_, a.o(re)))
                        ops[1].append((bb.o(oim), a.o(res), pi_, a.o(im)))
                        ops[2].append((bb.o(ore), a.o(ims), npi_, bb.o(ore)))
                        ops[3].append((bb.o(oim), a.o(ims), pr_, bb.o(oim)))
                for grp_ in ops:
                    for (o_, i0_, sc_, i1_) in grp_:
                        P.stt(e, o_, i0_, sc_, i1_, ALU.mult, ALU.add, pw=True)
            for bi, (cc0, cw) in enumerate(cblocks):
                if cc0 * 8 >= cfg.TOWN:
                    continue
                yt = Yt[bi % 2]
                for gl in range(2 * BP):
                    pl, par = gl // 2, gl % 2
                    rows = slice(par * 64, par * 64 + 64)
                    py = psY[gl % 2]
                    P.mm(py.o(py.t[:cw, :]), Ug.o(Ug.t[:, gl, cc0:cc0 + cw]), Toep[0][gl][:], start=True, stop=False, inc=True)
                    P.mm(py.o(py.t[:cw, :]), Ug.o(Ug.t[:, gl, cc0:cc0 + cw]), Toep[1][gl][:], start=False, stop=False, inc=True)
                    for d in range(2):
                        xt = X[d][curd[d]]
                        if d == 0:
                            xc0 = PADC + (CC + cc0 if cc0 < LC else cc0 - LC)
                        else:
                            xc0 = cc0
                        Ccr, nCci = M[d][6], M[d][7]
                        P.mm(py.o(py.t[:cw, :]), xt.o(xt.t[rows, pl, 0, xc0:xc0 + cw]),
                             Ccr.o(Ccr.t[rows, pl].rearrange("p a b -> p (a b)")), start=False, stop=False, inc=True)
                        P.mm(py.o(py.t[:cw, :]), xt.o(xt.t[rows, pl, 1, xc0:xc0 + cw]),
                             nCci.o(nCci.t[rows, pl].rearrange("p a b -> p (a b)")), start=False, stop=(d == 1), inc=True)
                    P.copy("act", yt.o(yt.t[:cw, :, gl * 16:(gl + 1) * 16]), py.o(py.t[:cw, :].rearrange("p (a b) -> p a b", b=16)), pw=True)
                dst = YS.t[cc0 * 8:(cc0 + cw) * 8, b * gw:(b + 1) * gw].rearrange("(c i) w -> c i w", i=8)
                P.dma("sp", YS.o(dst), yt.o(yt.t[:cw]), writes=(), pwrites=[YS[:]])

        stageA(0)
        for b in range(nb):
            if b + 1 < nb:
                stageA(b + 1)
            stageB(b)


def s5_post_phase(P, cfg, G, YS, U, dskB, gluw, glubT, YM, ident):
    CW, NT = cfg.C_W, cfg.NT
    KCc = CW // 128
    with Ctx(P) as c:
        idt = c.sb([128, 128], F32, name="idt")
        P.dma("sp", idt[:], ident[:])
        dsk = c.sb([128, CW], F32, name="dsk")
        P.dma("sp", dsk[:], dskB.o(dskB.t[:].partition_broadcast(128)))
        gb = c.sb([128, KCc], F32, name="gb")
        P.dma("sp", gb[:], glubT[:])
        gw = c.sb([128, KCc, CW], BF16, name="gw")
        P.dma("sp", gw[:], gluw[:])
        y = [c.sb([128, CW], F32, name="y") for _ in range(2)]
        u = [c.sb([128, CW], F32, name="u") for _ in range(2)]
        t = [c.sb([128, CW], F32, name="t") for _ in range(2)]
        g32 = c.sb([128, KCc, NT], F32, name="g32")
        g16 = c.sb([128, KCc, NT], BF16, name="g16")
        sg = [c.sb([128, NT], F32, name="sg") for _ in range(2)]
        ob = [c.sb([128, NT], BF16, name="ob") for _ in range(2)]
        psTr = [c.ps([128, 512], F32, name="psTr") for _ in range(2)]
        psG = [c.ps([128, 512], F32, name="psG") for _ in range(2)]
        k = 0
        for (c0, ncol, wch) in cfg.own_tiles:
            for tc in range(ncol // 128):
                r0 = c0 + tc * 128
                yy, uu, tt_ = y[k % 2], u[k % 2], t[k % 2]
                P.dma("sp", yy[:], YS.o(YS.t[r0:r0 + 128, :]))
                P.dma("act", uu[:], U.o(U.t[r0:r0 + 128, :]))
                P.tt("pool", uu[:], uu[:], dsk[:], ALU.mult)
                P.tt("dve", yy[:], yy[:], uu[:], ALU.add)
                P.tt("pool", tt_[:], yy[:], yy[:], ALU.mult)
                P.ts("dve", tt_[:], tt_[:], 0.044715, 1.0, ALU.mult, ALU.add)
                P.tt("dve", tt_[:], tt_[:], yy[:], ALU.mult)
                P.act(tt_[:], tt_[:], AF.Sigmoid, scale=1.5957691216057308)
                P.tt("dve", yy[:], yy[:], tt_[:], ALU.mult)
                for kc in range(KCc):
                    pt = psTr[kc % 2]
                    P.transpose(pt.o(pt.t[:, :128]), yy.o(yy.t[:, kc * 128:(kc + 1) * 128]), idt[:])
                    P.copy("act", g32.o(g32.t[:, kc, tc * 128:(tc + 1) * 128]), pt.o(pt.t[:, :128]), pw=True)
                    P.copy("dve", g16.o(g16.t[:, kc, tc * 128:(tc + 1) * 128]), g32.o(g32.t[:, kc, tc * 128:(tc + 1) * 128]), pw=True)
                k += 1
            for n in range(KCc):
                pg = psG[n % 2]
                for kc in range(KCc):
                    P.mm(pg.o(pg.t[:, :ncol]), gw.o(gw.t[:, kc, n * 128:(n + 1) * 128]), g16.o(g16.t[:, kc, :ncol]),
                         start=(kc == 0), stop=(kc == KCc - 1))
                s_ = sg[n % 2]
                P.act(s_.o(s_.t[:, :ncol]), pg.o(pg.t[:, :ncol]), AF.Sigmoid, bias=gb.o(gb.t[:, n:n + 1]))
                o = ob[n % 2]
                P.tt("dve", o.o(o.t[:, :ncol]), g32.o(g32.t[:, n, :ncol]), s_.o(s_.t[:, :ncol]), ALU.mult)
                P.dma("sp", YM.o(YM.t[n * 128:(n + 1) * 128, c0:c0 + ncol]), o.o(o.t[:, :ncol]), writes=(), pwrites=[YM[:]])


def kernel(**inputs):
    cfg = Cfg()
    inp = {k: np.asarray(v) for k, v in inputs.items()}
    nc, P = build(cfg)
    B = inp["x"].shape[0]
    in_maps = [prepare(cfg, inp, core // 2, flip=bool(core % 2)) for core in range(2 * B)]
    res = run_bass_kernel_spmd(nc, in_maps, core_ids=list(range(2 * B)))
    out = np.empty((B, cfg.T, cfg.D), np.float32)
    for b in range(B):
        out[b, :cfg.TOWN] = np.asarray(res.results[2 * b]["outT"]).T
        out[b, cfg.TOWN:] = np.asarray(res.results[2 * b + 1]["outT"]).T[::-1]
    return out
```
